# Optimizing a Trainium2 kernel written in Bass

```python
import math
import jax, jax.numpy as jnp
from jax import lax
import numpy as np


D_MODEL = 2048
BATCH = 8
SEQ = 2048
DEPTH = 4
DEC_BATCH = 8
DEC_SEQ = 64
PAST_LEN = 2048

CHUNK = 64
N_EVEN = (DEPTH + 1) // 2
N_ODD = DEPTH // 2
EPS = 1e-6
SSM_HEAD_DIM = 64
SSM_INNER = D_MODEL
SSM_HEADS = SSM_INNER // SSM_HEAD_DIM
SSM_GROUPS = 4
SSM_STATE = 128
SSM_CONV = 4
SSM_CONV_DIM = SSM_INNER + 2 * SSM_GROUPS * SSM_STATE
ATT_HEAD_DIM = 64
ATT_INNER = D_MODEL
ATT_HEADS = ATT_INNER // ATT_HEAD_DIM
ATT_KV_HEADS = 8
ATT_GROUP = ATT_HEADS // ATT_KV_HEADS
KV_DIM = ATT_KV_HEADS * ATT_HEAD_DIM
WINDOW = 128
CONV_WIDTH = 2 * D_MODEL
SHORT_CONV = 3
EVEN_PROJ = 2 * SSM_INNER + 2 * SSM_GROUPS * SSM_STATE + SSM_HEADS + 2 * ATT_INNER + 2 * KV_DIM
EVEN_MIX = SSM_INNER + ATT_INNER

kernel_name = "hybrid_ssd_swa_shortconv_stream_step"


def rms_norm(x, w):
    xf = x.astype(jnp.float32)
    y = xf * lax.rsqrt(jnp.mean(xf * xf, axis=-1, keepdims=True) + EPS)
    return (y * w.astype(jnp.float32)).astype(x.dtype)


def causal_conv(x, prev, w):
    width, length = w.shape[0], x.shape[1]
    xp = jnp.concatenate([prev.astype(x.dtype), x], axis=1)
    w = w.astype(x.dtype)
    out = sum(w[i] * xp[:, i:i + length] for i in range(width))
    return out, xp[:, length:]


def modulate(x, c, w_ada, b_ada, g_pre):
    mod = jax.nn.silu(c) @ w_ada + b_ada
    shift, scale, gate = jnp.split(mod[:, None, :], 3, axis=-1)
    return rms_norm(x, g_pre) * (1 + scale) + shift, gate


def ssd_scan(x, dt, a, bm, cm, h0, chunk):
    b, length, nh, p = x.shape
    g, n = bm.shape[2], bm.shape[3]
    r = nh // g
    nc = length // chunk
    x = x.reshape(b, nc, chunk, g, r, p)
    dt = dt.reshape(b, nc, chunk, g, r)
    bm = bm.reshape(b, nc, chunk, g, n)
    cm = cm.reshape(b, nc, chunk, g, n)
    a_cum = jnp.cumsum(dt * a.reshape(g, r), axis=2)
    idx = jnp.arange(chunk)
    causal = (idx[:, None] >= idx[None, :])[None, None, :, :, None, None]
    seg = a_cum[:, :, :, None] - a_cum[:, :, None, :]
    decay = jnp.exp(jnp.where(causal, seg, -jnp.inf))
    cb = jnp.einsum('bcign,bcjgn->bcijg', cm, bm)
    mix = cb[..., None] * decay * dt[:, :, None]
    y_diag = jnp.einsum('bcijgr,bcjgrp->bcigrp', mix, x)
    a_last = a_cum[:, :, -1]
    to_end = jnp.exp(a_last[:, :, None] - a_cum) * dt
    chunk_states = jnp.einsum('bclgn,bclgr,bclgrp->bcgrpn', bm, to_end, x)

    def step(h, inp):
        s, al = inp
        return jnp.exp(al)[..., None, None] * h + s, h

    h_final, h_start = lax.scan(step, h0.reshape(b, g, r, p, n),
                                (jnp.moveaxis(chunk_states, 1, 0), jnp.moveaxis(a_last, 1, 0)))
    h_start = jnp.moveaxis(h_start, 0, 1)
    y_off = jnp.einsum('bcign,bcigr,bcgrpn->bcigrp', cm, jnp.exp(a_cum), h_start)
    return (y_diag + y_off).reshape(b, length, nh, p), h_final.reshape(b, nh, p, n)


def band_attention(q, k, v, valid, sinks):
    s = jnp.einsum('bnqkrd,bnskd->bnkrqs', q, k).astype(jnp.float32) * (ATT_HEAD_DIM ** -0.5)
    s = jnp.where(valid[None, :, None, None, None, :], s, -jnp.inf)
    sink = sinks.astype(jnp.float32).reshape(1, 1, ATT_KV_HEADS, ATT_GROUP, 1, 1)
    m = jnp.maximum(jnp.max(s, axis=-1, keepdims=True), sink)
    pr = jnp.exp(s - m)
    pr = pr / (jnp.sum(pr, axis=-1, keepdims=True) + jnp.exp(sink - m))
    return jnp.einsum('bnkrqs,bnskd->bnqkrd', pr.astype(v.dtype), v)


def even_mixer(h, conv_prev, ssm_prev, k_prev, v_prev, w_in, conv_w, conv_b, dt_bias, a_log, d_skip,
               norm_w, sinks, w_out, first):
    b, length, _ = h.shape
    f32 = jnp.float32
    cuts = np.cumsum([SSM_INNER, SSM_CONV_DIM, SSM_HEADS, ATT_INNER, KV_DIM, KV_DIM]).tolist()
    z, xbc, dt_raw, q, k, v, g_att = jnp.split(h @ w_in, cuts, axis=-1)
    xbc, conv_new = causal_conv(xbc, conv_prev, conv_w)
    xbc = jax.nn.silu(xbc + conv_b).astype(f32)
    xs, bm, cm = jnp.split(xbc, [SSM_INNER, SSM_INNER + SSM_GROUPS * SSM_STATE], axis=-1)
    xs = xs.reshape(b, length, SSM_HEADS, SSM_HEAD_DIM)
    dt = jax.nn.softplus(dt_raw.astype(f32) + dt_bias.astype(f32))
    a = -jnp.exp(a_log.astype(f32))
    y, ssm_new = ssd_scan(xs, dt, a, bm.reshape(b, length, SSM_GROUPS, SSM_STATE),
                          cm.reshape(b, length, SSM_GROUPS, SSM_STATE), ssm_prev.astype(f32),
                          CHUNK if first else length)
    y = (y + d_skip.astype(f32)[:, None] * xs).reshape(b, length, SSM_GROUPS, SSM_INNER // SSM_GROUPS)
    y = y * jax.nn.silu(z.astype(f32)).reshape(b, length, SSM_GROUPS, SSM_INNER // SSM_GROUPS)
    y = y * lax.rsqrt(jnp.mean(y * y, axis=-1, keepdims=True) + EPS)
    y_ssm = (y.reshape(b, length, SSM_INNER) * norm_w.astype(f32)).astype(h.dtype)
    q = q.reshape(b, length, ATT_KV_HEADS, ATT_GROUP, ATT_HEAD_DIM)
    kp = jnp.concatenate([k_prev.astype(h.dtype), k.reshape(b, length, ATT_KV_HEADS, ATT_HEAD_DIM)], axis=1)
    vp = jnp.concatenate([v_prev.astype(h.dtype), v.reshape(b, length, ATT_KV_HEADS, ATT_HEAD_DIM)], axis=1)
    if first:
        n = length // CHUNK
        back = WINDOW // CHUNK
        kc = kp.reshape(b, n + back, CHUNK, ATT_KV_HEADS, ATT_HEAD_DIM)
        vc = vp.reshape(b, n + back, CHUNK, ATT_KV_HEADS, ATT_HEAD_DIM)
        kb = jnp.concatenate([kc[:, j:j + n] for j in range(back + 1)], axis=2)
        vb = jnp.concatenate([vc[:, j:j + n] for j in range(back + 1)], axis=2)
        key_chunk = jnp.arange(n)[:, None] - back + jnp.arange(WINDOW + CHUNK)[None, :] // CHUNK
        valid = key_chunk >= 0
        qb = q.reshape(b, n, CHUNK, ATT_KV_HEADS, ATT_GROUP, ATT_HEAD_DIM)
    else:
        kb, vb = kp[:, None], vp[:, None]
        valid = jnp.ones((1, WINDOW + length), dtype=bool)
        qb = q[:, None]
    o = band_attention(qb, kb, vb, valid, sinks)
    y_att = o.reshape(b, length, ATT_INNER) * jax.nn.silu(g_att)
    out = jnp.concatenate([y_ssm, y_att], axis=-1) @ w_out
    return out, conv_new, ssm_new, kp[:, -WINDOW:], vp[:, -WINDOW:]


def odd_mixer(h, conv_prev, w_in, conv_w, w_out):
    u, gb, gc, g = jnp.split(h @ w_in, 4, axis=-1)
    conv_out, conv_new = causal_conv(gc * u, conv_prev, conv_w)
    y = gb * conv_out * jax.nn.silu(g)
    return y @ w_out, conv_new


def setup_inputs(seed: int = 0) -> dict:
    key = jax.random.key(seed)
    ks = jax.random.split(key, 32)

    def nrm(k, shape, s=1.0):
        return s * jax.random.normal(k, shape, jnp.float32)

    D = D_MODEL
    dt0 = jnp.exp(jax.random.uniform(ks[13], (N_EVEN, SSM_HEADS), jnp.float32, math.log(1e-3), math.log(1e-1)))
    return {
        'x_prompt': nrm(ks[0], (BATCH, SEQ, D)),
        'x_sample': nrm(ks[1], (DEC_BATCH, DEC_SEQ, D)),
        'c_prompt': nrm(ks[2], (BATCH, D)),
        'c_sample': nrm(ks[3], (DEC_BATCH, D)),
        'cache_k': nrm(ks[4], (N_EVEN, DEC_BATCH, WINDOW, ATT_KV_HEADS, ATT_HEAD_DIM)),
        'cache_v': nrm(ks[5], (N_EVEN, DEC_BATCH, WINDOW, ATT_KV_HEADS, ATT_HEAD_DIM)),
        'state_conv_a': nrm(ks[6], (N_EVEN, DEC_BATCH, SSM_CONV - 1, SSM_CONV_DIM)),
        'state_ssm': nrm(ks[7], (N_EVEN, DEC_BATCH, SSM_HEADS, SSM_HEAD_DIM, SSM_STATE), 0.1),
        'state_conv_c': nrm(ks[8], (N_ODD, DEC_BATCH, SHORT_CONV - 1, CONV_WIDTH)),
        'w_ada': nrm(ks[9], (DEPTH, D, 3 * D), D ** -0.5),
        'b_ada': nrm(ks[10], (DEPTH, 3 * D), 0.01),
        'norm_pre': 1.0 + nrm(ks[11], (DEPTH, D), 0.1),
        'norm_post': 1.0 + nrm(ks[12], (DEPTH, D), 0.1),
        'w_in_even': nrm(ks[14], (N_EVEN, D, EVEN_PROJ), D ** -0.5),
        'conv_a_w': nrm(ks[15], (N_EVEN, SSM_CONV, SSM_CONV_DIM), SSM_CONV ** -0.5),
        'conv_a_b': nrm(ks[16], (N_EVEN, SSM_CONV_DIM), 0.1),
        'dt_bias': dt0 + jnp.log(-jnp.expm1(-dt0)),
        'a_log': jnp.log(jax.random.uniform(ks[17], (N_EVEN, SSM_HEADS), jnp.float32, 1.0, 16.0)),
        'd_skip': 1.0 + nrm(ks[18], (N_EVEN, SSM_HEADS), 0.1),
        'norm_ssm': 1.0 + nrm(ks[19], (N_EVEN, SSM_INNER), 0.1),
        'sinks': nrm(ks[20], (N_EVEN, ATT_HEADS)),
        'w_out_even': nrm(ks[21], (N_EVEN, EVEN_MIX, D), EVEN_MIX ** -0.5),
        'w_in_odd': nrm(ks[22], (N_ODD, D, 4 * CONV_WIDTH), D ** -0.5),
        'conv_c_w': nrm(ks[23], (N_ODD, SHORT_CONV, CONV_WIDTH), SHORT_CONV ** -0.5),
        'w_out_odd': nrm(ks[24], (N_ODD, CONV_WIDTH, D), CONV_WIDTH ** -0.5),
    }


def reference(x_prompt, x_sample, c_prompt, c_sample, cache_k, cache_v, state_conv_a, state_ssm, state_conv_c,
              w_ada, b_ada, norm_pre, norm_post, w_in_even, conv_a_w, conv_a_b, dt_bias, a_log, d_skip, norm_ssm,
              sinks, w_out_even, w_in_odd, conv_c_w, w_out_odd):
    xp, xs = x_prompt, x_sample
    bp = xp.shape[0]
    kp_l, vp_l, cap_l, ssp_l, ccp_l = [], [], [], [], []
    ks_l, vs_l, cas_l, sss_l, ccs_l = [], [], [], [], []
    for layer in range(DEPTH):
        i = layer // 2
        hp, gate_p = modulate(xp, c_prompt, w_ada[layer], b_ada[layer], norm_pre[layer])
        hs, gate_s = modulate(xs, c_sample, w_ada[layer], b_ada[layer], norm_pre[layer])
        if layer % 2 == 0:
            ew = (w_in_even[i], conv_a_w[i], conv_a_b[i], dt_bias[i], a_log[i], d_skip[i], norm_ssm[i], sinks[i],
                  w_out_even[i])
            zkv = jnp.zeros((bp, WINDOW, ATT_KV_HEADS, ATT_HEAD_DIM), hp.dtype)
            op, conv_p, ssm_p, k_p, v_p = even_mixer(
                hp, jnp.zeros((bp, SSM_CONV - 1, SSM_CONV_DIM), hp.dtype),
                jnp.zeros((bp, SSM_HEADS, SSM_HEAD_DIM, SSM_STATE), jnp.float32), zkv, zkv, *ew, first=True)
            os_, conv_s, ssm_s, k_s, v_s = even_mixer(
                hs, state_conv_a[i], state_ssm[i], cache_k[i], cache_v[i], *ew, first=False)
            kp_l.append(k_p); vp_l.append(v_p); cap_l.append(conv_p); ssp_l.append(ssm_p)
            ks_l.append(k_s); vs_l.append(v_s); cas_l.append(conv_s); sss_l.append(ssm_s)
        else:
            op, cc_p = odd_mixer(hp, jnp.zeros((bp, SHORT_CONV - 1, CONV_WIDTH), hp.dtype),
                                 w_in_odd[i], conv_c_w[i], w_out_odd[i])
            os_, cc_s = odd_mixer(hs, state_conv_c[i], w_in_odd[i], conv_c_w[i], w_out_odd[i])
            ccp_l.append(cc_p); ccs_l.append(cc_s)
        xp = xp + gate_p * rms_norm(op, norm_post[layer])
        xs = xs + gate_s * rms_norm(os_, norm_post[layer])
    return (xp, xs,
            jnp.stack(kp_l), jnp.stack(vp_l), jnp.stack(cap_l), jnp.stack(ssp_l), jnp.stack(ccp_l),
            jnp.stack(ks_l), jnp.stack(vs_l), jnp.stack(cas_l), jnp.stack(sss_l), jnp.stack(ccs_l))
```

```python
import numpy as np
from contextlib import ExitStack
import concourse.bass as bass
import concourse.mybir as mybir
from concourse.bass_utils import run_bass_kernel_spmd

F32 = mybir.dt.float32
BF16 = mybir.dt.bfloat16
AF = mybir.ActivationFunctionType
ALU = mybir.AluOpType
AX = mybir.AxisListType

D = 2048
KC = 16
CH = 64
EPS = 1e-6
NS = 3
SAME_SYNC = True

EV_COLS = dict(z=(0, 2048), xbc=(2048, 5120), dt=(5120, 5152), q=(5152, 7200), k=(7200, 7712),
               v=(7712, 8224), g=(8224, 10272))
EV_GROUPS = ([('dt', 0)] + [('xbc', i) for i in range(12)] + [('z', i) for i in range(8)] +
             [('k', i) for i in range(2)] + [('ks', i) for i in range(2)] + [('v', i) for i in range(2)] +
             [('q', i) for i in range(8)] + [('g', i) for i in range(8)])


class Op:
    __slots__ = ('eng', 'fn', 'waits', 'inc', 'chan', 'count')

    def __init__(self, eng, fn, chan):
        self.eng = eng; self.fn = fn; self.chan = chan; self.waits = set(); self.inc = False; self.count = 0


class Tracker:
    def __init__(self):
        self.ops = []
        self.lastw = {}
        self.readers = {}
        self.chan_last = {}

    def add(self, eng, fn, reads=(), writes=(), chan=None):
        idx = len(self.ops)
        op = Op(eng, fn, chan)
        deps = set()
        for k in reads:
            w = self.lastw.get(k)
            if w is not None:
                deps.add(w)
            if k in ('psW', 'psW2') or (isinstance(k, tuple) and k[0] == 'ps'):
                for st_, r in self.readers.get(k, {}).items():
                    if st_ != (('c', chan) if chan else ('e', eng)):
                        deps.add(r)
        for k in writes:
            w = self.lastw.get(k)
            if w is not None:
                deps.add(w)
            for r in self.readers.get(k, {}).values():
                deps.add(r)
        stream = ('c', chan) if chan else ('e', eng)
        for k in reads:
            self.readers.setdefault(k, {})[stream] = idx
        for k in writes:
            self.lastw[k] = idx
            self.readers[k] = {}
        for d in deps:
            p = self.ops[d]
            if p.chan is not None:
                op.waits.add(self.chan_last[p.chan])
            else:
                if p.eng == eng and chan is None and (eng == 'pe' or not SAME_SYNC):
                    continue
                p.inc = True
                op.waits.add(d)
        if chan:
            prev = self.chan_last.get(chan)
            if prev is not None:
                op.waits.add(prev)
            self.chan_last[chan] = idx
        self.ops.append(op)

    def emit(self, nc, es):
        engines = {'pe': nc.tensor, 'act': nc.scalar, 'dve': nc.vector, 'pool': nc.gpsimd, 'sp': nc.sync}
        ecount = {}
        ccount = {}
        for op in self.ops:
            if op.chan:
                ccount[op.chan] = ccount.get(op.chan, 0) + 16
                op.count = ccount[op.chan]
            elif op.inc:
                ecount[op.eng] = ecount.get(op.eng, 0) + 1
                op.count = ecount[op.eng]
        sems = {}
        for e in ecount:
            sems[('e', e)] = es.enter_context(nc.semaphore('se_' + e))
        for c in ccount:
            sems[('c', c)] = es.enter_context(nc.semaphore('sc_' + c))
        waited = {e: {} for e in engines}
        for op in self.ops:
            E = engines[op.eng]
            need = {}
            for d in op.waits:
                p = self.ops[d]
                key = ('c', p.chan) if p.chan else ('e', p.eng)
                need[key] = max(need.get(key, 0), p.count)
            wd = waited[op.eng]
            for key, val in need.items():
                if wd.get(key, 0) < val:
                    E.wait_ge(sems[key], val)
                    wd[key] = val
            ins = op.fn(E)
            if op.chan:
                ins.then_inc(sems[('c', op.chan)], 16)
            elif op.inc:
                ins.then_inc(sems[('e', op.eng)], 1)
        for c, val in ccount.items():
            nc.sync.wait_ge(sems[('c', c)], val)


class _Stop(Exception):
    pass


def build(cfg, dbg_names=()):
    SEQ = cfg['SEQ']; DEPTH = cfg['DEPTH']; TP = cfg['TP']
    NT = SEQ // TP
    TW = TP + CH
    NE = (DEPTH + 1) // 2
    NO = DEPTH // 2
    NCP = TP // CH
    nc = bass.Bass("TRN2", target_bir_lowering=False)
    es = ExitStack()
    T = Tracker()
    dbg_out = {}

    def din(name, shape):
        return nc.dram_tensor(name, list(shape), F32, kind="ExternalInput").ap()

    def dout(name, shape):
        return nc.dram_tensor(name, list(shape), F32, kind="ExternalOutput").ap()

    d_xp = din('xp', [128, KC, SEQ]); d_xs = din('xs', [128, KC, CH]); d_cT = din('cT', [128, KC * 2])
    d_wada = din('wada', [DEPTH, 128, KC, 6144])
    d_bada = din('bada', [128, DEPTH * 48]); d_npre = din('npre', [128, DEPTH * 16]); d_npost = din('npost', [128, DEPTH * 16])
    d_ident = din('c_ident', [128, 128]); d_triu = din('c_triu', [64, 64]); d_mask = din('c_mask', [64, 512])
    d_wie = [din(f'wie{i}', [len(EV_GROUPS), 128, 4096]) for i in range(NE)]
    d_woe = [din(f'woe{i}', [16, 128, 4096]) for i in range(NE)]
    d_wio = [din(f'wio{i}', [64, 128, 4096]) for i in range(NO)]
    d_woo = [din(f'woo{i}', [16, 128, 4096]) for i in range(NO)]
    d_caw = din('caw', [128, NE * 24 * 4]); d_cab = din('cab', [128, NE * 24])
    d_dtb = din('dtb', [64, NE * 32]); d_alog = din('alog', [64, NE * 32]); d_dsk = din('dsk', [64, NE * 32])
    d_snk = din('snk', [64, NE * 32]); d_nssm = din('nssm', [128, NE * 16])
    d_ccw = din('ccw', [128, max(NO, 1) * 32 * 3])
    d_ckT = din('ckT', [NE, 128, 4, 128]); d_cv = din('cv', [NE, 2, 64, 512])
    d_sca = din('sca', [NE, 128, 24 * 3]); d_sst = din('sst', [NE, 128, 2048]); d_scc = din('scc', [max(NO, 1), 128, 32 * 2])
    o_yp = dout('yp', [128, KC, SEQ]); o_ys = dout('ys', [128, KC, CH])
    o_nkp = dout('nkp', [NE, 128, 4, 128]); o_nvp = dout('nvp', [NE, 128, 512]); o_ncap = dout('ncap', [NE, 128, 72])
    o_nssp = dout('nssp', [NE, 128, 2048]); o_nccp = dout('nccp', [max(NO, 1), 128, 64])
    o_nks = dout('nks', [NE, 128, 4, 128]); o_nvs = dout('nvs', [NE, 128, 512]); o_ncas = dout('ncas', [NE, 128, 72])
    o_nsss = dout('nsss', [NE, 128, 2048]); o_nccs = dout('nccs', [max(NO, 1), 128, 64])

    def sb(name, shape, dt=F32):
        return es.enter_context(nc.sbuf_tensor('s_' + name, list(shape), dt))

    def ps(name, shape, dt=F32):
        return es.enter_context(nc.psum_tensor(name, list(shape), dt))

    xT = sb('xT', [128, KC, TW])
    hT = sb('hT', [128, KC, TW], BF16)
    hflat = hT[:].rearrange("p a b -> p (a b)")
    arena2 = sb('arena2', [128, 40 * TW], BF16)
    xbcs = arena2[:, 0:24 * TW].rearrange("p (a b) -> p a b", a=24)
    qT = arena2[:, 24 * TW:40 * TW].rearrange("p (a b) -> p a b", a=16)
    obuf = arena2[:, 0:32 * TW].bitcast(F32).rearrange("p (a b) -> p a b", a=16)
    mix = sb('mix', [128, 32, TW], BF16)
    sq = mix[:, 0:16, :]
    wbuf = [sb(f'wbuf{i}', [128, 4096], BF16) for i in range(NS)]
    C8 = TW + 8
    scr = sb('scr', [128, 5 * TW + 2 * C8 + 256 + TW // 2])
    tmpA = [scr[:, 0:TW], scr[:, TW:2 * TW]]; rstd = scr[:, 2 * TW:3 * TW]
    ident_f = sb('ident_f', [128, 128]); ident_b = sb('ident_b', [128, 128], BF16)
    ones_b = sb('ones_b', [128, 128], BF16); ones_f = sb('ones_f', [64, 128])
    triu_f = sb('triu_f', [64, 64]); mask_b = sb('mask_b', [64, 512], BF16)
    eps_c = sb('eps_c', [128, 2])
    cT = sb('cT', [128, KC * 2]); scT = sb('scT', [128, KC * 2]); scb = sb('scb', [128, KC * 2], BF16)
    bada = sb('bada', [128, DEPTH * 48]); npre = sb('npre', [128, DEPTH * 16]); npost = sb('npost', [128, DEPTH * 16])
    modt = sb('modt', [128, DEPTH * 96])
    modc = sb('modc', [128, DEPTH * 2 * 3 * 16])
    caw = sb('caw', [128, NE * 96]); cab = sb('cab', [128, NE * 24]); nssm = sb('nssm', [128, NE * 16])
    dtb = sb('dtb', [64, NE * 32]); alog = sb('alog', [64, NE * 32]); dsk = sb('dsk', [64, NE * 32]); snk = sb('snk', [64, NE * 32])
    aneg = sb('aneg', [64, NE * 32])
    Dm = [sb('Dm0', [64, 32 * 64], BF16)] * NE
    ccw = sb('ccw', [128, max(NO, 1) * 96])
    kP = [sb(f'kP{i}', [128, 4, 128 + TP], BF16) for i in range(NE)]
    kPs = [sb(f'kPs{i}', [128, 4, 128 + TP], BF16) for i in range(NE)]
    vP = [sb(f'vP{i}', [64, 2 + NCP, 512], BF16) for i in range(NE)]
    stT = [sb(f'stT{i}', [128, 2048]) for i in range(NE)]
    ccar = [sb(f'ccar{i}', [128, 24 * 3]) for i in range(NE)]
    vcar = [sb(f'vcar{i}', [128, 32 * 2]) for i in range(NO)]
    kS = sb('kS', [128, 4, 192], BF16); kSs = sb('kSs', [128, 4, 192], BF16)
    vS = sb('vS', [64, 3, 512], BF16)
    stS = sb('stS', [128, 2048]); scaS = sb('scaS', [128, 72]); sccS = sb('sccS', [128, 64])
    stbf = sb('stbf', [128, 2048], BF16)
    o_ = 3 * TW
    cst = [scr[:, o_:o_ + C8]] * 2
    cacc = [scr[:, o_ + C8:o_ + 2 * C8]] * 2
    o_ += 2 * C8
    stg = [tmpA[0], tmpA[1], scr[:, o_:o_ + TW], scr[:, o_ + TW:o_ + 2 * TW]]
    o_ += 2 * TW
    kvout = [scr[:, o_:o_ + 256]] * 2
    sqo = [scr[:, o_ + 256:o_ + 256 + TW // 2].bitcast(BF16)] * 2
    dtall = sb('dtall', [64, (NCP + 1) * 32]); spx = sb('spx', [64, (NCP + 1) * 32])
    sm = {n: sb('sm_' + n, [128, 32]) for n in ['dta', 'acum', 'ldt', 'amb', 'E2', 'wte', 'tw', 'Edec']}
    xtok = sb('xtok', [64, 2560], BF16); xw = sb('xw', [64, 2048], BF16)
    cbT = sb('cbT', [64, 256])
    Dgh = [sb('Dgh0', [64, 512], BF16)[:], hflat[0:64, 11 * TW:11 * TW + 512]]
    Dgl = [sb('Dgl0', [64, 512], BF16)[:], hflat[0:64, 11 * TW + 512:11 * TW + 1024]]
    DGK = [[('Dg', 0)], [('h', k_) for k_ in range(11, 15)]]
    ahl = sb('ahl', [64, 64], BF16)
    seg = [sb('seg0', [64, 512])[:], scr[0:64, 0:512]]
    mixT = [sb('mixT0', [64, 512], BF16)[:], scr[0:64, 1024:1280].bitcast(BF16)]
    t1 = [sb('t10', [64, 512])[:], scr[0:64, 512:1024]]
    ytok = [sb('ytok0', [64, 512], BF16)[:], scr[0:64, 1280:1536].bitcast(BF16)]
    gy = [sb('gy0', [128, 256])[:], scr[:, 1536:1792]]
    sqg = [sb('sqg0', [128, 256], BF16)[:], scr[:, 1792:1920].bitcast(BF16)]
    rsg = [sb('rsg0', [128, 64])[:], scr[:, 1920:1984]]
    Pn = [hflat[0:64, 0:768], hflat[0:64, 5 * TW:5 * TW + 768]]
    Pq = Pn
    PTq = [hflat[0:64, 768:1536], hflat[0:64, 5 * TW + 768:5 * TW + 1536]]
    asm = [{n: sb(f'asm{i}_' + n, [64, 4]) for n in ['mx', 'negm', 'rs', 'es', 'den', 'rinv']} for i in range(2)]
    NSB = 4
    psS = [ps(f'psS{i}', [128, 512]) for i in range(NSB)]
    psW = ps('psW', [128, 1024])
    psW2 = ps('psW2', [128, 1024])
    PW = [psW, psW2]; PWK = ['psW', 'psW2']
    ps_ctr = [0]

    def bank():
        i = ps_ctr[0] % NSB
        ps_ctr[0] += 1
        return psS[i], ('ps', i)

    def mm(out, lhsT, rhs, start, stop, reads, writes):
        T.add('pe', lambda e: e.matmul(out, lhsT, rhs, start=start, stop=stop), reads, writes)

    def tr(out, in_, ident, reads, writes):
        T.add('pe', lambda e: e.transpose(out, in_, ident), reads, writes)

    def act(out, in_, func, reads, writes, bias=None, scale=None, accum=None, eng='act'):
        kw = {}
        if bias is not None: kw['bias'] = bias
        if scale is not None: kw['scale'] = scale
        if accum is not None: kw['accum_out'] = accum
        T.add('act', lambda e: e.activation(out=out, in_=in_, func=func, **kw), reads, writes)

    def tt(eng, out, in0, in1, op, reads, writes):
        T.add(eng, lambda e: e.tensor_tensor(out, in0, in1, op), reads, writes)

    def tsc(eng, out, in0, s1, s2, op0, op1, reads, writes):
        if op1 is None:
            T.add(eng, lambda e: e.tensor_scalar(out, in0, s1, None, op0), reads, writes)
        else:
            T.add(eng, lambda e: e.tensor_scalar(out, in0, s1, s2, op0, op1), reads, writes)

    def stt(eng, out, in0, scalar, in1, op0, op1, reads, writes):
        T.add(eng, lambda e: e.scalar_tensor_tensor(out, in0, scalar, in1, op0, op1), reads, writes)

    def rsqrt(out, in_, scale, reads, wkey):
        act(out, in_, AF.Ln, reads, [wkey], bias=eps_c[0:out.shape[0], 0:1], scale=scale)
        act(out, out, AF.Exp, [wkey], [wkey], scale=-0.5)

    def cp(eng, out, in_, reads, writes):
        if eng == 'act':
            T.add('act', lambda e: e.copy(out, in_), reads, writes)
        else:
            T.add(eng, lambda e: e.tensor_copy(out, in_), reads, writes)

    def memset(eng, ap, val, writes):
        T.add(eng, lambda e: e.memset(ap, val), (), writes)

    def dma(eng, out, in_, reads, writes, chan):
        T.add(eng, lambda e: e.dma_start(out=out, in_=in_), reads, writes, chan=chan)

    def dbg(name, ap, reads, shape):
        if name not in dbg_names:
            return
        if name not in dbg_out:
            dbg_out[name] = dout('dbg_' + name, shape)
        dma('pool', dbg_out[name], ap, reads, [('dbgo', name)], 'dbg')

    def ckpt(name):
        if cfg.get('stop') == name:
            raise _Stop()

    def keys(name, rng):
        return [(name, i) for i in rng]

    XK = keys('x', range(KC)); HK = keys('h', range(KC))
    PKS = [keys('h', range(0, 5)), keys('h', range(5, 10))]
    STSK = [('stS', g_) for g_ in range(4)]; STBK = [('stbf', g_) for g_ in range(4)]
    SCRK = [('tmpA', 0), ('tmpA', 1), 'rstd', ('cst', 0), ('cacc', 0), ('stg', 2), ('stg', 3), ('kvout', 0), ('sqo', 0)]
    SETB = [(n_, 1) for n_ in ['seg', 'mixT', 't1', 'ytok', 'gy', 'sqg', 'rsg']]

    dma('sp', ident_f[:], d_ident, [], ['ident_f'], 'cst')
    dma('pool', ident_b[:], d_ident, [], ['ident_b'], 'cstb')
    dma('sp', triu_f[:], d_triu, [], ['triu_f'], 'cst')
    dma('pool', mask_b[:], d_mask, [], ['mask_b'], 'cstb')
    memset('dve', ones_b[:], 1.0, ['ones_b'])
    memset('dve', ones_f[:], 1.0, ['ones_f'])
    memset('dve', eps_c[:, 0:1], EPS, ['eps_c'])
    memset('dve', eps_c[:, 1:2], 1.0, ['eps_c'])
    for (t_, d_, k_) in [(cT, d_cT, 'cT'), (bada, d_bada, 'bada'), (npre, d_npre, 'npre'), (npost, d_npost, 'npost'),
                         (caw, d_caw, 'caw'), (cab, d_cab, 'cab'), (nssm, d_nssm, 'nssm'), (dtb, d_dtb, 'dtb'),
                         (alog, d_alog, 'alog'), (dsk, d_dsk, 'dsk'), (snk, d_snk, 'snk'), (ccw, d_ccw, 'ccw')]:
        dma('sp', t_[:], d_, [], [k_], 'cst')
    act(scT[:], cT[:], AF.Silu, ['cT'], ['scT'])
    act(aneg[:], alog[:], AF.Exp, ['alog'], ['aneg'])
    tsc('dve', aneg[:], aneg[:], -1.0, None, ALU.mult, None, ['aneg'], ['aneg'])
    cp('dve', scb[:], scT[:], ['scT'], ['scb'])
    nb_ctr = 0
    for l in range(DEPTH):
        pst, pk = psW[:, 512:1024], 'psW'
        for nb in range(24):
            s_ = nb_ctr % NS
            wt = wbuf[s_]; wk = ('w', s_)
            dma('pool', wt[:].rearrange("p (a b) -> p a b", a=KC), d_wada[l, :, :, nb * 256:(nb + 1) * 256], [], [wk], f'w{s_}')
            nb_ctr += 1
            pr, prk = bank()
            for kc in range(KC):
                mm(pr[0:2, 0:256], scb[:, kc * 2:kc * 2 + 2], wt[:, kc * 256:(kc + 1) * 256], kc == 0, kc == KC - 1, [wk, 'scb'], [prk])
            mr = tmpA[nb % 2]; mrk = ('tmpA', nb % 2)
            cp('dve', mr[0:2, 0:256], pr[0:2, 0:256], [prk], [mrk])
            for jj in range(2):
                j = nb * 2 + jj
                tr(pst[:, j * 2:j * 2 + 2], mr[0:2, jj * 128:(jj + 1) * 128], ident_f[0:2, 0:2], [mrk, 'ident_f'], [pk])
        tt('dve', modt[:, l * 96:(l + 1) * 96].rearrange("p (j w) -> p j w", w=2),
           pst[:, 0:96].rearrange("p (j w) -> p j w", w=2),
           bada[:, l * 48:(l + 1) * 48].unsqueeze(2).to_broadcast([128, 48, 2]), ALU.add,
           [pk, 'bada'], [('modt', l)])
        for w in range(2):
            base = ((l * 2 + w) * 3) * 16
            mv = modt[:, l * 96:(l + 1) * 96].rearrange("p (j w) -> p j w", w=2)
            stt('dve', modc[:, base:base + 16], mv[:, 16:32, w], 1.0, npre[:, l * 16:(l + 1) * 16], ALU.add, ALU.mult,
                [('modt', l), 'npre'], [('modc', l)])
            cp('dve', modc[:, base + 16:base + 32], mv[:, 0:16, w], [('modt', l)], [('modc', l)])
            tt('dve', modc[:, base + 32:base + 48], mv[:, 32:48, w], npost[:, l * 16:(l + 1) * 16], ALU.mult,
               [('modt', l), 'npost'], [('modc', l)])


    def mc_unused():
        pass

    def mc(l, w, kind, kc):
        o = ((l * 2 + w) * 3 + kind) * 16 + kc
        return modc[:, o:o + 1]

    wseq = []
    for ti in range(NT):
        for l in range(DEPTH):
            i = l // 2
            if l % 2 == 0:
                wseq += [d_wie[i][g] for g in range(len(EV_GROUPS))] + [d_woe[i][g] for g in range(16)]
            else:
                wseq += [d_wio[i][g] for g in range(64)] + [d_woo[i][g] for g in range(16)]
    wst = dict(issued=0, consumed=0)
    NG = len(wseq) // NT
    wsc = None
    if NT > 1:
        wsc = []
        for l in range(DEPTH):
            ng_l = (len(EV_GROUPS) + 16) if l % 2 == 0 else 80
            t_ = nc.dram_tensor(f'wscratch{l}', [ng_l, 128, 4096], BF16, kind="Internal").ap()
            wsc += [t_[g_] for g_ in range(ng_l)]
        assert len(wsc) == NG

    glayer = []
    for l in range(DEPTH):
        glayer += [l] * ((len(EV_GROUPS) + 16) if l % 2 == 0 else 80)

    def wnext():
        while wst['issued'] < min(len(wseq), wst['consumed'] + NS):
            n = wst['issued']
            s = n % NS
            g_ = n % NG
            ti_ = n // NG
            wbt = 0 if (glayer[g_] < 2 or NT < 3) else 1
            if wsc is None or ti_ <= wbt:
                dma('pool', wbuf[s][:], wseq[n], [], [('w', s)], f'w{s}')
                if wsc is not None and ti_ == wbt:
                    dma('sp', wsc[g_], wbuf[s][:], [('w', s)], [('wsc', g_)], f'wb{s}')
            else:
                dma('pool', wbuf[s][:], wsc[g_], [('wsc', g_)], [('w', s)], f'w{s}')
            wst['issued'] += 1
        s = wst['consumed'] % NS
        wst['consumed'] += 1
        return wbuf[s], ('w', s)

    def phase_A(l, Tt, segs):
        ckpt('A0')
        act(sq[:, :, 0:Tt], xT[:, :, 0:Tt], AF.Square, XK, keys('mix', range(16)))
        ckpt('A1')
        pst, pk = bank()
        for kc in range(KC):
            mm(pst[:, 0:Tt], ones_b[:], sq[:, kc, 0:Tt], kc == 0, kc == KC - 1, [('mix', kc), 'ones_b'], [pk])
        ckpt('A2')
        rsqrt(rstd[:, 0:Tt], pst[:, 0:Tt], 1.0 / D, [pk, 'eps_c'], 'rstd')
        ckpt('A3')
        for kc in range(KC):
            tb = tmpA[kc % 2]; tk = ('tmpA', kc % 2)
            tt('dve', tb[:, 0:Tt], xT[:, kc, 0:Tt], rstd[:, 0:Tt], ALU.mult, [('x', kc), 'rstd'], [tk])
            for (c0, c1, w) in segs:
                act(hT[:, kc, c0:c1], tb[:, c0:c1], AF.Identity, [tk, ('modc', l)], [('h', kc)],
                    bias=mc(l, w, 1, kc), scale=mc(l, w, 0, kc))

    def proj_fm(wt, wk, col0, ncols_k, Tt, rhs_fn, rkeys, nk):
        pst, pk = bank()
        for kc in range(nk):
            mm(pst[:, 0:Tt], wt[:, kc * ncols_k + col0: kc * ncols_k + col0 + 128], rhs_fn(kc), kc == 0, kc == nk - 1,
               [wk] + rkeys(kc), [pk])
        return pst, pk

    def phase_D(l, Tt, segs, last_tile, ti):
        ssb, ssk = psW[:, 512:1024], 'psW'
        for j in range(KC):
            wt, wk = wnext()
            pst, pk = proj_fm(wt, wk, 0, 128, Tt, lambda kc: mix[:, kc, 0:Tt], lambda kc: [('mix', kc)], 32)
            wr = [('o', j)]
            if j == 0:
                wr = wr + HK + keys('xb', range(24)) + keys('q', range(16))
            cp('act', obuf[:, j, 0:Tt], pst[:, 0:Tt], [pk], wr)
            sb_ = sqo[j % 2]; sk = ('sqo', 0)
            act(sb_[:, 0:Tt], pst[:, 0:Tt], AF.Square, [pk], [sk])
            mm(ssb[:, 0:Tt], ones_b[:], sb_[:, 0:Tt], j == 0, j == KC - 1, [sk, 'ones_b'], [ssk])
        rsqrt(rstd[:, 0:Tt], ssb[:, 0:Tt], 1.0 / D, [ssk, 'eps_c'], 'rstd')
        for j in range(KC):
            tb = tmpA[j % 2]; tk = ('tmpA', j % 2)
            eg = 'dve'
            tt(eg, tb[:, 0:Tt], obuf[:, j, 0:Tt], rstd[:, 0:Tt], ALU.mult, [('o', j), 'rstd'], [tk])
            for (c0, c1, w) in segs:
                stt('dve', xT[:, j, c0:c1], tb[:, c0:c1], mc(l, w, 2, j), xT[:, j, c0:c1], ALU.mult, ALU.add,
                    [tk, ('modc', l), ('x', j)], [('x', j)])

    def even_layer(l, ti, Tt, segs, chunks):
        i = l // 2
        last = (ti == NT - 1)
        has_s = (ti == 0)
        L = Tt + (3 if has_s else 0)
        if ti == 0:
            memset('dve', stT[i][:], 0.0, [('stT', i, g_) for g_ in range(4)])
            memset('dve', ccar[i][:], 0.0, [('ccar', i)])
            dma('sp', scaS[:], d_sca[i], [], ['scaS'], 'sin')
            dma('sp', stS[:], d_sst[i], [], STSK, 'sin')
            dma('pool', kS[:, :, 0:128], d_ckT[i], [], ['kS'], 'sinb')
            dma('pool', kSs[0:64, :, 0:128], d_ckT[i, 64:128], [], ['kSs'], 'sinb')
            dma('pool', kSs[64:128, :, 0:128], d_ckT[i, 0:64], [], ['kSs'], 'sinb')
            dma('pool', vS[:, 0:2, :], d_cv[i].rearrange("b s c -> s b c"), [], ['vS'], 'sinb')
            dma('sp', o_nks[i, :, :, 0:64], d_ckT[i, :, :, 64:128], [], [('onks', i)], 'outs')
            dma('sp', o_nvs[i, 0:64, :], d_cv[i, 1], [], [('onvs', i)], 'outs')
        ckpt(f'pre{l}_{ti}')
        tt('dve', Dm[i][:].rearrange("p (h c) -> p h c", h=32),
           dsk[:, i * 32:(i + 1) * 32].unsqueeze(2).to_broadcast([64, 32, 64]),
           ident_f[0:64, 0:64].unsqueeze(1).to_broadcast([64, 32, 64]), ALU.mult,
           ['dsk', 'ident_f'], [('Dm', 0)])
        phase_A(l, Tt, segs)
        ckpt(f'A{l}_{ti}')
        dbg(f'h{l}_{ti}', hT[:, :, 0:Tt], HK, [128, KC, Tt])
        def do_group(typ, gi):
            ckpt(f'G{typ}{gi}')
            wt, wk = wnext()
            if typ == 'dt':
                dtps, dtk = bank()
                for ci, c in enumerate(chunks):
                    for kc in range(KC):
                        mm(dtps[0:64, ci * 32:(ci + 1) * 32], hT[:, kc, c['col']:c['col'] + 64], wt[:, kc * 256:kc * 256 + 32],
                           kc == 0, kc == KC - 1, [wk, ('h', kc)], [dtk])
                nch = len(chunks)
                n32 = nch * 32
                xa = spx[:, 0:n32]
                dv = dtall[:, 0:n32]
                tt('dve', dv.rearrange("p (c h) -> p c h", h=32), dtps[0:64, 0:n32].rearrange("p (c h) -> p c h", h=32),
                   dtb[:, i * 32:(i + 1) * 32].unsqueeze(1).to_broadcast([64, nch, 32]), ALU.add, [dtk, 'dtb'], ['dtall'])
                act(xa, dv, AF.Abs, ['dtall'], ['spx'])
                act(xa, xa, AF.Exp, ['spx'], ['spx'], scale=-1.0)
                act(xa, xa, AF.Ln, ['spx', 'eps_c'], ['spx'], bias=eps_c[0:64, 1:2])
                stt('dve', dv, dv, 0.0, xa, ALU.max, ALU.add, ['dtall', 'spx'], ['dtall'])
            elif typ == 'xbc':
                for jj in range(2):
                    f = gi * 2 + jj
                    pst, pk = proj_fm(wt, wk, jj * 128, 256, Tt, lambda kc: hT[:, kc, 0:Tt], lambda kc: [('h', kc)], KC)
                    cs = cst[f % 2]; ck = ('cst', 0); ca = cacc[f % 2]; cak = ('cacc', 0)
                    cp('act', cs[:, 3:3 + TP], pst[:, 0:TP], [pk], [ck])
                    cp('dve', cs[:, 0:3], ccar[i][:, f * 3:f * 3 + 3], [('ccar', i)], [ck])
                    if has_s:
                        cp('act', cs[:, 6 + TP:6 + TP + CH], pst[:, TP:TP + CH], [pk], [ck])
                        cp('dve', cs[:, 3 + TP:6 + TP], scaS[:, f * 3:f * 3 + 3], ['scaS'], [ck])
                        cp('dve', scaS[:, f * 3:f * 3 + 3], cs[:, 3 + TP + CH:6 + TP + CH], [ck], ['scaS'])
                    cp('dve', ccar[i][:, f * 3:f * 3 + 3], cs[:, TP:TP + 3], [ck], [('ccar', i)])
                    wv = caw[:, (i * 24 + f) * 4:(i * 24 + f) * 4 + 4]
                    tsc('dve', ca[:, 0:L], cs[:, 3:3 + L], wv[:, 3:4], cab[:, i * 24 + f:i * 24 + f + 1], ALU.mult, ALU.add,
                        [ck, 'caw', 'cab'], [cak])
                    for tap in range(3):
                        stt('dve', ca[:, 0:L], cs[:, tap:tap + L], wv[:, tap:tap + 1], ca[:, 0:L], ALU.mult, ALU.add,
                            [ck, cak, 'caw'], [cak])
                    act(xbcs[:, f, 0:TP], ca[:, 0:TP], AF.Silu, [cak], [('xb', f)] + (keys('o', range(KC)) if f == 0 else []))
                    if has_s:
                        act(xbcs[:, f, TP:TP + CH], ca[:, TP + 3:TP + 3 + CH], AF.Silu, [cak], [('xb', f)])
            elif typ in ('z', 'g'):
                for jj in range(2):
                    f = gi * 2 + jj
                    pst, pk = proj_fm(wt, wk, jj * 128, 256, Tt, lambda kc: hT[:, kc, 0:Tt], lambda kc: [('h', kc)], KC)
                    mf = f if typ == 'z' else 16 + f
                    act(mix[:, mf, 0:Tt], pst[:, 0:Tt], AF.Silu, [pk], [('mix', mf)])
            elif typ == 'q':
                for jj in range(2):
                    f = gi * 2 + jj
                    pst, pk = proj_fm(wt, wk, jj * 128, 256, Tt, lambda kc: hT[:, kc, 0:Tt], lambda kc: [('h', kc)], KC)
                    tsc('dve', qT[:, f, 0:Tt], pst[:, 0:Tt], 0.125, None, ALU.mult, None, [pk], [('q', f)] + (keys('o', range(KC)) if f == 0 else []))
            elif typ == 'k':
                for jj in range(2):
                    f = gi * 2 + jj
                    pst, pk = proj_fm(wt, wk, jj * 128, 256, Tt, lambda kc: hT[:, kc, 0:Tt], lambda kc: [('h', kc)], KC)
                    cp('dve', kP[i][:, f, 128:128 + TP], pst[:, 0:TP], [pk], [('kP', i)])
                    ckpt(f'K1_{f}')
                    if has_s:
                        cp('dve', kS[:, f, 128:192], pst[:, TP:TP + CH], [pk], ['kS'])
                        ckpt(f'K2_{f}')
                        ko = kvout[f % 2]; kk = ('kvout', 0)
                        cp('dve', ko[:, 0:CH], pst[:, TP:TP + CH], [pk], [kk])
                        ckpt(f'K3_{f}')
                        dma('sp', o_nks[i, :, f, 64:128], ko[:, 0:CH], [kk], [('onks', i)], 'outs')
                        ckpt(f'K4_{f}')
                    if last:
                        ko = kvout[f % 2]; kk = ('kvout', 0)
                        cp('act', ko[:, 0:128], pst[:, TP - 128:TP], [pk], [kk])
                        dma('sp', o_nkp[i, :, f, :], ko[:, 0:128], [kk], [('onkp', i)], 'outs')
            elif typ == 'ks':
                for jj in range(2):
                    f = gi * 2 + jj
                    pst, pk = proj_fm(wt, wk, jj * 128, 256, Tt, lambda kc: hT[:, kc, 0:Tt], lambda kc: [('h', kc)], KC)
                    cp('dve', kPs[i][:, f, 128:128 + TP], pst[:, 0:TP], [pk], [('kPs', i)])
                    if has_s:
                        cp('dve', kSs[:, f, 128:192], pst[:, TP:TP + CH], [pk], ['kSs'])
            elif typ == 'v':
                for ci, c in enumerate(chunks):
                    pst, pk = bank()
                    for kc in range(KC):
                        mm(pst[0:64, 0:256], hT[:, kc, c['col']:c['col'] + 64], wt[:, kc * 256:(kc + 1) * 256],
                           kc == 0, kc == KC - 1, [wk, ('h', kc)], [pk])
                    if c['seg'] == 'p':
                        cp('dve', vP[i][:, 2 + c['cl'], gi * 256:(gi + 1) * 256], pst[0:64, 0:256], [pk], [('vP', i)])
                        if last and c['cl'] >= NCP - 2:
                            blk = c['cl'] - (NCP - 2)
                            ko = kvout[ci % 2]; kk = ('kvout', 0)
                            cp('act', ko[0:64, 0:256], pst[0:64, 0:256], [pk], [kk])
                            dma('sp', o_nvp[i, blk * 64:(blk + 1) * 64, gi * 256:(gi + 1) * 256], ko[0:64, 0:256], [kk], [('onvp', i)], 'outs')
                    else:
                        cp('dve', vS[:, 2, gi * 256:(gi + 1) * 256], pst[0:64, 0:256], [pk], ['vS'])
                        ko = kvout[ci % 2]; kk = ('kvout', 0)
                        cp('act', ko[0:64, 0:256], pst[0:64, 0:256], [pk], [kk])
                        dma('sp', o_nvs[i, 64:128, gi * 256:(gi + 1) * 256], ko[0:64, 0:256], [kk], [('onvs', i)], 'outs')

        NG1 = 1 + 12 + 8
        for (typ, gi) in EV_GROUPS[:NG1]:
            do_group(typ, gi)

        def projB2():
            for (typ, gi) in EV_GROUPS[NG1:]:
                do_group(typ, gi)
                yield
        if last:
            dma('sp', o_ncap[i], ccar[i][:], [('ccar', i)], [('oncap', i)], 'outs')
        if has_s:
            dma('sp', o_ncas[i], scaS[:], ['scaS'], [('oncas', i)], 'outs')
        dbg(f'xbcs{l}_{ti}', xbcs[:, :, 0:Tt], keys('xb', range(24)), [128, 24, Tt])
        dbg(f'dt{l}_{ti}', dtall[:, 0:len(chunks) * 32], ['dtall'], [64, len(chunks) * 32])
        dbg(f'q{l}_{ti}', qT[:, :, 0:Tt], keys('q', range(16)), [128, 16, Tt])
        ckpt(f'B{l}_{ti}')
        cp('act', stbf[:], stT[i][:], [('stT', i, g_) for g_ in range(4)], STBK)
        def ssd_ctx(ci, c):
            isS = c['seg'] == 's'
            d = dict(col=c['col'], isS=isS)
            d['st_f'] = stS if isS else stT[i]
            d['st_k'] = (lambda g: ('stS', g)) if isS else (lambda g: ('stT', i, g))
            d['dtc'] = dtall[:, ci * 32:(ci + 1) * 32]
            for n in ['dta', 'acum', 'ldt', 'amb', 'E2', 'wte', 'tw']:
                d[n] = sm[n][0:64, :]
            d['Edec'] = sm['Edec']
            return d

        def ssd_pre(ci, c):
            X = ssd_ctx(ci, c)
            dtc, dta, acum, ldt, amb, E2, wte, tw, Edec = (X[k_] for k_ in ['dtc', 'dta', 'acum', 'ldt', 'amb', 'E2', 'wte', 'tw', 'Edec'])
            col = c['col']
            isS = c['seg'] == 's'
            if isS:
                cp('act', stbf[:], stS[:], STSK, STBK)
            tt('dve', dta, dtc, aneg[:, i * 32:(i + 1) * 32], ALU.mult, ['dtall', 'aneg'], ['dta'])
            pss, psk = bank()
            mm(pss[0:64, 0:32], triu_f[:], dta, True, True, ['triu_f', 'dta'], [psk])
            mm(pss[:, 32:64], ones_f[:], dta, True, True, ['ones_f', 'dta'], [psk])
            cp('act', acum, pss[0:64, 0:32], [psk], ['acum'])
            cp('dve', ahl[:, 0:32], acum, ['acum'], ['ahl'])
            tt('dve', ldt, acum, ahl[:, 0:32], ALU.subtract, ['acum', 'ahl'], ['ldt'])
            cp('dve', ahl[:, 32:64], ldt, ['ldt'], ['ahl'])
            act(ldt, dtc, AF.Ln, ['dtall'], ['ldt'])
            tt('dve', amb, acum, ldt, ALU.subtract, ['acum', 'ldt'], ['amb'])
            act(E2, acum, AF.Exp, ['acum'], ['E2'])
            tt('dve', tw, pss[0:64, 32:64], acum, ALU.subtract, [psk, 'acum'], ['tw'])
            act(tw, tw, AF.Exp, ['tw'], ['tw'])
            tt('dve', wte, tw, dtc, ALU.mult, ['tw', 'dtall'], ['wte'])
            act(Edec[:], pss[:, 32:64], AF.Exp, [psk], ['Edec'])
            yield
            for b0 in range(0, 20, 8):
                nb_ = min(8, 20 - b0)
                ptb, ptk = bank()
                pv = ptb[:].bitcast(BF16)
                for f in range(b0, b0 + nb_):
                    tr(pv[0:64, (f - b0) * 128:(f - b0 + 1) * 128], xbcs[:, f, col:col + 64], ident_b[:], [('xb', f), 'ident_b'], [ptk])
                cp('act' if b0 == 8 else 'dve', xtok[:, b0 * 128:(b0 + nb_) * 128], pv[0:64, 0:nb_ * 128], [ptk], ['xtok'])
            tt('dve', xw[:].rearrange("p (h c) -> p h c", h=32), xtok[:, 0:2048].rearrange("p (h c) -> p h c", h=32),
               wte.unsqueeze(2).to_broadcast([64, 32, 64]), ALU.mult, ['xtok', 'wte'], ['xw'])
            yield
            pcb, pcbk = bank()
            for g in range(4):
                mm(pcb[0:64, g * 64:(g + 1) * 64], xbcs[:, 16 + g, col:col + 64], xbcs[:, 20 + g, col:col + 64], True, True,
                   [('xb', 16 + g), ('xb', 20 + g)], [pcbk])
            cp('act', cbT[:], pcb[0:64, 0:256], [pcbk], ['cbT'])
            yield

        def ssd_grp(ci, c, groups, b2):
            X = ssd_ctx(ci, c)
            col = X['col']; st_f = X['st_f']; st_kf = X['st_k']
            acum, amb, E2, Edec = X['acum'], X['amb'], X['E2'], X['Edec']
            st_b = stbf
            for g in groups:
                hs = slice(g * 8, (g + 1) * 8)
                for (dgt, off) in ((Dgh[b2], 0), (Dgl[b2], 32)):
                    tt('pool', dgt.rearrange("p (r c) -> p r c", r=8), ahl[:, off + g * 8:off + (g + 1) * 8].unsqueeze(2).to_broadcast([64, 8, 64]),
                       ident_b[0:64, 0:64].unsqueeze(1).to_broadcast([64, 8, 64]), ALU.mult, ['ahl', 'ident_b'], DGK[b2])
                pA, pAk = bank()
                mm(pA[0:64, :], ones_b[0:64, 0:64], Dgh[b2], True, False, ['ones_b'] + DGK[b2], [pAk])
                mm(pA[0:64, :], ones_b[0:64, 0:64], Dgl[b2], False, False, ['ones_b'] + DGK[b2], [pAk])
                mm(pA[0:64, :], ident_b[0:64, 0:64], mask_b[:], False, True, ['ident_b', 'mask_b'], [pAk])
                tt('dve', seg[b2][:].rearrange("p (r c) -> p r c", r=8), pA[0:64, :].rearrange("p (r c) -> p r c", r=8),
                   amb[:, hs].unsqueeze(2).to_broadcast([64, 8, 64]), ALU.subtract, [pAk, 'amb'], [('seg', b2)])
                act(seg[b2][:], seg[b2][:], AF.Exp, [('seg', b2)], [('seg', b2)])
                tt('dve', mixT[b2][:].rearrange("p (r c) -> p r c", r=8), seg[b2][:].rearrange("p (r c) -> p r c", r=8),
                   cbT[:, g * 64:(g + 1) * 64].unsqueeze(1).to_broadcast([64, 8, 64]), ALU.mult, [('seg', b2), 'cbT'], [('mixT', b2)])
                yield
                py, pyk = bank()
                for r in range(8):
                    h = g * 8 + r
                    mm(py[0:64, r * 64:(r + 1) * 64], mixT[b2][:, r * 64:(r + 1) * 64], xtok[:, h * 64:(h + 1) * 64], True, False,
                       [('mixT', b2), 'xtok'], [pyk])
                    mm(py[0:64, r * 64:(r + 1) * 64], Dm[i][:, h * 64:(h + 1) * 64], xtok[:, h * 64:(h + 1) * 64], False, True,
                       [('Dm', 0), 'xtok'], [pyk])
                po, pok = bank()
                mm(po[0:64, :], xbcs[:, 20 + g, col:col + 64], st_b[:, g * 512:(g + 1) * 512], True, True, [('xb', 20 + g), ('stbf', g)], [pok])
                tt('dve', t1[b2][:].rearrange("p (r c) -> p r c", r=8), po[0:64, :].rearrange("p (r c) -> p r c", r=8),
                   E2[:, hs].unsqueeze(2).to_broadcast([64, 8, 64]), ALU.mult, [pok, 'E2'], [('t1', b2)])
                tt('dve', ytok[b2][:], t1[b2][:], py[0:64, :], ALU.add, [('t1', b2), pyk], [('ytok', b2)])
                yield
                pyt, pytk = bank()
                pyv = pyt[:].bitcast(BF16)
                for fc in range(4):
                    tr(pyv[:, fc * 64:(fc + 1) * 64], ytok[b2][:, fc * 128:(fc + 1) * 128], ident_b[0:64, 0:64], [('ytok', b2), 'ident_b'], [pytk])
                mz = mix[:, g * 4:(g + 1) * 4, col:col + 64]
                mzk = keys('mix', range(g * 4, g * 4 + 4))
                gyv = gy[b2][:].rearrange("p (a c) -> p a c", a=4)
                tt('dve', gyv, pyv[:, 0:256].rearrange("p (a c) -> p a c", a=4), mz, ALU.mult, [pytk] + mzk, [('gy', b2)])
                act(sqg[b2][:], gy[b2][:], AF.Square, [('gy', b2)], [('sqg', b2)])
                pss2, pss2k = bank()
                for fc in range(4):
                    mm(pss2[:, 0:64], ones_b[:], sqg[b2][:, fc * 64:(fc + 1) * 64], fc == 0, fc == 3, [('sqg', b2), 'ones_b'], [pss2k])
                rsqrt(rsg[b2][:], pss2[:, 0:64], 1.0 / 512, [pss2k, 'eps_c'], ('rsg', b2))
                tt('dve', gyv, gyv, rsg[b2][:].unsqueeze(1).to_broadcast([128, 4, 64]), ALU.mult, [('gy', b2), ('rsg', b2)], [('gy', b2)])
                tt('pool', mz, gyv, nssm[:, i * 16 + g * 4:i * 16 + g * 4 + 4].unsqueeze(2).to_broadcast([128, 4, 64]), ALU.mult,
                   [('gy', b2), 'nssm'], mzk)
                yield
                pst_, pstk = bank()
                mm(pst_[:, :], xtok[:, 2048 + g * 128:2048 + (g + 1) * 128], xw[:, g * 512:(g + 1) * 512], True, True, ['xtok', 'xw'], [pstk])
                sv = st_f[:, g * 512:(g + 1) * 512]
                tt('dve', sv.rearrange("p (r c) -> p r c", r=8), sv.rearrange("p (r c) -> p r c", r=8),
                   Edec[:, hs].unsqueeze(2).to_broadcast([128, 8, 64]), ALU.mult, [st_kf(g), 'Edec', ('stbf', g)], [st_kf(g)])
                tt('dve', sv, sv, pst_[:, :], ALU.add, [st_kf(g), pstk], [st_kf(g)])
                cp('act', st_b[:, g * 512:(g + 1) * 512], sv, [st_kf(g)], [('stbf', g)])
                yield
            yield

        def attn_gen(ci, c, quads, sidx):
            col = c['col']
            isS = c['seg'] == 's'
            if isS:
                KB, KBs, kkey, kskey = kS, kSs, 'kS', 'kSs'
                kc0 = 0; nblk = 3
                vblk = [(vS, j, 'vS') for j in range(3)]
            else:
                KB, KBs, kkey, kskey = kP[i], kPs[i], ('kP', i), ('kPs', i)
                nblk = min(3, c['gidx'] + 1)
                kc0 = 64 * c['cl'] + 64 * (3 - nblk)
                vblk = [(vP[i], c['cl'] + (3 - nblk) + j, ('vP', i)) for j in range(nblk)]
            nv = nblk * 64
            psWt = PW[sidx]; pwk = PWK[sidx]
            PQK = PKS[sidx]; PXK = PKS[sidx]
            for qd in quads:
                b2 = sidx
                kv = qd
                fk = kv // 2; khalf = kv % 2
                scv = psWt[0:64, :].rearrange("p (h c) -> p h c", h=4)
                for hh in range(4):
                    half = hh // 2; fq = qd * 2 + hh % 2
                    ksrc, ksk = (KB, kkey) if khalf == half else (KBs, kskey)
                    mm(psWt[0:64, hh * 256:hh * 256 + nv], qT[half * 64:(half + 1) * 64, fq, col:col + 64],
                       ksrc[half * 64:(half + 1) * 64, fk, kc0:kc0 + nv], True, True, [('q', fq), ksk], [pwk])
                ckpt(f'AT1_{qd}')
                A = asm[b2]
                ak = lambda n: ('asm', b2, n)
                T.add('dve', lambda e, o=A['mx'][:], i_=scv[:, :, 0:nv]: e.reduce_max(o, i_, AX.X), [pwk], [ak('mx')])
                snq = snk[:, i * 32 + qd * 4:i * 32 + qd * 4 + 4].rearrange("p (f h) -> p h f", h=2)
                hv = lambda t_: t_[:].rearrange("p (h f) -> p h f", f=2)
                tt('dve', hv(A['mx']), hv(A['mx']), snq, ALU.max, [ak('mx'), 'snk'], [ak('mx')])
                tsc('dve', A['negm'][:], A['mx'][:], -1.0, None, ALU.mult, None, [ak('mx')], [ak('negm')])
                Pv = Pq[b2][:].rearrange("p (h c) -> p h c", h=4)
                memset('dve', A['rs'][:], 0.0, [ak('rs')])
                for hh in range(4):
                    act(Pv[:, hh, 0:nv], scv[:, hh, 0:nv], AF.Exp, [pwk, ak('negm')], [*PQK, ak('rs')],
                        bias=A['negm'][:, hh:hh + 1], accum=A['rs'][:, hh:hh + 1])
                tt('dve', hv(A['es']), snq, hv(A['negm']), ALU.add, ['snk', ak('negm')], [ak('es')])
                act(A['es'][:], A['es'][:], AF.Exp, [ak('es')], [ak('es')])
                tt('dve', A['den'][:], A['rs'][:], A['es'][:], ALU.add, [ak('rs'), ak('es')], [ak('den')])
                T.add('dve', lambda e, o=A['rinv'][:], i_=A['den'][:]: e.reciprocal(o, i_), [ak('den')], [ak('rinv')])
                Pnv = Pn[b2][:].rearrange("p (h c) -> p h c", h=4)
                tt('dve', Pnv[:, :, 0:nv], Pv[:, :, 0:nv], A['rinv'][:].unsqueeze(2).to_broadcast([64, 4, nv]), ALU.mult,
                   [*PQK, ak('rinv')], [*PXK])
                ckpt(f'AT2_{qd}')
                yield
                ppt, pptk = bank()
                ppv = ppt[:].bitcast(BF16)
                for hh in range(4):
                    for j in range(nblk):
                        tr(ppv[0:64, (hh * 3 + j) * 64:(hh * 3 + j + 1) * 64], Pnv[:, hh, j * 64:(j + 1) * 64], ident_b[0:64, 0:64],
                           [*PXK, 'ident_b'], [pptk])
                cp('act', PTq[b2][:, 0:768].rearrange("p (h c) -> p h c", h=4)[:, :, 0:nv], ppv[0:64, 0:768].rearrange("p (h c) -> p h c", h=4)[:, :, 0:nv], [pptk], [*PXK])
                ckpt(f'AT3_{qd}')
                pov, povk = bank()
                for hh in range(4):
                    fql = hh % 2; half = hh // 2
                    for j in range(nblk):
                        vt, vb, vk = vblk[j]
                        mm(pov[half * 64:(half + 1) * 64, fql * 64:(fql + 1) * 64], vt[:, vb, kv * 64:(kv + 1) * 64],
                           PTq[b2][:, (hh * 3 + j) * 64:(hh * 3 + j + 1) * 64], j == 0, j == nblk - 1, [vk, *PXK], [povk])
                ckpt(f'AT4_{qd}')
                mg = mix[:, 16 + qd * 2:16 + qd * 2 + 2, col:col + 64]
                mgk = keys('mix', range(16 + qd * 2, 16 + qd * 2 + 2))
                tt('dve', mg, pov[:, 0:128].rearrange("p (a c) -> p a c", a=2), mg, ALU.mult, [povk] + mgk, mgk)
                yield
            yield

        pb2 = projB2()
        active = [pb2]
        prev_attn = []

        def drain(required):
            req = list(required)
            for g_ in req:
                if g_ not in active:
                    active.append(g_)
            while any(g_ in active for g_ in req):
                for g_ in list(active):
                    try:
                        next(g_)
                    except StopIteration:
                        active.remove(g_)

        memset('dve', modt[:, 0:1], 0.0, [('modt', 0)] + SCRK + SETB)
        for ci, c in enumerate(chunks):
            drain([ssd_pre(ci, c)])
            if ci == 0:
                drain([ssd_grp(ci, c, [0, 1, 2, 3], 0)])
            else:
                drain([ssd_grp(ci, c, [0, 1], 0), ssd_grp(ci, c, [2, 3], 1)])
            ckpt(f'S{l}_{ti}_{ci}')
            if ci == 0:
                drain([pb2])
            drain([g_ for g_ in prev_attn if g_ in active])
            prev_attn[:] = [attn_gen(ci, c, range(0, 4), 0), attn_gen(ci, c, range(4, 8), 1)]
            active.extend(prev_attn)
        drain(list(active))
        memset('dve', modt[:, 0:1], 0.0, [('modt', 0)] + SCRK + SETB)
        dbg(f'mix{l}_{ti}', mix[:, :, 0:Tt], keys('mix', range(32)), [128, 32, Tt])
        ckpt(f'C{l}_{ti}')
        if has_s:
            dma('sp', o_nsss[i], stS[:], STSK, [('onsss', i)], 'outs')
        if last:
            dma('sp', o_nssp[i], stT[i][:], [('stT', i, g_) for g_ in range(4)], [('onssp', i)], 'outs')
        else:
            cp('pool', kP[i][:, :, 0:128], kP[i][:, :, TP:TP + 128], [('kP', i)], [('kP', i)])
            cp('pool', kPs[i][:, :, 0:128], kPs[i][:, :, TP:TP + 128], [('kPs', i)], [('kPs', i)])
            cp('pool', vP[i][:, 0:2, :], vP[i][:, NCP:NCP + 2, :], [('vP', i)], [('vP', i)])
        phase_D(l, Tt, segs, last, ti)

    def odd_layer(l, ti, Tt, segs, chunks):
        i = l // 2
        last = (ti == NT - 1)
        has_s = (ti == 0)
        L = Tt + (2 if has_s else 0)
        if ti == 0:
            memset('dve', vcar[i][:], 0.0, [('vcar', i)])
            dma('sp', sccS[:], d_scc[i], [], ['sccS'], 'sin')
        phase_A(l, Tt, segs)
        for j in range(32):
            wt, wk = wnext()
            pu, puk = proj_fm(wt, wk, 0, 256, Tt, lambda kc: hT[:, kc, 0:Tt], lambda kc: [('h', kc)], KC)
            pgc, pgck = proj_fm(wt, wk, 128, 256, Tt, lambda kc: hT[:, kc, 0:Tt], lambda kc: [('h', kc)], KC)
            wt2, wk2 = wnext()
            pgb, pgbk = proj_fm(wt2, wk2, 0, 256, Tt, lambda kc: hT[:, kc, 0:Tt], lambda kc: [('h', kc)], KC)
            pg, pgk = proj_fm(wt2, wk2, 128, 256, Tt, lambda kc: hT[:, kc, 0:Tt], lambda kc: [('h', kc)], KC)
            su = stg[(j % 2) * 2]; suk = ('tmpA', 0) if j % 2 == 0 else ('stg', 2)
            sg_ = stg[(j % 2) * 2 + 1]; sgk = ('tmpA', 1) if j % 2 == 0 else ('stg', 3)
            cs = cst[j % 2]; ck = ('cst', 0); ca = cacc[j % 2]; cak = ('cacc', 0)
            cp('act', su[:, 0:Tt], pu[:, 0:Tt], [puk], [suk])
            tt('dve', cs[:, 2:2 + TP], pgc[:, 0:TP], su[:, 0:TP], ALU.mult, [pgck, suk], [ck])
            cp('dve', cs[:, 0:2], vcar[i][:, j * 2:j * 2 + 2], [('vcar', i)], [ck])
            if has_s:
                tt('dve', cs[:, 4 + TP:4 + TP + CH], pgc[:, TP:TP + CH], su[:, TP:TP + CH], ALU.mult, [pgck, suk], [ck])
                cp('dve', cs[:, 2 + TP:4 + TP], sccS[:, j * 2:j * 2 + 2], ['sccS'], [ck])
                cp('dve', sccS[:, j * 2:j * 2 + 2], cs[:, 2 + TP + CH:4 + TP + CH], [ck], ['sccS'])
            cp('dve', vcar[i][:, j * 2:j * 2 + 2], cs[:, TP:TP + 2], [ck], [('vcar', i)])
            wv = ccw[:, (i * 32 + j) * 3:(i * 32 + j) * 3 + 3]
            tsc('dve', ca[:, 0:L], cs[:, 0:L], wv[:, 0:1], None, ALU.mult, None, [ck, 'ccw'], [cak])
            for tap in (1, 2):
                stt('dve', ca[:, 0:L], cs[:, tap:tap + L], wv[:, tap:tap + 1], ca[:, 0:L], ALU.mult, ALU.add, [ck, cak, 'ccw'], [cak])
            act(sg_[:, 0:Tt], pg[:, 0:Tt], AF.Silu, [pgk], [sgk])
            tt('dve', ca[:, 0:TP], ca[:, 0:TP], pgb[:, 0:TP], ALU.mult, [cak, pgbk], [cak])
            tt('pool', mix[:, j, 0:TP], ca[:, 0:TP], sg_[:, 0:TP], ALU.mult, [cak, sgk], [('mix', j)])
            if has_s:
                tt('dve', ca[:, TP + 2:TP + 2 + CH], ca[:, TP + 2:TP + 2 + CH], pgb[:, TP:TP + CH], ALU.mult, [cak, pgbk], [cak])
                tt('pool', mix[:, j, TP:TP + CH], ca[:, TP + 2:TP + 2 + CH], sg_[:, TP:TP + CH], ALU.mult, [cak, sgk], [('mix', j)])
        if last:
            dma('sp', o_nccp[i], vcar[i][:], [('vcar', i)], [('onccp', i)], 'outs')
        if has_s:
            dma('sp', o_nccs[i], sccS[:], ['sccS'], [('onccs', i)], 'outs')
        dbg(f'mix{l}_{ti}', mix[:, :, 0:Tt], keys('mix', range(32)), [128, 32, Tt])
        phase_D(l, Tt, segs, last, ti)

    def main_schedule():
        for ti in range(NT):
            has_s = (ti == 0)
            Tt = TP + (CH if has_s else 0)
            segs = [(0, TP, 0)] + ([(TP, TP + CH, 1)] if has_s else [])
            chunks = [dict(col=cl * CH, seg='p', cl=cl, gidx=ti * NCP + cl) for cl in range(NCP)]
            if has_s:
                chunks.append(dict(col=TP, seg='s', cl=0, gidx=0))
            dma('sp', xT[:, :, 0:TP], d_xp[:, :, ti * TP:(ti + 1) * TP], [], XK, 'xin')
            if has_s:
                dma('sp', xT[:, :, TP:TP + CH], d_xs, [], XK, 'xin')
            for l in range(DEPTH):
                if l % 2 == 0:
                    even_layer(l, ti, Tt, segs, chunks)
                else:
                    odd_layer(l, ti, Tt, segs, chunks)
                dbg(f'x{l}_{ti}', xT[:, :, 0:Tt], XK, [128, KC, Tt])
            dma('sp', o_yp[:, :, ti * TP:(ti + 1) * TP], xT[:, :, 0:TP], XK, [('oyp', ti)], 'xout')
            if has_s:
                dma('sp', o_ys, xT[:, :, TP:TP + CH], XK, ['oys'], 'xout')


    try:
        ckpt('prologue')
        main_schedule()
    except _Stop:
        pass

    T.emit(nc, es)
    es.close()
    return nc, sorted(dbg_out.keys())


def _fm(a, nch):
    return np.ascontiguousarray(a.reshape(nch, 128, -1).transpose(1, 0, 2))


def _wgroups(w, col_lists, kchunks):
    out = []
    for cols in col_lists:
        cols = np.asarray(cols)
        blk = w[:, np.maximum(cols, 0)]
        if (cols < 0).any():
            blk = blk.copy(); blk[:, cols < 0] = 0.0
        out.append(blk.reshape(kchunks, 128, len(cols)).transpose(1, 0, 2).reshape(128, kchunks * len(cols)))
    return np.ascontiguousarray(np.stack(out))


def prep_shared(inp, cfg):
    DEPTH = cfg['DEPTH']
    NE = (DEPTH + 1) // 2; NO = DEPTH // 2
    f32 = np.float32
    sh = {}
    sh['wada'] = np.ascontiguousarray(inp['w_ada'].reshape(DEPTH, KC, 128, 6144).transpose(0, 2, 1, 3))
    sh['bada'] = np.ascontiguousarray(inp['b_ada'].reshape(DEPTH, 48, 128).transpose(2, 0, 1).reshape(128, DEPTH * 48))
    sh['npre'] = np.ascontiguousarray(inp['norm_pre'].reshape(DEPTH, 16, 128).transpose(2, 0, 1).reshape(128, DEPTH * 16))
    sh['npost'] = np.ascontiguousarray(inp['norm_post'].reshape(DEPTH, 16, 128).transpose(2, 0, 1).reshape(128, DEPTH * 16))
    sh['c_ident'] = np.eye(128, dtype=f32)
    sh['c_triu'] = np.triu(np.ones((64, 64), f32))
    m = np.where(np.arange(64)[None, :] >= np.arange(64)[:, None], 0.0, -1e30).astype(f32)
    sh['c_mask'] = np.ascontiguousarray(np.tile(m, (1, 8)))
    for i in range(NE):
        w = inp['w_in_even'][i]
        gl = []
        for (typ, gi) in EV_GROUPS:
            c0 = EV_COLS[typ][0] if typ in EV_COLS else 0
            if typ == 'ks':
                c0 = EV_COLS['k'][0]
                cols = []
                for jj in range(2):
                    b0 = c0 + (gi * 2 + jj) * 128
                    cols += list(range(b0 + 64, b0 + 128)) + list(range(b0, b0 + 64))
            elif typ == 'dt':
                cols = list(range(c0, c0 + 32)) + [-1] * 224
            else:
                cols = list(range(c0 + gi * 256, c0 + (gi + 1) * 256))
            gl.append(cols)
        sh[f'wie{i}'] = _wgroups(w, gl, 16)
        sh[f'woe{i}'] = _wgroups(inp['w_out_even'][i], [list(range(g * 128, (g + 1) * 128)) for g in range(16)], 32)
    for i in range(NO):
        w = inp['w_in_odd'][i]
        gl = []
        for j in range(32):
            gl.append(list(range(j * 128, (j + 1) * 128)) + list(range(8192 + j * 128, 8192 + (j + 1) * 128)))
            gl.append(list(range(4096 + j * 128, 4096 + (j + 1) * 128)) + list(range(12288 + j * 128, 12288 + (j + 1) * 128)))
        sh[f'wio{i}'] = _wgroups(w, gl, 16)
        sh[f'woo{i}'] = _wgroups(inp['w_out_odd'][i], [list(range(g * 128, (g + 1) * 128)) for g in range(16)], 32)
    sh['caw'] = np.ascontiguousarray(inp['conv_a_w'].reshape(NE, 4, 24, 128).transpose(3, 0, 2, 1).reshape(128, NE * 96))
    sh['cab'] = np.ascontiguousarray(inp['conv_a_b'].reshape(NE, 24, 128).transpose(2, 0, 1).reshape(128, NE * 24))
    sh['nssm'] = np.ascontiguousarray(inp['norm_ssm'].reshape(NE, 16, 128).transpose(2, 0, 1).reshape(128, NE * 16))
    for nm, key in [('dtb', 'dt_bias'), ('alog', 'a_log'), ('dsk', 'd_skip'), ('snk', 'sinks')]:
        sh[nm] = np.ascontiguousarray(np.broadcast_to(inp[key].reshape(1, NE * 32), (64, NE * 32)))
    if NO > 0:
        sh['ccw'] = np.ascontiguousarray(inp['conv_c_w'].reshape(NO, 3, 32, 128).transpose(3, 0, 2, 1).reshape(128, NO * 96))
    else:
        sh['ccw'] = np.zeros((128, 96), f32)
    return sh


def prep_core(inp, b, cfg):
    DEPTH = cfg['DEPTH']
    NE = (DEPTH + 1) // 2; NO = DEPTH // 2
    c = {}
    c['xp'] = _fm(inp['x_prompt'][b].T, 16)
    c['xs'] = _fm(inp['x_sample'][b].T, 16)
    cc = np.stack([inp['c_prompt'][b], inp['c_sample'][b]], axis=1)
    c['cT'] = _fm(cc, 16).reshape(128, 32)
    c['ckT'] = np.ascontiguousarray(inp['cache_k'][:, b].reshape(NE, 128, 4, 128).transpose(0, 3, 2, 1))
    c['cv'] = np.ascontiguousarray(inp['cache_v'][:, b].reshape(NE, 2, 64, 512))
    c['sca'] = np.ascontiguousarray(inp['state_conv_a'][:, b].reshape(NE, 3, 24, 128).transpose(0, 3, 2, 1).reshape(NE, 128, 72))
    c['sst'] = np.ascontiguousarray(inp['state_ssm'][:, b].reshape(NE, 2048, 128).transpose(0, 2, 1))
    if NO > 0:
        c['scc'] = np.ascontiguousarray(inp['state_conv_c'][:, b].reshape(NO, 2, 32, 128).transpose(0, 3, 2, 1).reshape(NO, 128, 64))
    else:
        c['scc'] = np.zeros((1, 128, 64), np.float32)
    return c


def assemble(results, cfg, nb):
    DEPTH = cfg['DEPTH']; SEQ = cfg['SEQ']
    NE = (DEPTH + 1) // 2; NO = DEPTH // 2

    def st(f):
        return np.stack([f(r) for r in results], axis=0)

    def unfm(a):
        return a.transpose(2, 1, 0).reshape(a.shape[2], -1)
    yp = st(lambda r: unfm(r['yp']))
    ys = st(lambda r: unfm(r['ys']))

    def kfix(a):
        return a.transpose(0, 3, 2, 1).reshape(NE, 128, 8, 64)

    def cafix(a):
        return a.reshape(NE, 128, 24, 3).transpose(0, 3, 2, 1).reshape(NE, 3, 3072)

    def ssfix(a):
        return a.transpose(0, 2, 1).reshape(NE, 32, 64, 128)

    def ccfix(a):
        return a[:NO].reshape(NO, 128, 32, 2).transpose(0, 3, 2, 1).reshape(NO, 2, 4096)
    outs = [yp, ys]
    for sfx in ['p', 's']:
        outs.append(np.stack([kfix(r['nk' + sfx]) for r in results], axis=1))
        outs.append(np.stack([r['nv' + sfx].reshape(NE, 128, 8, 64) for r in results], axis=1))
        outs.append(np.stack([cafix(r['nca' + sfx]) for r in results], axis=1))
        outs.append(np.stack([ssfix(r['nss' + sfx]) for r in results], axis=1))
        outs.append(np.stack([ccfix(r['ncc' + sfx]) for r in results], axis=1))
    return tuple(np.ascontiguousarray(o.astype(np.float32)) for o in outs)


_CACHE = {}


def kernel(**inputs):
    inp = {k: np.asarray(v) for k, v in inputs.items()}
    B, SEQ, _ = inp['x_prompt'].shape
    DEPTH = inp['w_ada'].shape[0]
    cfg = dict(SEQ=SEQ, DEPTH=DEPTH, TP=256)
    key = (SEQ, DEPTH)
    if key not in _CACHE:
        _CACHE[key] = build(cfg)[0]
    nc = _CACHE[key]
    sh = prep_shared(inp, cfg)
    in_maps = []
    for b in range(B):
        m = dict(sh)
        m.update(prep_core(inp, b, cfg))
        in_maps.append(m)
    res = run_bass_kernel_spmd(nc, in_maps, core_ids=list(range(B)))
    return assemble(res.results, cfg, B)
```

```python
import numpy as np
from contextlib import ExitStack
import concourse.bass as bass
import concourse.mybir as mybir
from concourse.bass_utils import run_bass_kernel_spmd

F32 = mybir.dt.float32
BF16 = mybir.dt.bfloat16
AF = mybir.ActivationFunctionType
ALU = mybir.AluOpType
AX = mybir.AxisListType

D = 2048
KC = 16
CH = 64
EPS = 1e-6
NS = 3
SAME_SYNC = True

EV_COLS = dict(z=(0, 2048), xbc=(2048, 5120), dt=(5120, 5152), q=(5152, 7200), k=(7200, 7712),
               v=(7712, 8224), g=(8224, 10272))
EV_GROUPS = ([('dt', 0)] + [('xbc', i) for i in range(12)] + [('z', i) for i in range(8)] +
             [('k', i) for i in range(2)] + [('ks', i) for i in range(2)] + [('v', i) for i in range(2)] +
             [('q', i) for i in range(8)] + [('g', i) for i in range(8)])


class Op:
    __slots__ = ('eng', 'fn', 'waits', 'inc', 'chan', 'count')

    def __init__(self, eng, fn, chan):
        self.eng = eng; self.fn = fn; self.chan = chan; self.waits = set(); self.inc = False; self.count = 0


class Tracker:
    def __init__(self):
        self.ops = []
        self.lastw = {}
        self.readers = {}
        self.chan_last = {}

    def add(self, eng, fn, reads=(), writes=(), chan=None):
        idx = len(self.ops)
        op = Op(eng, fn, chan)
        deps = set()
        for k in reads:
            w = self.lastw.get(k)
            if w is not None:
                deps.add(w)
            if k in ('psW', 'psW2') or (isinstance(k, tuple) and k[0] == 'ps'):
                for st_, r in self.readers.get(k, {}).items():
                    if st_ != (('c', chan) if chan else ('e', eng)):
                        deps.add(r)
        for k in writes:
            w = self.lastw.get(k)
            if w is not None:
                deps.add(w)
            for r in self.readers.get(k, {}).values():
                deps.add(r)
        stream = ('c', chan) if chan else ('e', eng)
        for k in reads:
            self.readers.setdefault(k, {})[stream] = idx
        for k in writes:
            self.lastw[k] = idx
            self.readers[k] = {}
        for d in deps:
            p = self.ops[d]
            if p.chan is not None:
                op.waits.add(self.chan_last[p.chan])
            else:
                if p.eng == eng and chan is None and (eng == 'pe' or not SAME_SYNC):
                    continue
                p.inc = True
                op.waits.add(d)
        if chan:
            prev = self.chan_last.get(chan)
            if prev is not None:
                op.waits.add(prev)
            self.chan_last[chan] = idx
        self.ops.append(op)

    def emit(self, nc, es):
        engines = {'pe': nc.tensor, 'act': nc.scalar, 'dve': nc.vector, 'pool': nc.gpsimd, 'sp': nc.sync}
        ecount = {}
        ccount = {}
        for op in self.ops:
            if op.chan:
                ccount[op.chan] = ccount.get(op.chan, 0) + 16
                op.count = ccount[op.chan]
            elif op.inc:
                ecount[op.eng] = ecount.get(op.eng, 0) + 1
                op.count = ecount[op.eng]
        sems = {}
        for e in ecount:
            sems[('e', e)] = es.enter_context(nc.semaphore('se_' + e))
        for c in ccount:
            sems[('c', c)] = es.enter_context(nc.semaphore('sc_' + c))
        waited = {e: {} for e in engines}
        for op in self.ops:
            E = engines[op.eng]
            need = {}
            for d in op.waits:
                p = self.ops[d]
                key = ('c', p.chan) if p.chan else ('e', p.eng)
                need[key] = max(need.get(key, 0), p.count)
            wd = waited[op.eng]
            for key, val in need.items():
                if wd.get(key, 0) < val:
                    E.wait_ge(sems[key], val)
                    wd[key] = val
            ins = op.fn(E)
            if op.chan:
                ins.then_inc(sems[('c', op.chan)], 16)
            elif op.inc:
                ins.then_inc(sems[('e', op.eng)], 1)
        for c, val in ccount.items():
            nc.sync.wait_ge(sems[('c', c)], val)


class _Stop(Exception):
    pass


def build(cfg, dbg_names=()):
    SEQ = cfg['SEQ']; DEPTH = cfg['DEPTH']; TP = cfg['TP']
    NT = SEQ // TP
    TW = TP + CH
    NE = (DEPTH + 1) // 2
    NO = DEPTH // 2
    NCP = TP // CH
    nc = bass.Bass("TRN2", target_bir_lowering=False)
    es = ExitStack()
    T = Tracker()
    dbg_out = {}

    def din(name, shape):
        return nc.dram_tensor(name, list(shape), F32, kind="ExternalInput").ap()

    def dout(name, shape):
        return nc.dram_tensor(name, list(shape), F32, kind="ExternalOutput").ap()

    d_xp = din('xp', [128, KC, SEQ]); d_xs = din('xs', [128, KC, CH]); d_cT = din('cT', [128, KC * 2])
    d_wada = din('wada', [DEPTH, 128, KC, 6144])
    d_bada = din('bada', [128, DEPTH * 48]); d_npre = din('npre', [128, DEPTH * 16]); d_npost = din('npost', [128, DEPTH * 16])
    d_ident = din('c_ident', [128, 128]); d_triu = din('c_triu', [64, 64]); d_mask = din('c_mask', [64, 512])
    d_wie = [din(f'wie{i}', [len(EV_GROUPS), 128, 4096]) for i in range(NE)]
    d_woe = [din(f'woe{i}', [16, 128, 4096]) for i in range(NE)]
    d_wio = [din(f'wio{i}', [64, 128, 4096]) for i in range(NO)]
    d_woo = [din(f'woo{i}', [16, 128, 4096]) for i in range(NO)]
    d_caw = din('caw', [128, NE * 24 * 4]); d_cab = din('cab', [128, NE * 24])
    d_dtb = din('dtb', [64, NE * 32]); d_alog = din('alog', [64, NE * 32]); d_dsk = din('dsk', [64, NE * 32])
    d_snk = din('snk', [64, NE * 32]); d_nssm = din('nssm', [128, NE * 16])
    d_ccw = din('ccw', [128, max(NO, 1) * 32 * 3])
    d_ckT = din('ckT', [NE, 128, 4, 128]); d_cv = din('cv', [NE, 2, 64, 512])
    d_sca = din('sca', [NE, 128, 24 * 3]); d_sst = din('sst', [NE, 128, 2048]); d_scc = din('scc', [max(NO, 1), 128, 32 * 2])
    o_yp = dout('yp', [128, KC, SEQ]); o_ys = dout('ys', [128, KC, CH])
    o_nkp = dout('nkp', [NE, 128, 4, 128]); o_nvp = dout('nvp', [NE, 128, 512]); o_ncap = dout('ncap', [NE, 128, 72])
    o_nssp = dout('nssp', [NE, 128, 2048]); o_nccp = dout('nccp', [max(NO, 1), 128, 64])
    o_nks = dout('nks', [NE, 128, 4, 128]); o_nvs = dout('nvs', [NE, 128, 512]); o_ncas = dout('ncas', [NE, 128, 72])
    o_nsss = dout('nsss', [NE, 128, 2048]); o_nccs = dout('nccs', [max(NO, 1), 128, 64])

    def sb(name, shape, dt=F32):
        return es.enter_context(nc.sbuf_tensor('s_' + name, list(shape), dt))

    def ps(name, shape, dt=F32):
        return es.enter_context(nc.psum_tensor(name, list(shape), dt))

    xT = sb('xT', [128, KC, TW])
    hT = sb('hT', [128, KC, TW], BF16)
    hflat = hT[:].rearrange("p a b -> p (a b)")
    arena2 = sb('arena2', [128, 40 * TW], BF16)
    xbcs = arena2[:, 0:24 * TW].rearrange("p (a b) -> p a b", a=24)
    qT = arena2[:, 24 * TW:40 * TW].rearrange("p (a b) -> p a b", a=16)
    obuf = arena2[:, 0:32 * TW].bitcast(F32).rearrange("p (a b) -> p a b", a=16)
    mix = sb('mix', [128, 32, TW], BF16)
    sq = mix[:, 0:16, :]
    wbuf = [sb(f'wbuf{i}', [128, 4096], BF16) for i in range(NS)]
    C8 = TW + 8
    scr = sb('scr', [128, 5 * TW + 2 * C8 + 256 + TW // 2])
    tmpA = [scr[:, 0:TW], scr[:, TW:2 * TW]]; rstd = scr[:, 2 * TW:3 * TW]
    ident_f = sb('ident_f', [128, 128]); ident_b = sb('ident_b', [128, 128], BF16)
    ones_b = sb('ones_b', [128, 128], BF16); ones_f = sb('ones_f', [64, 128])
    triu_f = sb('triu_f', [64, 64]); mask_b = sb('mask_b', [64, 512], BF16)
    eps_c = sb('eps_c', [128, 2])
    cT = sb('cT', [128, KC * 2]); scT = sb('scT', [128, KC * 2]); scb = sb('scb', [128, KC * 2], BF16)
    bada = sb('bada', [128, DEPTH * 48]); npre = sb('npre', [128, DEPTH * 16]); npost = sb('npost', [128, DEPTH * 16])
    modt = sb('modt', [128, DEPTH * 96])
    modc = sb('modc', [128, DEPTH * 2 * 3 * 16])
    caw = sb('caw', [128, NE * 96]); cab = sb('cab', [128, NE * 24]); nssm = sb('nssm', [128, NE * 16])
    dtb = sb('dtb', [64, NE * 32]); alog = sb('alog', [64, NE * 32]); dsk = sb('dsk', [64, NE * 32]); snk = sb('snk', [64, NE * 32])
    aneg = sb('aneg', [64, NE * 32])
    Dm = [sb('Dm0', [64, 32 * 64], BF16)] * NE
    ccw = sb('ccw', [128, max(NO, 1) * 96])
    kP = [sb(f'kP{i}', [128, 4, 128 + TP], BF16) for i in range(NE)]
    kPs = [sb(f'kPs{i}', [128, 4, 128 + TP], BF16) for i in range(NE)]
    vP = [sb(f'vP{i}', [64, 2 + NCP, 512], BF16) for i in range(NE)]
    stT = [sb(f'stT{i}', [128, 2048]) for i in range(NE)]
    ccar = [sb(f'ccar{i}', [128, 24 * 3]) for i in range(NE)]
    vcar = [sb(f'vcar{i}', [128, 32 * 2]) for i in range(NO)]
    kS = sb('kS', [128, 4, 192], BF16); kSs = sb('kSs', [128, 4, 192], BF16)
    vS = sb('vS', [64, 3, 512], BF16)
    stS = sb('stS', [128, 2048]); scaS = sb('scaS', [128, 72]); sccS = sb('sccS', [128, 64])
    stbf = sb('stbf', [128, 2048], BF16)
    o_ = 3 * TW
    cst = [scr[:, o_:o_ + C8]] * 2
    cacc = [scr[:, o_ + C8:o_ + 2 * C8]] * 2
    o_ += 2 * C8
    stg = [tmpA[0], tmpA[1], scr[:, o_:o_ + TW], scr[:, o_ + TW:o_ + 2 * TW]]
    o_ += 2 * TW
    kvout = [scr[:, o_:o_ + 256]] * 2
    sqo = [scr[:, o_ + 256:o_ + 256 + TW // 2].bitcast(BF16)] * 2
    dtall = sb('dtall', [64, (NCP + 1) * 32]); spx = sb('spx', [64, (NCP + 1) * 32])
    sm = {n: sb('sm_' + n, [128, 32]) for n in ['dta', 'acum', 'ldt', 'amb', 'E2', 'wte', 'tw', 'Edec']}
    xtok = sb('xtok', [64, 2560], BF16); xw = sb('xw', [64, 2048], BF16)
    cbT = sb('cbT', [64, 256])
    Dgh = [sb('Dgh0', [64, 512], BF16)[:], hflat[0:64, 11 * TW:11 * TW + 512]]
    Dgl = [sb('Dgl0', [64, 512], BF16)[:], hflat[0:64, 11 * TW + 512:11 * TW + 1024]]
    DGK = [[('Dg', 0)], [('h', k_) for k_ in range(11, 15)]]
    ahl = sb('ahl', [64, 64], BF16)
    seg = [sb('seg0', [64, 512])[:], scr[0:64, 0:512]]
    mixT = [sb('mixT0', [64, 512], BF16)[:], scr[0:64, 1024:1280].bitcast(BF16)]
    t1 = [sb('t10', [64, 512])[:], scr[0:64, 512:1024]]
    ytok = [sb('ytok0', [64, 512], BF16)[:], scr[0:64, 1280:1536].bitcast(BF16)]
    gy = [sb('gy0', [128, 256])[:], scr[:, 1536:1792]]
    sqg = [sb('sqg0', [128, 256], BF16)[:], scr[:, 1792:1920].bitcast(BF16)]
    rsg = [sb('rsg0', [128, 64])[:], scr[:, 1920:1984]]
    Pn = [hflat[0:64, 0:768], hflat[0:64, 5 * TW:5 * TW + 768]]
    Pq = Pn
    PTq = [hflat[0:64, 768:1536], hflat[0:64, 5 * TW + 768:5 * TW + 1536]]
    asm = [{n: sb(f'asm{i}_' + n, [64, 4]) for n in ['mx', 'negm', 'rs', 'es', 'den', 'rinv']} for i in range(2)]
    NSB = 4
    psS = [ps(f'psS{i}', [128, 512]) for i in range(NSB)]
    psW = ps('psW', [128, 1024])
    psW2 = ps('psW2', [128, 1024])
    PW = [psW, psW2]; PWK = ['psW', 'psW2']
    ps_ctr = [0]

    def bank():
        i = ps_ctr[0] % NSB
        ps_ctr[0] += 1
        return psS[i], ('ps', i)

    def mm(out, lhsT, rhs, start, stop, reads, writes):
        T.add('pe', lambda e: e.matmul(out, lhsT, rhs, start=start, stop=stop), reads, writes)

    def tr(out, in_, ident, reads, writes):
        T.add('pe', lambda e: e.transpose(out, in_, ident), reads, writes)

    def act(out, in_, func, reads, writes, bias=None, scale=None, accum=None, eng='act'):
        kw = {}
        if bias is not None: kw['bias'] = bias
        if scale is not None: kw['scale'] = scale
        if accum is not None: kw['accum_out'] = accum
        T.add('act', lambda e: e.activation(out=out, in_=in_, func=func, **kw), reads, writes)

    def tt(eng, out, in0, in1, op, reads, writes):
        T.add(eng, lambda e: e.tensor_tensor(out, in0, in1, op), reads, writes)

    def tsc(eng, out, in0, s1, s2, op0, op1, reads, writes):
        if op1 is None:
            T.add(eng, lambda e: e.tensor_scalar(out, in0, s1, None, op0), reads, writes)
        else:
            T.add(eng, lambda e: e.tensor_scalar(out, in0, s1, s2, op0, op1), reads, writes)

    def stt(eng, out, in0, scalar, in1, op0, op1, reads, writes):
        T.add(eng, lambda e: e.scalar_tensor_tensor(out, in0, scalar, in1, op0, op1), reads, writes)

    def rsqrt(out, in_, scale, reads, wkey):
        act(out, in_, AF.Ln, reads, [wkey], bias=eps_c[0:out.shape[0], 0:1], scale=scale)
        act(out, out, AF.Exp, [wkey], [wkey], scale=-0.5)

    def cp(eng, out, in_, reads, writes):
        if eng == 'act':
            T.add('act', lambda e: e.copy(out, in_), reads, writes)
        else:
            T.add(eng, lambda e: e.tensor_copy(out, in_), reads, writes)

    def memset(eng, ap, val, writes):
        T.add(eng, lambda e: e.memset(ap, val), (), writes)

    def dma(eng, out, in_, reads, writes, chan):
        T.add(eng, lambda e: e.dma_start(out=out, in_=in_), reads, writes, chan=chan)

    def dbg(name, ap, reads, shape):
        if name not in dbg_names:
            return
        if name not in dbg_out:
            dbg_out[name] = dout('dbg_' + name, shape)
        dma('pool', dbg_out[name], ap, reads, [('dbgo', name)], 'dbg')

    def ckpt(name):
        if cfg.get('stop') == name:
            raise _Stop()

    def keys(name, rng):
        return [(name, i) for i in rng]

    XK = keys('x', range(KC)); HK = keys('h', range(KC))
    PKS = [keys('h', range(0, 5)), keys('h', range(5, 10))]
    STSK = [('stS', g_) for g_ in range(4)]; STBK = [('stbf', g_) for g_ in range(4)]
    SCRK = [('tmpA', 0), ('tmpA', 1), 'rstd', ('cst', 0), ('cacc', 0), ('stg', 2), ('stg', 3), ('kvout', 0), ('sqo', 0)]
    SETB = [(n_, 1) for n_ in ['seg', 'mixT', 't1', 'ytok', 'gy', 'sqg', 'rsg']]

    dma('sp', ident_f[:], d_ident, [], ['ident_f'], 'cst')
    dma('pool', ident_b[:], d_ident, [], ['ident_b'], 'cstb')
    dma('sp', triu_f[:], d_triu, [], ['triu_f'], 'cst')
    dma('pool', mask_b[:], d_mask, [], ['mask_b'], 'cstb')
    memset('dve', ones_b[:], 1.0, ['ones_b'])
    memset('dve', ones_f[:], 1.0, ['ones_f'])
    memset('dve', eps_c[:, 0:1], EPS, ['eps_c'])
    memset('dve', eps_c[:, 1:2], 1.0, ['eps_c'])
    for (t_, d_, k_) in [(cT, d_cT, 'cT'), (bada, d_bada, 'bada'), (npre, d_npre, 'npre'), (npost, d_npost, 'npost'),
                         (caw, d_caw, 'caw'), (cab, d_cab, 'cab'), (nssm, d_nssm, 'nssm'), (dtb, d_dtb, 'dtb'),
                         (alog, d_alog, 'alog'), (dsk, d_dsk, 'dsk'), (snk, d_snk, 'snk'), (ccw, d_ccw, 'ccw')]:
        dma('sp', t_[:], d_, [], [k_], 'cst')
    act(scT[:], cT[:], AF.Silu, ['cT'], ['scT'])
    act(aneg[:], alog[:], AF.Exp, ['alog'], ['aneg'])
    tsc('dve', aneg[:], aneg[:], -1.0, None, ALU.mult, None, ['aneg'], ['aneg'])
    cp('dve', scb[:], scT[:], ['scT'], ['scb'])
    nb_ctr = 0
    for l in range(DEPTH):
        pst, pk = psW[:, 512:1024], 'psW'
        for nb in range(24):
            s_ = nb_ctr % NS
            wt = wbuf[s_]; wk = ('w', s_)
            dma('pool', wt[:].rearrange("p (a b) -> p a b", a=KC), d_wada[l, :, :, nb * 256:(nb + 1) * 256], [], [wk], f'w{s_}')
            nb_ctr += 1
            pr, prk = bank()
            for kc in range(KC):
                mm(pr[0:2, 0:256], scb[:, kc * 2:kc * 2 + 2], wt[:, kc * 256:(kc + 1) * 256], kc == 0, kc == KC - 1, [wk, 'scb'], [prk])
            mr = tmpA[nb % 2]; mrk = ('tmpA', nb % 2)
            cp('dve', mr[0:2, 0:256], pr[0:2, 0:256], [prk], [mrk])
            for jj in range(2):
                j = nb * 2 + jj
                tr(pst[:, j * 2:j * 2 + 2], mr[0:2, jj * 128:(jj + 1) * 128], ident_f[0:2, 0:2], [mrk, 'ident_f'], [pk])
        tt('dve', modt[:, l * 96:(l + 1) * 96].rearrange("p (j w) -> p j w", w=2),
           pst[:, 0:96].rearrange("p (j w) -> p j w", w=2),
           bada[:, l * 48:(l + 1) * 48].unsqueeze(2).to_broadcast([128, 48, 2]), ALU.add,
           [pk, 'bada'], [('modt', l)])
        for w in range(2):
            base = ((l * 2 + w) * 3) * 16
            mv = modt[:, l * 96:(l + 1) * 96].rearrange("p (j w) -> p j w", w=2)
            stt('dve', modc[:, base:base + 16], mv[:, 16:32, w], 1.0, npre[:, l * 16:(l + 1) * 16], ALU.add, ALU.mult,
                [('modt', l), 'npre'], [('modc', l)])
            cp('dve', modc[:, base + 16:base + 32], mv[:, 0:16, w], [('modt', l)], [('modc', l)])
            tt('dve', modc[:, base + 32:base + 48], mv[:, 32:48, w], npost[:, l * 16:(l + 1) * 16], ALU.mult,
               [('modt', l), 'npost'], [('modc', l)])


    def mc_unused():
        pass

    def mc(l, w, kind, kc):
        o = ((l * 2 + w) * 3 + kind) * 16 + kc
        return modc[:, o:o + 1]

    wseq = []
    for ti in range(NT):
        for l in range(DEPTH):
            i = l // 2
            if l % 2 == 0:
                wseq += [d_wie[i][g] for g in range(len(EV_GROUPS))] + [d_woe[i][g] for g in range(16)]
            else:
                wseq += [d_wio[i][g] for g in range(64)] + [d_woo[i][g] for g in range(16)]
    wst = dict(issued=0, consumed=0)
    NG = len(wseq) // NT
    wsc = None
    if NT > 1:
        wsc = []
        for l in range(DEPTH):
            ng_l = (len(EV_GROUPS) + 16) if l % 2 == 0 else 80
            t_ = nc.dram_tensor(f'wscratch{l}', [ng_l, 128, 4096], BF16, kind="Internal").ap()
            wsc += [t_[g_] for g_ in range(ng_l)]
        assert len(wsc) == NG

    def wnext():
        while wst['issued'] < min(len(wseq), wst['consumed'] + NS):
            n = wst['issued']
            s = n % NS
            g_ = n % NG
            if n < NG:
                dma('pool', wbuf[s][:], wseq[n], [], [('w', s)], f'w{s}')
                if wsc is not None:
                    dma('sp', wsc[g_], wbuf[s][:], [('w', s)], [('wsc', g_)], f'wb{s}')
            else:
                dma('pool', wbuf[s][:], wsc[g_], [('wsc', g_)], [('w', s)], f'w{s}')
            wst['issued'] += 1
        s = wst['consumed'] % NS
        wst['consumed'] += 1
        return wbuf[s], ('w', s)

    def phase_A(l, Tt, segs):
        ckpt('A0')
        act(sq[:, :, 0:Tt], xT[:, :, 0:Tt], AF.Square, XK, keys('mix', range(16)))
        ckpt('A1')
        pst, pk = bank()
        for kc in range(KC):
            mm(pst[:, 0:Tt], ones_b[:], sq[:, kc, 0:Tt], kc == 0, kc == KC - 1, [('mix', kc), 'ones_b'], [pk])
        ckpt('A2')
        rsqrt(rstd[:, 0:Tt], pst[:, 0:Tt], 1.0 / D, [pk, 'eps_c'], 'rstd')
        ckpt('A3')
        for kc in range(KC):
            tb = tmpA[kc % 2]; tk = ('tmpA', kc % 2)
            tt('dve', tb[:, 0:Tt], xT[:, kc, 0:Tt], rstd[:, 0:Tt], ALU.mult, [('x', kc), 'rstd'], [tk])
            for (c0, c1, w) in segs:
                act(hT[:, kc, c0:c1], tb[:, c0:c1], AF.Identity, [tk, ('modc', l)], [('h', kc)],
                    bias=mc(l, w, 1, kc), scale=mc(l, w, 0, kc))

    def proj_fm(wt, wk, col0, ncols_k, Tt, rhs_fn, rkeys, nk):
        pst, pk = bank()
        for kc in range(nk):
            mm(pst[:, 0:Tt], wt[:, kc * ncols_k + col0: kc * ncols_k + col0 + 128], rhs_fn(kc), kc == 0, kc == nk - 1,
               [wk] + rkeys(kc), [pk])
        return pst, pk

    def phase_D(l, Tt, segs, last_tile, ti):
        ssb, ssk = psW[:, 512:1024], 'psW'
        for j in range(KC):
            wt, wk = wnext()
            pst, pk = proj_fm(wt, wk, 0, 128, Tt, lambda kc: mix[:, kc, 0:Tt], lambda kc: [('mix', kc)], 32)
            wr = [('o', j)]
            if j == 0:
                wr = wr + HK + keys('xb', range(24)) + keys('q', range(16))
            cp('act', obuf[:, j, 0:Tt], pst[:, 0:Tt], [pk], wr)
            sb_ = sqo[j % 2]; sk = ('sqo', 0)
            act(sb_[:, 0:Tt], pst[:, 0:Tt], AF.Square, [pk], [sk])
            mm(ssb[:, 0:Tt], ones_b[:], sb_[:, 0:Tt], j == 0, j == KC - 1, [sk, 'ones_b'], [ssk])
        rsqrt(rstd[:, 0:Tt], ssb[:, 0:Tt], 1.0 / D, [ssk, 'eps_c'], 'rstd')
        for j in range(KC):
            tb = tmpA[j % 2]; tk = ('tmpA', j % 2)
            tt('dve' if j % 2 == 0 else 'pool', tb[:, 0:Tt], obuf[:, j, 0:Tt], rstd[:, 0:Tt], ALU.mult, [('o', j), 'rstd'], [tk])
            for (c0, c1, w) in segs:
                stt('dve', xT[:, j, c0:c1], tb[:, c0:c1], mc(l, w, 2, j), xT[:, j, c0:c1], ALU.mult, ALU.add,
                    [tk, ('modc', l), ('x', j)], [('x', j)])

    def even_layer(l, ti, Tt, segs, chunks):
        i = l // 2
        last = (ti == NT - 1)
        has_s = (ti == 0)
        L = Tt + (3 if has_s else 0)
        if ti == 0:
            memset('dve', stT[i][:], 0.0, [('stT', i, g_) for g_ in range(4)])
            memset('dve', ccar[i][:], 0.0, [('ccar', i)])
            dma('sp', scaS[:], d_sca[i], [], ['scaS'], 'sin')
            dma('sp', stS[:], d_sst[i], [], STSK, 'sin')
            dma('pool', kS[:, :, 0:128], d_ckT[i], [], ['kS'], 'sinb')
            dma('pool', kSs[0:64, :, 0:128], d_ckT[i, 64:128], [], ['kSs'], 'sinb')
            dma('pool', kSs[64:128, :, 0:128], d_ckT[i, 0:64], [], ['kSs'], 'sinb')
            dma('pool', vS[:, 0:2, :], d_cv[i].rearrange("b s c -> s b c"), [], ['vS'], 'sinb')
            dma('sp', o_nks[i, :, :, 0:64], d_ckT[i, :, :, 64:128], [], [('onks', i)], 'outs')
            dma('sp', o_nvs[i, 0:64, :], d_cv[i, 1], [], [('onvs', i)], 'outs')
        ckpt(f'pre{l}_{ti}')
        tt('dve', Dm[i][:].rearrange("p (h c) -> p h c", h=32),
           dsk[:, i * 32:(i + 1) * 32].unsqueeze(2).to_broadcast([64, 32, 64]),
           ident_f[0:64, 0:64].unsqueeze(1).to_broadcast([64, 32, 64]), ALU.mult,
           ['dsk', 'ident_f'], [('Dm', 0)])
        phase_A(l, Tt, segs)
        ckpt(f'A{l}_{ti}')
        dbg(f'h{l}_{ti}', hT[:, :, 0:Tt], HK, [128, KC, Tt])
        def do_group(typ, gi):
            ckpt(f'G{typ}{gi}')
            wt, wk = wnext()
            if typ == 'dt':
                dtps, dtk = bank()
                for ci, c in enumerate(chunks):
                    for kc in range(KC):
                        mm(dtps[0:64, ci * 32:(ci + 1) * 32], hT[:, kc, c['col']:c['col'] + 64], wt[:, kc * 256:kc * 256 + 32],
                           kc == 0, kc == KC - 1, [wk, ('h', kc)], [dtk])
                nch = len(chunks)
                n32 = nch * 32
                xa = spx[:, 0:n32]
                dv = dtall[:, 0:n32]
                tt('dve', dv.rearrange("p (c h) -> p c h", h=32), dtps[0:64, 0:n32].rearrange("p (c h) -> p c h", h=32),
                   dtb[:, i * 32:(i + 1) * 32].unsqueeze(1).to_broadcast([64, nch, 32]), ALU.add, [dtk, 'dtb'], ['dtall'])
                act(xa, dv, AF.Abs, ['dtall'], ['spx'])
                act(xa, xa, AF.Exp, ['spx'], ['spx'], scale=-1.0)
                act(xa, xa, AF.Ln, ['spx', 'eps_c'], ['spx'], bias=eps_c[0:64, 1:2])
                stt('dve', dv, dv, 0.0, xa, ALU.max, ALU.add, ['dtall', 'spx'], ['dtall'])
            elif typ == 'xbc':
                for jj in range(2):
                    f = gi * 2 + jj
                    pst, pk = proj_fm(wt, wk, jj * 128, 256, Tt, lambda kc: hT[:, kc, 0:Tt], lambda kc: [('h', kc)], KC)
                    cs = cst[f % 2]; ck = ('cst', 0); ca = cacc[f % 2]; cak = ('cacc', 0)
                    cp('act', cs[:, 3:3 + TP], pst[:, 0:TP], [pk], [ck])
                    cp('dve', cs[:, 0:3], ccar[i][:, f * 3:f * 3 + 3], [('ccar', i)], [ck])
                    if has_s:
                        cp('act', cs[:, 6 + TP:6 + TP + CH], pst[:, TP:TP + CH], [pk], [ck])
                        cp('dve', cs[:, 3 + TP:6 + TP], scaS[:, f * 3:f * 3 + 3], ['scaS'], [ck])
                        cp('dve', scaS[:, f * 3:f * 3 + 3], cs[:, 3 + TP + CH:6 + TP + CH], [ck], ['scaS'])
                    cp('dve', ccar[i][:, f * 3:f * 3 + 3], cs[:, TP:TP + 3], [ck], [('ccar', i)])
                    wv = caw[:, (i * 24 + f) * 4:(i * 24 + f) * 4 + 4]
                    tsc('dve', ca[:, 0:L], cs[:, 3:3 + L], wv[:, 3:4], cab[:, i * 24 + f:i * 24 + f + 1], ALU.mult, ALU.add,
                        [ck, 'caw', 'cab'], [cak])
                    for tap in range(3):
                        stt('dve', ca[:, 0:L], cs[:, tap:tap + L], wv[:, tap:tap + 1], ca[:, 0:L], ALU.mult, ALU.add,
                            [ck, cak, 'caw'], [cak])
                    act(xbcs[:, f, 0:TP], ca[:, 0:TP], AF.Silu, [cak], [('xb', f)] + (keys('o', range(KC)) if f == 0 else []))
                    if has_s:
                        act(xbcs[:, f, TP:TP + CH], ca[:, TP + 3:TP + 3 + CH], AF.Silu, [cak], [('xb', f)])
            elif typ in ('z', 'g'):
                for jj in range(2):
                    f = gi * 2 + jj
                    pst, pk = proj_fm(wt, wk, jj * 128, 256, Tt, lambda kc: hT[:, kc, 0:Tt], lambda kc: [('h', kc)], KC)
                    mf = f if typ == 'z' else 16 + f
                    act(mix[:, mf, 0:Tt], pst[:, 0:Tt], AF.Silu, [pk], [('mix', mf)])
            elif typ == 'q':
                for jj in range(2):
                    f = gi * 2 + jj
                    pst, pk = proj_fm(wt, wk, jj * 128, 256, Tt, lambda kc: hT[:, kc, 0:Tt], lambda kc: [('h', kc)], KC)
                    tsc('dve', qT[:, f, 0:Tt], pst[:, 0:Tt], 0.125, None, ALU.mult, None, [pk], [('q', f)] + (keys('o', range(KC)) if f == 0 else []))
            elif typ == 'k':
                for jj in range(2):
                    f = gi * 2 + jj
                    pst, pk = proj_fm(wt, wk, jj * 128, 256, Tt, lambda kc: hT[:, kc, 0:Tt], lambda kc: [('h', kc)], KC)
                    cp('dve', kP[i][:, f, 128:128 + TP], pst[:, 0:TP], [pk], [('kP', i)])
                    ckpt(f'K1_{f}')
                    if has_s:
                        cp('dve', kS[:, f, 128:192], pst[:, TP:TP + CH], [pk], ['kS'])
                        ckpt(f'K2_{f}')
                        ko = kvout[f % 2]; kk = ('kvout', 0)
                        cp('dve', ko[:, 0:CH], pst[:, TP:TP + CH], [pk], [kk])
                        ckpt(f'K3_{f}')
                        dma('sp', o_nks[i, :, f, 64:128], ko[:, 0:CH], [kk], [('onks', i)], 'outs')
                        ckpt(f'K4_{f}')
                    if last:
                        ko = kvout[f % 2]; kk = ('kvout', 0)
                        cp('act', ko[:, 0:128], pst[:, TP - 128:TP], [pk], [kk])
                        dma('sp', o_nkp[i, :, f, :], ko[:, 0:128], [kk], [('onkp', i)], 'outs')
            elif typ == 'ks':
                for jj in range(2):
                    f = gi * 2 + jj
                    pst, pk = proj_fm(wt, wk, jj * 128, 256, Tt, lambda kc: hT[:, kc, 0:Tt], lambda kc: [('h', kc)], KC)
                    cp('dve', kPs[i][:, f, 128:128 + TP], pst[:, 0:TP], [pk], [('kPs', i)])
                    if has_s:
                        cp('dve', kSs[:, f, 128:192], pst[:, TP:TP + CH], [pk], ['kSs'])
            elif typ == 'v':
                for ci, c in enumerate(chunks):
                    pst, pk = bank()
                    for kc in range(KC):
                        mm(pst[0:64, 0:256], hT[:, kc, c['col']:c['col'] + 64], wt[:, kc * 256:(kc + 1) * 256],
                           kc == 0, kc == KC - 1, [wk, ('h', kc)], [pk])
                    if c['seg'] == 'p':
                        cp('dve', vP[i][:, 2 + c['cl'], gi * 256:(gi + 1) * 256], pst[0:64, 0:256], [pk], [('vP', i)])
                        if last and c['cl'] >= NCP - 2:
                            blk = c['cl'] - (NCP - 2)
                            ko = kvout[ci % 2]; kk = ('kvout', 0)
                            cp('act', ko[0:64, 0:256], pst[0:64, 0:256], [pk], [kk])
                            dma('sp', o_nvp[i, blk * 64:(blk + 1) * 64, gi * 256:(gi + 1) * 256], ko[0:64, 0:256], [kk], [('onvp', i)], 'outs')
                    else:
                        cp('dve', vS[:, 2, gi * 256:(gi + 1) * 256], pst[0:64, 0:256], [pk], ['vS'])
                        ko = kvout[ci % 2]; kk = ('kvout', 0)
                        cp('act', ko[0:64, 0:256], pst[0:64, 0:256], [pk], [kk])
                        dma('sp', o_nvs[i, 64:128, gi * 256:(gi + 1) * 256], ko[0:64, 0:256], [kk], [('onvs', i)], 'outs')

        NG1 = 1 + 12 + 8
        for (typ, gi) in EV_GROUPS[:NG1]:
            do_group(typ, gi)

        def projB2():
            for (typ, gi) in EV_GROUPS[NG1:]:
                do_group(typ, gi)
                yield
        if last:
            dma('sp', o_ncap[i], ccar[i][:], [('ccar', i)], [('oncap', i)], 'outs')
        if has_s:
            dma('sp', o_ncas[i], scaS[:], ['scaS'], [('oncas', i)], 'outs')
        dbg(f'xbcs{l}_{ti}', xbcs[:, :, 0:Tt], keys('xb', range(24)), [128, 24, Tt])
        dbg(f'dt{l}_{ti}', dtall[:, 0:len(chunks) * 32], ['dtall'], [64, len(chunks) * 32])
        dbg(f'q{l}_{ti}', qT[:, :, 0:Tt], keys('q', range(16)), [128, 16, Tt])
        ckpt(f'B{l}_{ti}')
        cp('act', stbf[:], stT[i][:], [('stT', i, g_) for g_ in range(4)], STBK)
        def ssd_ctx(ci, c):
            isS = c['seg'] == 's'
            d = dict(col=c['col'], isS=isS)
            d['st_f'] = stS if isS else stT[i]
            d['st_k'] = (lambda g: ('stS', g)) if isS else (lambda g: ('stT', i, g))
            d['dtc'] = dtall[:, ci * 32:(ci + 1) * 32]
            for n in ['dta', 'acum', 'ldt', 'amb', 'E2', 'wte', 'tw']:
                d[n] = sm[n][0:64, :]
            d['Edec'] = sm['Edec']
            return d

        def ssd_pre(ci, c):
            X = ssd_ctx(ci, c)
            dtc, dta, acum, ldt, amb, E2, wte, tw, Edec = (X[k_] for k_ in ['dtc', 'dta', 'acum', 'ldt', 'amb', 'E2', 'wte', 'tw', 'Edec'])
            col = c['col']
            isS = c['seg'] == 's'
            if isS:
                cp('act', stbf[:], stS[:], STSK, STBK)
            tt('dve', dta, dtc, aneg[:, i * 32:(i + 1) * 32], ALU.mult, ['dtall', 'aneg'], ['dta'])
            pss, psk = bank()
            mm(pss[0:64, 0:32], triu_f[:], dta, True, True, ['triu_f', 'dta'], [psk])
            mm(pss[:, 32:64], ones_f[:], dta, True, True, ['ones_f', 'dta'], [psk])
            cp('act', acum, pss[0:64, 0:32], [psk], ['acum'])
            cp('dve', ahl[:, 0:32], acum, ['acum'], ['ahl'])
            tt('dve', ldt, acum, ahl[:, 0:32], ALU.subtract, ['acum', 'ahl'], ['ldt'])
            cp('dve', ahl[:, 32:64], ldt, ['ldt'], ['ahl'])
            act(ldt, dtc, AF.Ln, ['dtall'], ['ldt'])
            tt('dve', amb, acum, ldt, ALU.subtract, ['acum', 'ldt'], ['amb'])
            act(E2, acum, AF.Exp, ['acum'], ['E2'])
            tt('dve', tw, pss[0:64, 32:64], acum, ALU.subtract, [psk, 'acum'], ['tw'])
            act(tw, tw, AF.Exp, ['tw'], ['tw'])
            tt('dve', wte, tw, dtc, ALU.mult, ['tw', 'dtall'], ['wte'])
            act(Edec[:], pss[:, 32:64], AF.Exp, [psk], ['Edec'])
            yield
            for b0 in range(0, 20, 8):
                nb_ = min(8, 20 - b0)
                ptb, ptk = bank()
                pv = ptb[:].bitcast(BF16)
                for f in range(b0, b0 + nb_):
                    tr(pv[0:64, (f - b0) * 128:(f - b0 + 1) * 128], xbcs[:, f, col:col + 64], ident_b[:], [('xb', f), 'ident_b'], [ptk])
                cp('act' if b0 == 8 else 'dve', xtok[:, b0 * 128:(b0 + nb_) * 128], pv[0:64, 0:nb_ * 128], [ptk], ['xtok'])
            tt('pool', xw[:].rearrange("p (h c) -> p h c", h=32), xtok[:, 0:2048].rearrange("p (h c) -> p h c", h=32),
               wte.unsqueeze(2).to_broadcast([64, 32, 64]), ALU.mult, ['xtok', 'wte'], ['xw'])
            yield
            pcb, pcbk = bank()
            for g in range(4):
                mm(pcb[0:64, g * 64:(g + 1) * 64], xbcs[:, 16 + g, col:col + 64], xbcs[:, 20 + g, col:col + 64], True, True,
                   [('xb', 16 + g), ('xb', 20 + g)], [pcbk])
            cp('act', cbT[:], pcb[0:64, 0:256], [pcbk], ['cbT'])
            yield

        def ssd_grp(ci, c, groups, b2):
            X = ssd_ctx(ci, c)
            col = X['col']; st_f = X['st_f']; st_kf = X['st_k']
            acum, amb, E2, Edec = X['acum'], X['amb'], X['E2'], X['Edec']
            st_b = stbf
            for g in groups:
                hs = slice(g * 8, (g + 1) * 8)
                for (dgt, off) in ((Dgh[b2], 0), (Dgl[b2], 32)):
                    tt('pool', dgt.rearrange("p (r c) -> p r c", r=8), ahl[:, off + g * 8:off + (g + 1) * 8].unsqueeze(2).to_broadcast([64, 8, 64]),
                       ident_b[0:64, 0:64].unsqueeze(1).to_broadcast([64, 8, 64]), ALU.mult, ['ahl', 'ident_b'], DGK[b2])
                pA, pAk = bank()
                mm(pA[0:64, :], ones_b[0:64, 0:64], Dgh[b2], True, False, ['ones_b'] + DGK[b2], [pAk])
                mm(pA[0:64, :], ones_b[0:64, 0:64], Dgl[b2], False, False, ['ones_b'] + DGK[b2], [pAk])
                mm(pA[0:64, :], ident_b[0:64, 0:64], mask_b[:], False, True, ['ident_b', 'mask_b'], [pAk])
                tt('dve', seg[b2][:].rearrange("p (r c) -> p r c", r=8), pA[0:64, :].rearrange("p (r c) -> p r c", r=8),
                   amb[:, hs].unsqueeze(2).to_broadcast([64, 8, 64]), ALU.subtract, [pAk, 'amb'], [('seg', b2)])
                act(seg[b2][:], seg[b2][:], AF.Exp, [('seg', b2)], [('seg', b2)])
                tt('dve', mixT[b2][:].rearrange("p (r c) -> p r c", r=8), seg[b2][:].rearrange("p (r c) -> p r c", r=8),
                   cbT[:, g * 64:(g + 1) * 64].unsqueeze(1).to_broadcast([64, 8, 64]), ALU.mult, [('seg', b2), 'cbT'], [('mixT', b2)])
                yield
                py, pyk = bank()
                for r in range(8):
                    h = g * 8 + r
                    mm(py[0:64, r * 64:(r + 1) * 64], mixT[b2][:, r * 64:(r + 1) * 64], xtok[:, h * 64:(h + 1) * 64], True, False,
                       [('mixT', b2), 'xtok'], [pyk])
                    mm(py[0:64, r * 64:(r + 1) * 64], Dm[i][:, h * 64:(h + 1) * 64], xtok[:, h * 64:(h + 1) * 64], False, True,
                       [('Dm', 0), 'xtok'], [pyk])
                po, pok = bank()
                mm(po[0:64, :], xbcs[:, 20 + g, col:col + 64], st_b[:, g * 512:(g + 1) * 512], True, True, [('xb', 20 + g), ('stbf', g)], [pok])
                tt('dve', t1[b2][:].rearrange("p (r c) -> p r c", r=8), po[0:64, :].rearrange("p (r c) -> p r c", r=8),
                   E2[:, hs].unsqueeze(2).to_broadcast([64, 8, 64]), ALU.mult, [pok, 'E2'], [('t1', b2)])
                tt('dve', ytok[b2][:], t1[b2][:], py[0:64, :], ALU.add, [('t1', b2), pyk], [('ytok', b2)])
                yield
                pyt, pytk = bank()
                pyv = pyt[:].bitcast(BF16)
                for fc in range(4):
                    tr(pyv[:, fc * 64:(fc + 1) * 64], ytok[b2][:, fc * 128:(fc + 1) * 128], ident_b[0:64, 0:64], [('ytok', b2), 'ident_b'], [pytk])
                mz = mix[:, g * 4:(g + 1) * 4, col:col + 64]
                mzk = keys('mix', range(g * 4, g * 4 + 4))
                gyv = gy[b2][:].rearrange("p (a c) -> p a c", a=4)
                tt('dve', gyv, pyv[:, 0:256].rearrange("p (a c) -> p a c", a=4), mz, ALU.mult, [pytk] + mzk, [('gy', b2)])
                act(sqg[b2][:], gy[b2][:], AF.Square, [('gy', b2)], [('sqg', b2)])
                pss2, pss2k = bank()
                for fc in range(4):
                    mm(pss2[:, 0:64], ones_b[:], sqg[b2][:, fc * 64:(fc + 1) * 64], fc == 0, fc == 3, [('sqg', b2), 'ones_b'], [pss2k])
                rsqrt(rsg[b2][:], pss2[:, 0:64], 1.0 / 512, [pss2k, 'eps_c'], ('rsg', b2))
                tt('pool', gyv, gyv, rsg[b2][:].unsqueeze(1).to_broadcast([128, 4, 64]), ALU.mult, [('gy', b2), ('rsg', b2)], [('gy', b2)])
                tt('pool', mz, gyv, nssm[:, i * 16 + g * 4:i * 16 + g * 4 + 4].unsqueeze(2).to_broadcast([128, 4, 64]), ALU.mult,
                   [('gy', b2), 'nssm'], mzk)
                yield
                pst_, pstk = bank()
                mm(pst_[:, :], xtok[:, 2048 + g * 128:2048 + (g + 1) * 128], xw[:, g * 512:(g + 1) * 512], True, True, ['xtok', 'xw'], [pstk])
                sv = st_f[:, g * 512:(g + 1) * 512]
                tt('pool', sv.rearrange("p (r c) -> p r c", r=8), sv.rearrange("p (r c) -> p r c", r=8),
                   Edec[:, hs].unsqueeze(2).to_broadcast([128, 8, 64]), ALU.mult, [st_kf(g), 'Edec', ('stbf', g)], [st_kf(g)])
                tt('dve', sv, sv, pst_[:, :], ALU.add, [st_kf(g), pstk], [st_kf(g)])
                cp('act', st_b[:, g * 512:(g + 1) * 512], sv, [st_kf(g)], [('stbf', g)])
                yield
            yield

        def attn_gen(ci, c, quads, sidx):
            col = c['col']
            isS = c['seg'] == 's'
            if isS:
                KB, KBs, kkey, kskey = kS, kSs, 'kS', 'kSs'
                kc0 = 0; nblk = 3
                vblk = [(vS, j, 'vS') for j in range(3)]
            else:
                KB, KBs, kkey, kskey = kP[i], kPs[i], ('kP', i), ('kPs', i)
                nblk = min(3, c['gidx'] + 1)
                kc0 = 64 * c['cl'] + 64 * (3 - nblk)
                vblk = [(vP[i], c['cl'] + (3 - nblk) + j, ('vP', i)) for j in range(nblk)]
            nv = nblk * 64
            psWt = PW[sidx]; pwk = PWK[sidx]
            PQK = PKS[sidx]; PXK = PKS[sidx]
            for qd in quads:
                b2 = sidx
                kv = qd
                fk = kv // 2; khalf = kv % 2
                scv = psWt[0:64, :].rearrange("p (h c) -> p h c", h=4)
                for hh in range(4):
                    half = hh // 2; fq = qd * 2 + hh % 2
                    ksrc, ksk = (KB, kkey) if khalf == half else (KBs, kskey)
                    mm(psWt[0:64, hh * 256:hh * 256 + nv], qT[half * 64:(half + 1) * 64, fq, col:col + 64],
                       ksrc[half * 64:(half + 1) * 64, fk, kc0:kc0 + nv], True, True, [('q', fq), ksk], [pwk])
                ckpt(f'AT1_{qd}')
                A = asm[b2]
                ak = lambda n: ('asm', b2, n)
                T.add('dve', lambda e, o=A['mx'][:], i_=scv[:, :, 0:nv]: e.reduce_max(o, i_, AX.X), [pwk], [ak('mx')])
                snq = snk[:, i * 32 + qd * 4:i * 32 + qd * 4 + 4].rearrange("p (f h) -> p h f", h=2)
                hv = lambda t_: t_[:].rearrange("p (h f) -> p h f", f=2)
                tt('dve', hv(A['mx']), hv(A['mx']), snq, ALU.max, [ak('mx'), 'snk'], [ak('mx')])
                tsc('dve', A['negm'][:], A['mx'][:], -1.0, None, ALU.mult, None, [ak('mx')], [ak('negm')])
                Pv = Pq[b2][:].rearrange("p (h c) -> p h c", h=4)
                memset('dve', A['rs'][:], 0.0, [ak('rs')])
                for hh in range(4):
                    act(Pv[:, hh, 0:nv], scv[:, hh, 0:nv], AF.Exp, [pwk, ak('negm')], [*PQK, ak('rs')],
                        bias=A['negm'][:, hh:hh + 1], accum=A['rs'][:, hh:hh + 1])
                tt('dve', hv(A['es']), snq, hv(A['negm']), ALU.add, ['snk', ak('negm')], [ak('es')])
                act(A['es'][:], A['es'][:], AF.Exp, [ak('es')], [ak('es')])
                tt('dve', A['den'][:], A['rs'][:], A['es'][:], ALU.add, [ak('rs'), ak('es')], [ak('den')])
                T.add('dve', lambda e, o=A['rinv'][:], i_=A['den'][:]: e.reciprocal(o, i_), [ak('den')], [ak('rinv')])
                Pnv = Pn[b2][:].rearrange("p (h c) -> p h c", h=4)
                tt('dve', Pnv[:, :, 0:nv], Pv[:, :, 0:nv], A['rinv'][:].unsqueeze(2).to_broadcast([64, 4, nv]), ALU.mult,
                   [*PQK, ak('rinv')], [*PXK])
                ckpt(f'AT2_{qd}')
                yield
                ppt, pptk = bank()
                ppv = ppt[:].bitcast(BF16)
                for hh in range(4):
                    for j in range(nblk):
                        tr(ppv[0:64, (hh * 3 + j) * 64:(hh * 3 + j + 1) * 64], Pnv[:, hh, j * 64:(j + 1) * 64], ident_b[0:64, 0:64],
                           [*PXK, 'ident_b'], [pptk])
                cp('act', PTq[b2][:, 0:768].rearrange("p (h c) -> p h c", h=4)[:, :, 0:nv], ppv[0:64, 0:768].rearrange("p (h c) -> p h c", h=4)[:, :, 0:nv], [pptk], [*PXK])
                ckpt(f'AT3_{qd}')
                pov, povk = bank()
                for hh in range(4):
                    fql = hh % 2; half = hh // 2
                    for j in range(nblk):
                        vt, vb, vk = vblk[j]
                        mm(pov[half * 64:(half + 1) * 64, fql * 64:(fql + 1) * 64], vt[:, vb, kv * 64:(kv + 1) * 64],
                           PTq[b2][:, (hh * 3 + j) * 64:(hh * 3 + j + 1) * 64], j == 0, j == nblk - 1, [vk, *PXK], [povk])
                ckpt(f'AT4_{qd}')
                mg = mix[:, 16 + qd * 2:16 + qd * 2 + 2, col:col + 64]
                mgk = keys('mix', range(16 + qd * 2, 16 + qd * 2 + 2))
                tt('dve', mg, pov[:, 0:128].rearrange("p (a c) -> p a c", a=2), mg, ALU.mult, [povk] + mgk, mgk)
                yield
            yield

        pb2 = projB2()
        active = [pb2]
        prev_attn = []

        def drain(required):
            req = list(required)
            for g_ in req:
                if g_ not in active:
                    active.append(g_)
            while any(g_ in active for g_ in req):
                for g_ in list(active):
                    try:
                        next(g_)
                    except StopIteration:
                        active.remove(g_)

        memset('dve', modt[:, 0:1], 0.0, [('modt', 0)] + SCRK + SETB)
        for ci, c in enumerate(chunks):
            drain([ssd_pre(ci, c)])
            if ci == 0:
                drain([ssd_grp(ci, c, [0, 1, 2, 3], 0)])
            else:
                drain([ssd_grp(ci, c, [0, 1], 0), ssd_grp(ci, c, [2, 3], 1)])
            ckpt(f'S{l}_{ti}_{ci}')
            if ci == 0:
                drain([pb2])
            drain([g_ for g_ in prev_attn if g_ in active])
            prev_attn[:] = [attn_gen(ci, c, range(0, 4), 0), attn_gen(ci, c, range(4, 8), 1)]
            active.extend(prev_attn)
        drain(list(active))
        memset('dve', modt[:, 0:1], 0.0, [('modt', 0)] + SCRK + SETB)
        dbg(f'mix{l}_{ti}', mix[:, :, 0:Tt], keys('mix', range(32)), [128, 32, Tt])
        ckpt(f'C{l}_{ti}')
        if has_s:
            dma('sp', o_nsss[i], stS[:], STSK, [('onsss', i)], 'outs')
        if last:
            dma('sp', o_nssp[i], stT[i][:], [('stT', i, g_) for g_ in range(4)], [('onssp', i)], 'outs')
        else:
            cp('pool', kP[i][:, :, 0:128], kP[i][:, :, TP:TP + 128], [('kP', i)], [('kP', i)])
            cp('pool', kPs[i][:, :, 0:128], kPs[i][:, :, TP:TP + 128], [('kPs', i)], [('kPs', i)])
            cp('pool', vP[i][:, 0:2, :], vP[i][:, NCP:NCP + 2, :], [('vP', i)], [('vP', i)])
        phase_D(l, Tt, segs, last, ti)

    def odd_layer(l, ti, Tt, segs, chunks):
        i = l // 2
        last = (ti == NT - 1)
        has_s = (ti == 0)
        L = Tt + (2 if has_s else 0)
        if ti == 0:
            memset('dve', vcar[i][:], 0.0, [('vcar', i)])
            dma('sp', sccS[:], d_scc[i], [], ['sccS'], 'sin')
        phase_A(l, Tt, segs)
        for j in range(32):
            wt, wk = wnext()
            pu, puk = proj_fm(wt, wk, 0, 256, Tt, lambda kc: hT[:, kc, 0:Tt], lambda kc: [('h', kc)], KC)
            pgc, pgck = proj_fm(wt, wk, 128, 256, Tt, lambda kc: hT[:, kc, 0:Tt], lambda kc: [('h', kc)], KC)
            wt2, wk2 = wnext()
            pgb, pgbk = proj_fm(wt2, wk2, 0, 256, Tt, lambda kc: hT[:, kc, 0:Tt], lambda kc: [('h', kc)], KC)
            pg, pgk = proj_fm(wt2, wk2, 128, 256, Tt, lambda kc: hT[:, kc, 0:Tt], lambda kc: [('h', kc)], KC)
            su = stg[(j % 2) * 2]; suk = ('tmpA', 0) if j % 2 == 0 else ('stg', 2)
            sg_ = stg[(j % 2) * 2 + 1]; sgk = ('tmpA', 1) if j % 2 == 0 else ('stg', 3)
            cs = cst[j % 2]; ck = ('cst', 0); ca = cacc[j % 2]; cak = ('cacc', 0)
            cp('act', su[:, 0:Tt], pu[:, 0:Tt], [puk], [suk])
            tt('dve', cs[:, 2:2 + TP], pgc[:, 0:TP], su[:, 0:TP], ALU.mult, [pgck, suk], [ck])
            cp('dve', cs[:, 0:2], vcar[i][:, j * 2:j * 2 + 2], [('vcar', i)], [ck])
            if has_s:
                tt('dve', cs[:, 4 + TP:4 + TP + CH], pgc[:, TP:TP + CH], su[:, TP:TP + CH], ALU.mult, [pgck, suk], [ck])
                cp('dve', cs[:, 2 + TP:4 + TP], sccS[:, j * 2:j * 2 + 2], ['sccS'], [ck])
                cp('dve', sccS[:, j * 2:j * 2 + 2], cs[:, 2 + TP + CH:4 + TP + CH], [ck], ['sccS'])
            cp('dve', vcar[i][:, j * 2:j * 2 + 2], cs[:, TP:TP + 2], [ck], [('vcar', i)])
            wv = ccw[:, (i * 32 + j) * 3:(i * 32 + j) * 3 + 3]
            tsc('dve', ca[:, 0:L], cs[:, 0:L], wv[:, 0:1], None, ALU.mult, None, [ck, 'ccw'], [cak])
            for tap in (1, 2):
                stt('dve', ca[:, 0:L], cs[:, tap:tap + L], wv[:, tap:tap + 1], ca[:, 0:L], ALU.mult, ALU.add, [ck, cak, 'ccw'], [cak])
            act(sg_[:, 0:Tt], pg[:, 0:Tt], AF.Silu, [pgk], [sgk])
            tt('dve', ca[:, 0:TP], ca[:, 0:TP], pgb[:, 0:TP], ALU.mult, [cak, pgbk], [cak])
            tt('pool', mix[:, j, 0:TP], ca[:, 0:TP], sg_[:, 0:TP], ALU.mult, [cak, sgk], [('mix', j)])
            if has_s:
                tt('dve', ca[:, TP + 2:TP + 2 + CH], ca[:, TP + 2:TP + 2 + CH], pgb[:, TP:TP + CH], ALU.mult, [cak, pgbk], [cak])
                tt('pool', mix[:, j, TP:TP + CH], ca[:, TP + 2:TP + 2 + CH], sg_[:, TP:TP + CH], ALU.mult, [cak, sgk], [('mix', j)])
        if last:
            dma('sp', o_nccp[i], vcar[i][:], [('vcar', i)], [('onccp', i)], 'outs')
        if has_s:
            dma('sp', o_nccs[i], sccS[:], ['sccS'], [('onccs', i)], 'outs')
        dbg(f'mix{l}_{ti}', mix[:, :, 0:Tt], keys('mix', range(32)), [128, 32, Tt])
        phase_D(l, Tt, segs, last, ti)

    def main_schedule():
        for ti in range(NT):
            has_s = (ti == 0)
            Tt = TP + (CH if has_s else 0)
            segs = [(0, TP, 0)] + ([(TP, TP + CH, 1)] if has_s else [])
            chunks = [dict(col=cl * CH, seg='p', cl=cl, gidx=ti * NCP + cl) for cl in range(NCP)]
            if has_s:
                chunks.append(dict(col=TP, seg='s', cl=0, gidx=0))
            dma('sp', xT[:, :, 0:TP], d_xp[:, :, ti * TP:(ti + 1) * TP], [], XK, 'xin')
            if has_s:
                dma('sp', xT[:, :, TP:TP + CH], d_xs, [], XK, 'xin')
            for l in range(DEPTH):
                if l % 2 == 0:
                    even_layer(l, ti, Tt, segs, chunks)
                else:
                    odd_layer(l, ti, Tt, segs, chunks)
                dbg(f'x{l}_{ti}', xT[:, :, 0:Tt], XK, [128, KC, Tt])
            dma('sp', o_yp[:, :, ti * TP:(ti + 1) * TP], xT[:, :, 0:TP], XK, [('oyp', ti)], 'xout')
            if has_s:
                dma('sp', o_ys, xT[:, :, TP:TP + CH], XK, ['oys'], 'xout')


    try:
        ckpt('prologue')
        main_schedule()
    except _Stop:
        pass

    T.emit(nc, es)
    es.close()
    return nc, sorted(dbg_out.keys())


def _fm(a, nch):
    return np.ascontiguousarray(a.reshape(nch, 128, -1).transpose(1, 0, 2))


def _wgroups(w, col_lists, kchunks):
    out = []
    for cols in col_lists:
        cols = np.asarray(cols)
        blk = w[:, np.maximum(cols, 0)]
        if (cols < 0).any():
            blk = blk.copy(); blk[:, cols < 0] = 0.0
        out.append(blk.reshape(kchunks, 128, len(cols)).transpose(1, 0, 2).reshape(128, kchunks * len(cols)))
    return np.ascontiguousarray(np.stack(out))


def prep_shared(inp, cfg):
    DEPTH = cfg['DEPTH']
    NE = (DEPTH + 1) // 2; NO = DEPTH // 2
    f32 = np.float32
    sh = {}
    sh['wada'] = np.ascontiguousarray(inp['w_ada'].reshape(DEPTH, KC, 128, 6144).transpose(0, 2, 1, 3))
    sh['bada'] = np.ascontiguousarray(inp['b_ada'].reshape(DEPTH, 48, 128).transpose(2, 0, 1).reshape(128, DEPTH * 48))
    sh['npre'] = np.ascontiguousarray(inp['norm_pre'].reshape(DEPTH, 16, 128).transpose(2, 0, 1).reshape(128, DEPTH * 16))
    sh['npost'] = np.ascontiguousarray(inp['norm_post'].reshape(DEPTH, 16, 128).transpose(2, 0, 1).reshape(128, DEPTH * 16))
    sh['c_ident'] = np.eye(128, dtype=f32)
    sh['c_triu'] = np.triu(np.ones((64, 64), f32))
    m = np.where(np.arange(64)[None, :] >= np.arange(64)[:, None], 0.0, -1e30).astype(f32)
    sh['c_mask'] = np.ascontiguousarray(np.tile(m, (1, 8)))
    for i in range(NE):
        w = inp['w_in_even'][i]
        gl = []
        for (typ, gi) in EV_GROUPS:
            c0 = EV_COLS[typ][0] if typ in EV_COLS else 0
            if typ == 'ks':
                c0 = EV_COLS['k'][0]
                cols = []
                for jj in range(2):
                    b0 = c0 + (gi * 2 + jj) * 128
                    cols += list(range(b0 + 64, b0 + 128)) + list(range(b0, b0 + 64))
            elif typ == 'dt':
                cols = list(range(c0, c0 + 32)) + [-1] * 224
            else:
                cols = list(range(c0 + gi * 256, c0 + (gi + 1) * 256))
            gl.append(cols)
        sh[f'wie{i}'] = _wgroups(w, gl, 16)
        sh[f'woe{i}'] = _wgroups(inp['w_out_even'][i], [list(range(g * 128, (g + 1) * 128)) for g in range(16)], 32)
    for i in range(NO):
        w = inp['w_in_odd'][i]
        gl = []
        for j in range(32):
            gl.append(list(range(j * 128, (j + 1) * 128)) + list(range(8192 + j * 128, 8192 + (j + 1) * 128)))
            gl.append(list(range(4096 + j * 128, 4096 + (j + 1) * 128)) + list(range(12288 + j * 128, 12288 + (j + 1) * 128)))
        sh[f'wio{i}'] = _wgroups(w, gl, 16)
        sh[f'woo{i}'] = _wgroups(inp['w_out_odd'][i], [list(range(g * 128, (g + 1) * 128)) for g in range(16)], 32)
    sh['caw'] = np.ascontiguousarray(inp['conv_a_w'].reshape(NE, 4, 24, 128).transpose(3, 0, 2, 1).reshape(128, NE * 96))
    sh['cab'] = np.ascontiguousarray(inp['conv_a_b'].reshape(NE, 24, 128).transpose(2, 0, 1).reshape(128, NE * 24))
    sh['nssm'] = np.ascontiguousarray(inp['norm_ssm'].reshape(NE, 16, 128).transpose(2, 0, 1).reshape(128, NE * 16))
    for nm, key in [('dtb', 'dt_bias'), ('alog', 'a_log'), ('dsk', 'd_skip'), ('snk', 'sinks')]:
        sh[nm] = np.ascontiguousarray(np.broadcast_to(inp[key].reshape(1, NE * 32), (64, NE * 32)))
    if NO > 0:
        sh['ccw'] = np.ascontiguousarray(inp['conv_c_w'].reshape(NO, 3, 32, 128).transpose(3, 0, 2, 1).reshape(128, NO * 96))
    else:
        sh['ccw'] = np.zeros((128, 96), f32)
    return sh


def prep_core(inp, b, cfg):
    DEPTH = cfg['DEPTH']
    NE = (DEPTH + 1) // 2; NO = DEPTH // 2
    c = {}
    c['xp'] = _fm(inp['x_prompt'][b].T, 16)
    c['xs'] = _fm(inp['x_sample'][b].T, 16)
    cc = np.stack([inp['c_prompt'][b], inp['c_sample'][b]], axis=1)
    c['cT'] = _fm(cc, 16).reshape(128, 32)
    c['ckT'] = np.ascontiguousarray(inp['cache_k'][:, b].reshape(NE, 128, 4, 128).transpose(0, 3, 2, 1))
    c['cv'] = np.ascontiguousarray(inp['cache_v'][:, b].reshape(NE, 2, 64, 512))
    c['sca'] = np.ascontiguousarray(inp['state_conv_a'][:, b].reshape(NE, 3, 24, 128).transpose(0, 3, 2, 1).reshape(NE, 128, 72))
    c['sst'] = np.ascontiguousarray(inp['state_ssm'][:, b].reshape(NE, 2048, 128).transpose(0, 2, 1))
    if NO > 0:
        c['scc'] = np.ascontiguousarray(inp['state_conv_c'][:, b].reshape(NO, 2, 32, 128).transpose(0, 3, 2, 1).reshape(NO, 128, 64))
    else:
        c['scc'] = np.zeros((1, 128, 64), np.float32)
    return c


def assemble(results, cfg, nb):
    DEPTH = cfg['DEPTH']; SEQ = cfg['SEQ']
    NE = (DEPTH + 1) // 2; NO = DEPTH // 2

    def st(f):
        return np.stack([f(r) for r in results], axis=0)

    def unfm(a):
        return a.transpose(2, 1, 0).reshape(a.shape[2], -1)
    yp = st(lambda r: unfm(r['yp']))
    ys = st(lambda r: unfm(r['ys']))

    def kfix(a):
        return a.transpose(0, 3, 2, 1).reshape(NE, 128, 8, 64)

    def cafix(a):
        return a.reshape(NE, 128, 24, 3).transpose(0, 3, 2, 1).reshape(NE, 3, 3072)

    def ssfix(a):
        return a.transpose(0, 2, 1).reshape(NE, 32, 64, 128)

    def ccfix(a):
        return a[:NO].reshape(NO, 128, 32, 2).transpose(0, 3, 2, 1).reshape(NO, 2, 4096)
    outs = [yp, ys]
    for sfx in ['p', 's']:
        outs.append(np.stack([kfix(r['nk' + sfx]) for r in results], axis=1))
        outs.append(np.stack([r['nv' + sfx].reshape(NE, 128, 8, 64) for r in results], axis=1))
        outs.append(np.stack([cafix(r['nca' + sfx]) for r in results], axis=1))
        outs.append(np.stack([ssfix(r['nss' + sfx]) for r in results], axis=1))
        outs.append(np.stack([ccfix(r['ncc' + sfx]) for r in results], axis=1))
    return tuple(np.ascontiguousarray(o.astype(np.float32)) for o in outs)


_CACHE = {}


def kernel(**inputs):
    inp = {k: np.asarray(v) for k, v in inputs.items()}
    B, SEQ, _ = inp['x_prompt'].shape
    DEPTH = inp['w_ada'].shape[0]
    cfg = dict(SEQ=SEQ, DEPTH=DEPTH, TP=256)
    key = (SEQ, DEPTH)
    if key not in _CACHE:
        _CACHE[key] = build(cfg)[0]
    nc = _CACHE[key]
    sh = prep_shared(inp, cfg)
    in_maps = []
    for b in range(B):
        m = dict(sh)
        m.update(prep_core(inp, b, cfg))
        in_maps.append(m)
    res = run_bass_kernel_spmd(nc, in_maps, core_ids=list(range(B)))
    return assemble(res.results, cfg, B)
```

```python
import numpy as np
from contextlib import ExitStack
import concourse.bass as bass
import concourse.mybir as mybir
from concourse.bass_utils import run_bass_kernel_spmd

F32 = mybir.dt.float32
BF16 = mybir.dt.bfloat16
AF = mybir.ActivationFunctionType
ALU = mybir.AluOpType
AX = mybir.AxisListType

D = 2048
KC = 16
CH = 64
EPS = 1e-6
NS = 3
SAME_SYNC = True

EV_COLS = dict(z=(0, 2048), xbc=(2048, 5120), dt=(5120, 5152), q=(5152, 7200), k=(7200, 7712),
               v=(7712, 8224), g=(8224, 10272))
EV_GROUPS = ([('dt', 0)] + [('xbc', i) for i in range(12)] + [('z', i) for i in range(8)] +
             [('k', i) for i in range(2)] + [('ks', i) for i in range(2)] + [('v', i) for i in range(2)] +
             [('q', i) for i in range(8)] + [('g', i) for i in range(8)])


class Op:
    __slots__ = ('eng', 'fn', 'waits', 'inc', 'chan', 'count')

    def __init__(self, eng, fn, chan):
        self.eng = eng; self.fn = fn; self.chan = chan; self.waits = set(); self.inc = False; self.count = 0


class Tracker:
    def __init__(self):
        self.ops = []
        self.lastw = {}
        self.readers = {}
        self.chan_last = {}

    def add(self, eng, fn, reads=(), writes=(), chan=None):
        idx = len(self.ops)
        op = Op(eng, fn, chan)
        deps = set()
        for k in reads:
            w = self.lastw.get(k)
            if w is not None:
                deps.add(w)
            if k in ('psW', 'psW2') or (isinstance(k, tuple) and k[0] == 'ps'):
                for st_, r in self.readers.get(k, {}).items():
                    if st_ != (('c', chan) if chan else ('e', eng)):
                        deps.add(r)
        for k in writes:
            w = self.lastw.get(k)
            if w is not None:
                deps.add(w)
            for r in self.readers.get(k, {}).values():
                deps.add(r)
        stream = ('c', chan) if chan else ('e', eng)
        for k in reads:
            self.readers.setdefault(k, {})[stream] = idx
        for k in writes:
            self.lastw[k] = idx
            self.readers[k] = {}
        for d in deps:
            p = self.ops[d]
            if p.chan is not None:
                op.waits.add(self.chan_last[p.chan])
            else:
                if p.eng == eng and chan is None and (eng == 'pe' or not SAME_SYNC):
                    continue
                p.inc = True
                op.waits.add(d)
        if chan:
            prev = self.chan_last.get(chan)
            if prev is not None:
                op.waits.add(prev)
            self.chan_last[chan] = idx
        self.ops.append(op)

    def emit(self, nc, es):
        engines = {'pe': nc.tensor, 'act': nc.scalar, 'dve': nc.vector, 'pool': nc.gpsimd, 'sp': nc.sync}
        ecount = {}
        ccount = {}
        for op in self.ops:
            if op.chan:
                ccount[op.chan] = ccount.get(op.chan, 0) + 16
                op.count = ccount[op.chan]
            elif op.inc:
                ecount[op.eng] = ecount.get(op.eng, 0) + 1
                op.count = ecount[op.eng]
        sems = {}
        for e in ecount:
            sems[('e', e)] = es.enter_context(nc.semaphore('se_' + e))
        for c in ccount:
            sems[('c', c)] = es.enter_context(nc.semaphore('sc_' + c))
        waited = {e: {} for e in engines}
        for op in self.ops:
            E = engines[op.eng]
            need = {}
            for d in op.waits:
                p = self.ops[d]
                key = ('c', p.chan) if p.chan else ('e', p.eng)
                need[key] = max(need.get(key, 0), p.count)
            wd = waited[op.eng]
            for key, val in need.items():
                if wd.get(key, 0) < val:
                    E.wait_ge(sems[key], val)
                    wd[key] = val
            ins = op.fn(E)
            if op.chan:
                ins.then_inc(sems[('c', op.chan)], 16)
            elif op.inc:
                ins.then_inc(sems[('e', op.eng)], 1)
        for c, val in ccount.items():
            nc.sync.wait_ge(sems[('c', c)], val)


class _Stop(Exception):
    pass


def build(cfg, dbg_names=()):
    SEQ = cfg['SEQ']; DEPTH = cfg['DEPTH']; TP = cfg['TP']
    NT = SEQ // TP
    TW = TP + CH
    NE = (DEPTH + 1) // 2
    NO = DEPTH // 2
    NCP = TP // CH
    nc = bass.Bass("TRN2", target_bir_lowering=False)
    es = ExitStack()
    T = Tracker()
    dbg_out = {}

    def din(name, shape):
        return nc.dram_tensor(name, list(shape), F32, kind="ExternalInput").ap()

    def dout(name, shape):
        return nc.dram_tensor(name, list(shape), F32, kind="ExternalOutput").ap()

    d_xp = din('xp', [128, KC, SEQ]); d_xs = din('xs', [128, KC, CH]); d_cT = din('cT', [128, KC * 2])
    d_wada = din('wada', [DEPTH, 128, KC, 6144])
    d_bada = din('bada', [128, DEPTH * 48]); d_npre = din('npre', [128, DEPTH * 16]); d_npost = din('npost', [128, DEPTH * 16])
    d_ident = din('c_ident', [128, 128]); d_triu = din('c_triu', [64, 64]); d_mask = din('c_mask', [64, 512])
    d_wie = [din(f'wie{i}', [len(EV_GROUPS), 128, 4096]) for i in range(NE)]
    d_woe = [din(f'woe{i}', [16, 128, 4096]) for i in range(NE)]
    d_wio = [din(f'wio{i}', [64, 128, 4096]) for i in range(NO)]
    d_woo = [din(f'woo{i}', [16, 128, 4096]) for i in range(NO)]
    d_caw = din('caw', [128, NE * 24 * 4]); d_cab = din('cab', [128, NE * 24])
    d_dtb = din('dtb', [64, NE * 32]); d_alog = din('alog', [64, NE * 32]); d_dsk = din('dsk', [64, NE * 32])
    d_snk = din('snk', [64, NE * 32]); d_nssm = din('nssm', [128, NE * 16])
    d_ccw = din('ccw', [128, max(NO, 1) * 32 * 3])
    d_ckT = din('ckT', [NE, 128, 4, 128]); d_cv = din('cv', [NE, 2, 64, 512])
    d_sca = din('sca', [NE, 128, 24 * 3]); d_sst = din('sst', [NE, 128, 2048]); d_scc = din('scc', [max(NO, 1), 128, 32 * 2])
    o_yp = dout('yp', [128, KC, SEQ]); o_ys = dout('ys', [128, KC, CH])
    o_nkp = dout('nkp', [NE, 128, 4, 128]); o_nvp = dout('nvp', [NE, 128, 512]); o_ncap = dout('ncap', [NE, 128, 72])
    o_nssp = dout('nssp', [NE, 128, 2048]); o_nccp = dout('nccp', [max(NO, 1), 128, 64])
    o_nks = dout('nks', [NE, 128, 4, 128]); o_nvs = dout('nvs', [NE, 128, 512]); o_ncas = dout('ncas', [NE, 128, 72])
    o_nsss = dout('nsss', [NE, 128, 2048]); o_nccs = dout('nccs', [max(NO, 1), 128, 64])

    def sb(name, shape, dt=F32):
        return es.enter_context(nc.sbuf_tensor('s_' + name, list(shape), dt))

    def ps(name, shape, dt=F32):
        return es.enter_context(nc.psum_tensor(name, list(shape), dt))

    xT = sb('xT', [128, KC, TW])
    hT = sb('hT', [128, KC, TW], BF16)
    hflat = hT[:].rearrange("p a b -> p (a b)")
    arena2 = sb('arena2', [128, 40 * TW], BF16)
    xbcs = arena2[:, 0:24 * TW].rearrange("p (a b) -> p a b", a=24)
    qT = arena2[:, 24 * TW:40 * TW].rearrange("p (a b) -> p a b", a=16)
    obuf = arena2[:, 0:32 * TW].bitcast(F32).rearrange("p (a b) -> p a b", a=16)
    mix = sb('mix', [128, 32, TW], BF16)
    sq = mix[:, 0:16, :]
    wbuf = [sb(f'wbuf{i}', [128, 4096], BF16) for i in range(NS)]
    C8 = TW + 8
    scr = sb('scr', [128, 5 * TW + 2 * C8 + 256 + TW // 2])
    tmpA = [scr[:, 0:TW], scr[:, TW:2 * TW]]; rstd = scr[:, 2 * TW:3 * TW]
    ident_f = sb('ident_f', [128, 128]); ident_b = sb('ident_b', [128, 128], BF16)
    ones_b = sb('ones_b', [128, 128], BF16); ones_f = sb('ones_f', [64, 128])
    triu_f = sb('triu_f', [64, 64]); mask_b = sb('mask_b', [64, 512], BF16)
    eps_c = sb('eps_c', [128, 2])
    cT = sb('cT', [128, KC * 2]); scT = sb('scT', [128, KC * 2]); scb = sb('scb', [128, KC * 2], BF16)
    bada = sb('bada', [128, DEPTH * 48]); npre = sb('npre', [128, DEPTH * 16]); npost = sb('npost', [128, DEPTH * 16])
    modt = sb('modt', [128, DEPTH * 96])
    modc = sb('modc', [128, DEPTH * 2 * 3 * 16])
    caw = sb('caw', [128, NE * 96]); cab = sb('cab', [128, NE * 24]); nssm = sb('nssm', [128, NE * 16])
    dtb = sb('dtb', [64, NE * 32]); alog = sb('alog', [64, NE * 32]); dsk = sb('dsk', [64, NE * 32]); snk = sb('snk', [64, NE * 32])
    aneg = sb('aneg', [64, NE * 32])
    Dm = [sb('Dm0', [64, 32 * 64], BF16)] * NE
    ccw = sb('ccw', [128, max(NO, 1) * 96])
    kP = [sb(f'kP{i}', [128, 4, 128 + TP], BF16) for i in range(NE)]
    kPs = [sb(f'kPs{i}', [128, 4, 128 + TP], BF16) for i in range(NE)]
    vP = [sb(f'vP{i}', [64, 2 + NCP, 512], BF16) for i in range(NE)]
    stT = [sb(f'stT{i}', [128, 2048]) for i in range(NE)]
    ccar = [sb(f'ccar{i}', [128, 24 * 3]) for i in range(NE)]
    vcar = [sb(f'vcar{i}', [128, 32 * 2]) for i in range(NO)]
    kS = sb('kS', [128, 4, 192], BF16); kSs = sb('kSs', [128, 4, 192], BF16)
    vS = sb('vS', [64, 3, 512], BF16)
    stS = sb('stS', [128, 2048]); scaS = sb('scaS', [128, 72]); sccS = sb('sccS', [128, 64])
    stbf = sb('stbf', [128, 2048], BF16)
    o_ = 3 * TW
    cst = [scr[:, o_:o_ + C8], scr[:, 0:C8]]
    cacc = [scr[:, o_ + C8:o_ + 2 * C8], scr[:, C8:2 * C8]]
    o_ += 2 * C8
    stg = [tmpA[0], tmpA[1], scr[:, o_:o_ + TW], scr[:, o_ + TW:o_ + 2 * TW]]
    o_ += 2 * TW
    kvout = [scr[:, o_:o_ + 256]] * 2
    sqo = [scr[:, o_ + 256:o_ + 256 + TW // 2].bitcast(BF16)] * 2
    dtall = sb('dtall', [64, (NCP + 1) * 32]); spx = sb('spx', [64, (NCP + 1) * 32])
    sm = {n: sb('sm_' + n, [128, 32]) for n in ['dta', 'acum', 'ldt', 'amb', 'E2', 'wte', 'tw', 'Edec']}
    xtok = sb('xtok', [64, 2560], BF16); xw = sb('xw', [64, 2048], BF16)
    cbT = sb('cbT', [64, 256])
    Dgh = [sb('Dgh0', [64, 512], BF16)[:], hflat[0:64, 11 * TW:11 * TW + 512]]
    Dgl = [sb('Dgl0', [64, 512], BF16)[:], hflat[0:64, 11 * TW + 512:11 * TW + 1024]]
    DGK = [[('Dg', 0)], [('h', k_) for k_ in range(11, 15)]]
    ahl = sb('ahl', [64, 64], BF16)
    seg = [sb('seg0', [64, 512])[:], scr[0:64, 0:512]]
    mixT = [sb('mixT0', [64, 512], BF16)[:], scr[0:64, 1024:1280].bitcast(BF16)]
    t1 = [sb('t10', [64, 512])[:], scr[0:64, 512:1024]]
    ytok = [sb('ytok0', [64, 512], BF16)[:], scr[0:64, 1280:1536].bitcast(BF16)]
    gy = [sb('gy0', [128, 256])[:], scr[:, 1536:1792]]
    sqg = [sb('sqg0', [128, 256], BF16)[:], scr[:, 1792:1920].bitcast(BF16)]
    rsg = [sb('rsg0', [128, 64])[:], scr[:, 1920:1984]]
    Pn = [hflat[0:64, 0:768], hflat[0:64, 5 * TW:5 * TW + 768]]
    Pq = Pn
    PTq = [hflat[0:64, 768:1536], hflat[0:64, 5 * TW + 768:5 * TW + 1536]]
    asm = [{n: sb(f'asm{i}_' + n, [64, 4]) for n in ['mx', 'negm', 'rs', 'es', 'den', 'rinv']} for i in range(2)]
    NSB = 4
    psS = [ps(f'psS{i}', [128, 512]) for i in range(NSB)]
    psW = ps('psW', [128, 1024])
    psW2 = ps('psW2', [128, 1024])
    PW = [psW, psW2]; PWK = ['psW', 'psW2']
    ps_ctr = [0]

    def bank():
        i = ps_ctr[0] % NSB
        ps_ctr[0] += 1
        return psS[i], ('ps', i)

    def mm(out, lhsT, rhs, start, stop, reads, writes):
        T.add('pe', lambda e: e.matmul(out, lhsT, rhs, start=start, stop=stop), reads, writes)

    def tr(out, in_, ident, reads, writes):
        T.add('pe', lambda e: e.transpose(out, in_, ident), reads, writes)

    def act(out, in_, func, reads, writes, bias=None, scale=None, accum=None, eng='act'):
        kw = {}
        if bias is not None: kw['bias'] = bias
        if scale is not None: kw['scale'] = scale
        if accum is not None: kw['accum_out'] = accum
        T.add('act', lambda e: e.activation(out=out, in_=in_, func=func, **kw), reads, writes)

    def tt(eng, out, in0, in1, op, reads, writes):
        T.add(eng, lambda e: e.tensor_tensor(out, in0, in1, op), reads, writes)

    def tsc(eng, out, in0, s1, s2, op0, op1, reads, writes):
        if op1 is None:
            T.add(eng, lambda e: e.tensor_scalar(out, in0, s1, None, op0), reads, writes)
        else:
            T.add(eng, lambda e: e.tensor_scalar(out, in0, s1, s2, op0, op1), reads, writes)

    def stt(eng, out, in0, scalar, in1, op0, op1, reads, writes):
        T.add(eng, lambda e: e.scalar_tensor_tensor(out, in0, scalar, in1, op0, op1), reads, writes)

    def rsqrt(out, in_, scale, reads, wkey):
        act(out, in_, AF.Ln, reads, [wkey], bias=eps_c[0:out.shape[0], 0:1], scale=scale)
        act(out, out, AF.Exp, [wkey], [wkey], scale=-0.5)

    def cp(eng, out, in_, reads, writes):
        if eng == 'act':
            T.add('act', lambda e: e.copy(out, in_), reads, writes)
        else:
            T.add(eng, lambda e: e.tensor_copy(out, in_), reads, writes)

    def memset(eng, ap, val, writes):
        T.add(eng, lambda e: e.memset(ap, val), (), writes)

    def dma(eng, out, in_, reads, writes, chan):
        T.add(eng, lambda e: e.dma_start(out=out, in_=in_), reads, writes, chan=chan)

    def dbg(name, ap, reads, shape):
        if name not in dbg_names:
            return
        if name not in dbg_out:
            dbg_out[name] = dout('dbg_' + name, shape)
        dma('pool', dbg_out[name], ap, reads, [('dbgo', name)], 'dbg')

    def ckpt(name):
        if cfg.get('stop') == name:
            raise _Stop()

    def keys(name, rng):
        return [(name, i) for i in rng]

    XK = keys('x', range(KC)); HK = keys('h', range(KC))
    PKS = [keys('h', range(0, 5)), keys('h', range(5, 10))]
    STSK = [('stS', g_) for g_ in range(4)]; STBK = [('stbf', g_) for g_ in range(4)]
    SCRK = [('tmpA', 0), ('tmpA', 1), 'rstd', ('cst', 0), ('cacc', 0), ('cst', 1), ('cacc', 1), ('stg', 2), ('stg', 3), ('kvout', 0), ('sqo', 0)]
    SETB = [(n_, 1) for n_ in ['seg', 'mixT', 't1', 'ytok', 'gy', 'sqg', 'rsg']]

    dma('sp', ident_f[:], d_ident, [], ['ident_f'], 'cst')
    dma('pool', ident_b[:], d_ident, [], ['ident_b'], 'cstb')
    dma('sp', triu_f[:], d_triu, [], ['triu_f'], 'cst')
    dma('pool', mask_b[:], d_mask, [], ['mask_b'], 'cstb')
    memset('dve', ones_b[:], 1.0, ['ones_b'])
    memset('dve', ones_f[:], 1.0, ['ones_f'])
    memset('dve', eps_c[:, 0:1], EPS, ['eps_c'])
    memset('dve', eps_c[:, 1:2], 1.0, ['eps_c'])
    for (t_, d_, k_) in [(cT, d_cT, 'cT'), (bada, d_bada, 'bada'), (npre, d_npre, 'npre'), (npost, d_npost, 'npost'),
                         (caw, d_caw, 'caw'), (cab, d_cab, 'cab'), (nssm, d_nssm, 'nssm'), (dtb, d_dtb, 'dtb'),
                         (alog, d_alog, 'alog'), (dsk, d_dsk, 'dsk'), (snk, d_snk, 'snk'), (ccw, d_ccw, 'ccw')]:
        dma('sp', t_[:], d_, [], [k_], 'cst')
    act(scT[:], cT[:], AF.Silu, ['cT'], ['scT'])
    act(aneg[:], alog[:], AF.Exp, ['alog'], ['aneg'])
    tsc('dve', aneg[:], aneg[:], -1.0, None, ALU.mult, None, ['aneg'], ['aneg'])
    cp('dve', scb[:], scT[:], ['scT'], ['scb'])
    nb_ctr = 0
    for l in range(DEPTH):
        pst, pk = psW[:, 512:1024], 'psW'
        for nb in range(24):
            s_ = nb_ctr % NS
            wt = wbuf[s_]; wk = ('w', s_)
            dma('pool', wt[:].rearrange("p (a b) -> p a b", a=KC), d_wada[l, :, :, nb * 256:(nb + 1) * 256], [], [wk], f'w{s_}')
            nb_ctr += 1
            pr, prk = bank()
            for kc in range(KC):
                mm(pr[0:2, 0:256], scb[:, kc * 2:kc * 2 + 2], wt[:, kc * 256:(kc + 1) * 256], kc == 0, kc == KC - 1, [wk, 'scb'], [prk])
            mr = tmpA[nb % 2]; mrk = ('tmpA', nb % 2)
            cp('dve', mr[0:2, 0:256], pr[0:2, 0:256], [prk], [mrk])
            for jj in range(2):
                j = nb * 2 + jj
                tr(pst[:, j * 2:j * 2 + 2], mr[0:2, jj * 128:(jj + 1) * 128], ident_f[0:2, 0:2], [mrk, 'ident_f'], [pk])
        tt('dve', modt[:, l * 96:(l + 1) * 96].rearrange("p (j w) -> p j w", w=2),
           pst[:, 0:96].rearrange("p (j w) -> p j w", w=2),
           bada[:, l * 48:(l + 1) * 48].unsqueeze(2).to_broadcast([128, 48, 2]), ALU.add,
           [pk, 'bada'], [('modt', l)])
        for w in range(2):
            base = ((l * 2 + w) * 3) * 16
            mv = modt[:, l * 96:(l + 1) * 96].rearrange("p (j w) -> p j w", w=2)
            stt('dve', modc[:, base:base + 16], mv[:, 16:32, w], 1.0, npre[:, l * 16:(l + 1) * 16], ALU.add, ALU.mult,
                [('modt', l), 'npre'], [('modc', l)])
            cp('dve', modc[:, base + 16:base + 32], mv[:, 0:16, w], [('modt', l)], [('modc', l)])
            tt('dve', modc[:, base + 32:base + 48], mv[:, 32:48, w], npost[:, l * 16:(l + 1) * 16], ALU.mult,
               [('modt', l), 'npost'], [('modc', l)])


    def mc_unused():
        pass

    def mc(l, w, kind, kc):
        o = ((l * 2 + w) * 3 + kind) * 16 + kc
        return modc[:, o:o + 1]

    wseq = []
    for ti in range(NT):
        for l in range(DEPTH):
            i = l // 2
            if l % 2 == 0:
                wseq += [d_wie[i][g] for g in range(len(EV_GROUPS))] + [d_woe[i][g] for g in range(16)]
            else:
                wseq += [d_wio[i][g] for g in range(64)] + [d_woo[i][g] for g in range(16)]
    wst = dict(issued=0, consumed=0)
    NG = len(wseq) // NT
    wsc = None
    if NT > 1:
        wsc = []
        for l in range(DEPTH):
            ng_l = (len(EV_GROUPS) + 16) if l % 2 == 0 else 80
            t_ = nc.dram_tensor(f'wscratch{l}', [ng_l, 128, 4096], BF16, kind="Internal").ap()
            wsc += [t_[g_] for g_ in range(ng_l)]
        assert len(wsc) == NG

    def wnext():
        while wst['issued'] < min(len(wseq), wst['consumed'] + NS):
            n = wst['issued']
            s = n % NS
            g_ = n % NG
            if n < NG:
                dma('pool', wbuf[s][:], wseq[n], [], [('w', s)], f'w{s}')
                if wsc is not None:
                    dma('sp', wsc[g_], wbuf[s][:], [('w', s)], [('wsc', g_)], f'wb{s}')
            else:
                dma('pool', wbuf[s][:], wsc[g_], [('wsc', g_)], [('w', s)], f'w{s}')
            wst['issued'] += 1
        s = wst['consumed'] % NS
        wst['consumed'] += 1
        return wbuf[s], ('w', s)

    def phase_A(l, Tt, segs):
        ckpt('A0')
        act(sq[:, :, 0:Tt], xT[:, :, 0:Tt], AF.Square, XK, keys('mix', range(16)))
        ckpt('A1')
        pst, pk = bank()
        for kc in range(KC):
            mm(pst[:, 0:Tt], ones_b[:], sq[:, kc, 0:Tt], kc == 0, kc == KC - 1, [('mix', kc), 'ones_b'], [pk])
        ckpt('A2')
        rsqrt(rstd[:, 0:Tt], pst[:, 0:Tt], 1.0 / D, [pk, 'eps_c'], 'rstd')
        ckpt('A3')
        for kc in range(KC):
            tb = tmpA[kc % 2]; tk = ('tmpA', kc % 2)
            tt('dve', tb[:, 0:Tt], xT[:, kc, 0:Tt], rstd[:, 0:Tt], ALU.mult, [('x', kc), 'rstd'], [tk])
            for (c0, c1, w) in segs:
                act(hT[:, kc, c0:c1], tb[:, c0:c1], AF.Identity, [tk, ('modc', l)], [('h', kc)],
                    bias=mc(l, w, 1, kc), scale=mc(l, w, 0, kc))

    def proj_fm(wt, wk, col0, ncols_k, Tt, rhs_fn, rkeys, nk):
        pst, pk = bank()
        for kc in range(nk):
            mm(pst[:, 0:Tt], wt[:, kc * ncols_k + col0: kc * ncols_k + col0 + 128], rhs_fn(kc), kc == 0, kc == nk - 1,
               [wk] + rkeys(kc), [pk])
        return pst, pk

    def phase_D(l, Tt, segs, last_tile, ti):
        ssb, ssk = psW[:, 512:1024], 'psW'
        for j in range(KC):
            wt, wk = wnext()
            pst, pk = proj_fm(wt, wk, 0, 128, Tt, lambda kc: mix[:, kc, 0:Tt], lambda kc: [('mix', kc)], 32)
            wr = [('o', j)]
            if j == 0:
                wr = wr + HK + keys('xb', range(24)) + keys('q', range(16))
            cp('act', obuf[:, j, 0:Tt], pst[:, 0:Tt], [pk], wr)
            sb_ = sqo[j % 2]; sk = ('sqo', 0)
            act(sb_[:, 0:Tt], pst[:, 0:Tt], AF.Square, [pk], [sk])
            mm(ssb[:, 0:Tt], ones_b[:], sb_[:, 0:Tt], j == 0, j == KC - 1, [sk, 'ones_b'], [ssk])
        rsqrt(rstd[:, 0:Tt], ssb[:, 0:Tt], 1.0 / D, [ssk, 'eps_c'], 'rstd')
        for j in range(KC):
            tb = tmpA[j % 2]; tk = ('tmpA', j % 2)
            tt('dve', tb[:, 0:Tt], obuf[:, j, 0:Tt], rstd[:, 0:Tt], ALU.mult, [('o', j), 'rstd'], [tk])
            for (c0, c1, w) in segs:
                stt('dve', xT[:, j, c0:c1], tb[:, c0:c1], mc(l, w, 2, j), xT[:, j, c0:c1], ALU.mult, ALU.add,
                    [tk, ('modc', l), ('x', j)], [('x', j)])

    def even_layer(l, ti, Tt, segs, chunks):
        i = l // 2
        last = (ti == NT - 1)
        has_s = (ti == 0)
        L = Tt + (3 if has_s else 0)
        if ti == 0:
            memset('dve', stT[i][:], 0.0, [('stT', i, g_) for g_ in range(4)])
            memset('dve', ccar[i][:], 0.0, [('ccar', i)])
            dma('sp', scaS[:], d_sca[i], [], ['scaS'], 'sin')
            dma('sp', stS[:], d_sst[i], [], STSK, 'sin')
            dma('pool', kS[:, :, 0:128], d_ckT[i], [], ['kS'], 'sinb')
            dma('pool', kSs[0:64, :, 0:128], d_ckT[i, 64:128], [], ['kSs'], 'sinb')
            dma('pool', kSs[64:128, :, 0:128], d_ckT[i, 0:64], [], ['kSs'], 'sinb')
            dma('pool', vS[:, 0:2, :], d_cv[i].rearrange("b s c -> s b c"), [], ['vS'], 'sinb')
            dma('sp', o_nks[i, :, :, 0:64], d_ckT[i, :, :, 64:128], [], [('onks', i)], 'outs')
            dma('sp', o_nvs[i, 0:64, :], d_cv[i, 1], [], [('onvs', i)], 'outs')
        ckpt(f'pre{l}_{ti}')
        tt('dve', Dm[i][:].rearrange("p (h c) -> p h c", h=32),
           dsk[:, i * 32:(i + 1) * 32].unsqueeze(2).to_broadcast([64, 32, 64]),
           ident_f[0:64, 0:64].unsqueeze(1).to_broadcast([64, 32, 64]), ALU.mult,
           ['dsk', 'ident_f'], [('Dm', 0)])
        phase_A(l, Tt, segs)
        ckpt(f'A{l}_{ti}')
        dbg(f'h{l}_{ti}', hT[:, :, 0:Tt], HK, [128, KC, Tt])
        def do_group(typ, gi):
            ckpt(f'G{typ}{gi}')
            wt, wk = wnext()
            if typ == 'dt':
                dtps, dtk = bank()
                for ci, c in enumerate(chunks):
                    for kc in range(KC):
                        mm(dtps[0:64, ci * 32:(ci + 1) * 32], hT[:, kc, c['col']:c['col'] + 64], wt[:, kc * 256:kc * 256 + 32],
                           kc == 0, kc == KC - 1, [wk, ('h', kc)], [dtk])
                nch = len(chunks)
                n32 = nch * 32
                xa = spx[:, 0:n32]
                dv = dtall[:, 0:n32]
                tt('dve', dv.rearrange("p (c h) -> p c h", h=32), dtps[0:64, 0:n32].rearrange("p (c h) -> p c h", h=32),
                   dtb[:, i * 32:(i + 1) * 32].unsqueeze(1).to_broadcast([64, nch, 32]), ALU.add, [dtk, 'dtb'], ['dtall'])
                act(xa, dv, AF.Abs, ['dtall'], ['spx'])
                act(xa, xa, AF.Exp, ['spx'], ['spx'], scale=-1.0)
                act(xa, xa, AF.Ln, ['spx', 'eps_c'], ['spx'], bias=eps_c[0:64, 1:2])
                stt('dve', dv, dv, 0.0, xa, ALU.max, ALU.add, ['dtall', 'spx'], ['dtall'])
            elif typ == 'xbc':
                for jj in range(2):
                    f = gi * 2 + jj
                    pst, pk = proj_fm(wt, wk, jj * 128, 256, Tt, lambda kc: hT[:, kc, 0:Tt], lambda kc: [('h', kc)], KC)
                    cs = cst[f % 2]; ck = ('cst', f % 2); ca = cacc[f % 2]; cak = ('cacc', f % 2)
                    cp('act', cs[:, 3:3 + TP], pst[:, 0:TP], [pk], [ck])
                    cp('dve', cs[:, 0:3], ccar[i][:, f * 3:f * 3 + 3], [('ccar', i)], [ck])
                    if has_s:
                        cp('act', cs[:, 6 + TP:6 + TP + CH], pst[:, TP:TP + CH], [pk], [ck])
                        cp('dve', cs[:, 3 + TP:6 + TP], scaS[:, f * 3:f * 3 + 3], ['scaS'], [ck])
                        cp('dve', scaS[:, f * 3:f * 3 + 3], cs[:, 3 + TP + CH:6 + TP + CH], [ck], ['scaS'])
                    cp('dve', ccar[i][:, f * 3:f * 3 + 3], cs[:, TP:TP + 3], [ck], [('ccar', i)])
                    wv = caw[:, (i * 24 + f) * 4:(i * 24 + f) * 4 + 4]
                    tsc('dve', ca[:, 0:L], cs[:, 3:3 + L], wv[:, 3:4], cab[:, i * 24 + f:i * 24 + f + 1], ALU.mult, ALU.add,
                        [ck, 'caw', 'cab'], [cak])
                    for tap in range(3):
                        stt('dve', ca[:, 0:L], cs[:, tap:tap + L], wv[:, tap:tap + 1], ca[:, 0:L], ALU.mult, ALU.add,
                            [ck, cak, 'caw'], [cak])
                    act(xbcs[:, f, 0:TP], ca[:, 0:TP], AF.Silu, [cak], [('xb', f)] + (keys('o', range(KC)) if f == 0 else []))
                    if has_s:
                        act(xbcs[:, f, TP:TP + CH], ca[:, TP + 3:TP + 3 + CH], AF.Silu, [cak], [('xb', f)])
            elif typ in ('z', 'g'):
                for jj in range(2):
                    f = gi * 2 + jj
                    pst, pk = proj_fm(wt, wk, jj * 128, 256, Tt, lambda kc: hT[:, kc, 0:Tt], lambda kc: [('h', kc)], KC)
                    mf = f if typ == 'z' else 16 + f
                    act(mix[:, mf, 0:Tt], pst[:, 0:Tt], AF.Silu, [pk], [('mix', mf)])
            elif typ == 'q':
                for jj in range(2):
                    f = gi * 2 + jj
                    pst, pk = proj_fm(wt, wk, jj * 128, 256, Tt, lambda kc: hT[:, kc, 0:Tt], lambda kc: [('h', kc)], KC)
                    tsc('dve', qT[:, f, 0:Tt], pst[:, 0:Tt], 0.125, None, ALU.mult, None, [pk], [('q', f)] + (keys('o', range(KC)) if f == 0 else []))
            elif typ == 'k':
                for jj in range(2):
                    f = gi * 2 + jj
                    pst, pk = proj_fm(wt, wk, jj * 128, 256, Tt, lambda kc: hT[:, kc, 0:Tt], lambda kc: [('h', kc)], KC)
                    cp('dve', kP[i][:, f, 128:128 + TP], pst[:, 0:TP], [pk], [('kP', i)])
                    ckpt(f'K1_{f}')
                    if has_s:
                        cp('dve', kS[:, f, 128:192], pst[:, TP:TP + CH], [pk], ['kS'])
                        ckpt(f'K2_{f}')
                        ko = kvout[f % 2]; kk = ('kvout', 0)
                        cp('dve', ko[:, 0:CH], pst[:, TP:TP + CH], [pk], [kk])
                        ckpt(f'K3_{f}')
                        dma('sp', o_nks[i, :, f, 64:128], ko[:, 0:CH], [kk], [('onks', i)], 'outs')
                        ckpt(f'K4_{f}')
                    if last:
                        ko = kvout[f % 2]; kk = ('kvout', 0)
                        cp('act', ko[:, 0:128], pst[:, TP - 128:TP], [pk], [kk])
                        dma('sp', o_nkp[i, :, f, :], ko[:, 0:128], [kk], [('onkp', i)], 'outs')
            elif typ == 'ks':
                for jj in range(2):
                    f = gi * 2 + jj
                    pst, pk = proj_fm(wt, wk, jj * 128, 256, Tt, lambda kc: hT[:, kc, 0:Tt], lambda kc: [('h', kc)], KC)
                    cp('dve', kPs[i][:, f, 128:128 + TP], pst[:, 0:TP], [pk], [('kPs', i)])
                    if has_s:
                        cp('dve', kSs[:, f, 128:192], pst[:, TP:TP + CH], [pk], ['kSs'])
            elif typ == 'v':
                for ci, c in enumerate(chunks):
                    pst, pk = bank()
                    for kc in range(KC):
                        mm(pst[0:64, 0:256], hT[:, kc, c['col']:c['col'] + 64], wt[:, kc * 256:(kc + 1) * 256],
                           kc == 0, kc == KC - 1, [wk, ('h', kc)], [pk])
                    if c['seg'] == 'p':
                        cp('dve', vP[i][:, 2 + c['cl'], gi * 256:(gi + 1) * 256], pst[0:64, 0:256], [pk], [('vP', i)])
                        if last and c['cl'] >= NCP - 2:
                            blk = c['cl'] - (NCP - 2)
                            ko = kvout[ci % 2]; kk = ('kvout', 0)
                            cp('act', ko[0:64, 0:256], pst[0:64, 0:256], [pk], [kk])
                            dma('sp', o_nvp[i, blk * 64:(blk + 1) * 64, gi * 256:(gi + 1) * 256], ko[0:64, 0:256], [kk], [('onvp', i)], 'outs')
                    else:
                        cp('dve', vS[:, 2, gi * 256:(gi + 1) * 256], pst[0:64, 0:256], [pk], ['vS'])
                        ko = kvout[ci % 2]; kk = ('kvout', 0)
                        cp('act', ko[0:64, 0:256], pst[0:64, 0:256], [pk], [kk])
                        dma('sp', o_nvs[i, 64:128, gi * 256:(gi + 1) * 256], ko[0:64, 0:256], [kk], [('onvs', i)], 'outs')

        NG1 = 1 + 12 + 8
        for (typ, gi) in EV_GROUPS[:NG1]:
            do_group(typ, gi)

        def projB2():
            for (typ, gi) in EV_GROUPS[NG1:]:
                do_group(typ, gi)
                yield
        if last:
            dma('sp', o_ncap[i], ccar[i][:], [('ccar', i)], [('oncap', i)], 'outs')
        if has_s:
            dma('sp', o_ncas[i], scaS[:], ['scaS'], [('oncas', i)], 'outs')
        dbg(f'xbcs{l}_{ti}', xbcs[:, :, 0:Tt], keys('xb', range(24)), [128, 24, Tt])
        dbg(f'dt{l}_{ti}', dtall[:, 0:len(chunks) * 32], ['dtall'], [64, len(chunks) * 32])
        dbg(f'q{l}_{ti}', qT[:, :, 0:Tt], keys('q', range(16)), [128, 16, Tt])
        ckpt(f'B{l}_{ti}')
        cp('act', stbf[:], stT[i][:], [('stT', i, g_) for g_ in range(4)], STBK)
        def ssd_ctx(ci, c):
            isS = c['seg'] == 's'
            d = dict(col=c['col'], isS=isS)
            d['st_f'] = stS if isS else stT[i]
            d['st_k'] = (lambda g: ('stS', g)) if isS else (lambda g: ('stT', i, g))
            d['dtc'] = dtall[:, ci * 32:(ci + 1) * 32]
            for n in ['dta', 'acum', 'ldt', 'amb', 'E2', 'wte', 'tw']:
                d[n] = sm[n][0:64, :]
            d['Edec'] = sm['Edec']
            return d

        def ssd_pre(ci, c):
            X = ssd_ctx(ci, c)
            dtc, dta, acum, ldt, amb, E2, wte, tw, Edec = (X[k_] for k_ in ['dtc', 'dta', 'acum', 'ldt', 'amb', 'E2', 'wte', 'tw', 'Edec'])
            col = c['col']
            isS = c['seg'] == 's'
            if isS:
                cp('act', stbf[:], stS[:], STSK, STBK)
            tt('dve', dta, dtc, aneg[:, i * 32:(i + 1) * 32], ALU.mult, ['dtall', 'aneg'], ['dta'])
            pss, psk = bank()
            mm(pss[0:64, 0:32], triu_f[:], dta, True, True, ['triu_f', 'dta'], [psk])
            mm(pss[:, 32:64], ones_f[:], dta, True, True, ['ones_f', 'dta'], [psk])
            cp('act', acum, pss[0:64, 0:32], [psk], ['acum'])
            cp('dve', ahl[:, 0:32], acum, ['acum'], ['ahl'])
            tt('dve', ldt, acum, ahl[:, 0:32], ALU.subtract, ['acum', 'ahl'], ['ldt'])
            cp('dve', ahl[:, 32:64], ldt, ['ldt'], ['ahl'])
            act(ldt, dtc, AF.Ln, ['dtall'], ['ldt'])
            tt('dve', amb, acum, ldt, ALU.subtract, ['acum', 'ldt'], ['amb'])
            act(E2, acum, AF.Exp, ['acum'], ['E2'])
            tt('dve', tw, pss[0:64, 32:64], acum, ALU.subtract, [psk, 'acum'], ['tw'])
            act(tw, tw, AF.Exp, ['tw'], ['tw'])
            tt('dve', wte, tw, dtc, ALU.mult, ['tw', 'dtall'], ['wte'])
            act(Edec[:], pss[:, 32:64], AF.Exp, [psk], ['Edec'])
            yield
            for b0 in range(0, 20, 8):
                nb_ = min(8, 20 - b0)
                ptb, ptk = bank()
                pv = ptb[:].bitcast(BF16)
                for f in range(b0, b0 + nb_):
                    tr(pv[0:64, (f - b0) * 128:(f - b0 + 1) * 128], xbcs[:, f, col:col + 64], ident_b[:], [('xb', f), 'ident_b'], [ptk])
                cp('act' if b0 == 8 else 'dve', xtok[:, b0 * 128:(b0 + nb_) * 128], pv[0:64, 0:nb_ * 128], [ptk], ['xtok'])
            tt('dve', xw[:].rearrange("p (h c) -> p h c", h=32), xtok[:, 0:2048].rearrange("p (h c) -> p h c", h=32),
               wte.unsqueeze(2).to_broadcast([64, 32, 64]), ALU.mult, ['xtok', 'wte'], ['xw'])
            yield
            pcb, pcbk = bank()
            for g in range(4):
                mm(pcb[0:64, g * 64:(g + 1) * 64], xbcs[:, 16 + g, col:col + 64], xbcs[:, 20 + g, col:col + 64], True, True,
                   [('xb', 16 + g), ('xb', 20 + g)], [pcbk])
            cp('act', cbT[:], pcb[0:64, 0:256], [pcbk], ['cbT'])
            yield

        def ssd_grp(ci, c, groups, b2):
            X = ssd_ctx(ci, c)
            col = X['col']; st_f = X['st_f']; st_kf = X['st_k']
            acum, amb, E2, Edec = X['acum'], X['amb'], X['E2'], X['Edec']
            st_b = stbf
            for g in groups:
                hs = slice(g * 8, (g + 1) * 8)
                for (dgt, off) in ((Dgh[b2], 0), (Dgl[b2], 32)):
                    tt('pool', dgt.rearrange("p (r c) -> p r c", r=8), ahl[:, off + g * 8:off + (g + 1) * 8].unsqueeze(2).to_broadcast([64, 8, 64]),
                       ident_b[0:64, 0:64].unsqueeze(1).to_broadcast([64, 8, 64]), ALU.mult, ['ahl', 'ident_b'], DGK[b2])
                pA, pAk = bank()
                mm(pA[0:64, :], ones_b[0:64, 0:64], Dgh[b2], True, False, ['ones_b'] + DGK[b2], [pAk])
                mm(pA[0:64, :], ones_b[0:64, 0:64], Dgl[b2], False, False, ['ones_b'] + DGK[b2], [pAk])
                mm(pA[0:64, :], ident_b[0:64, 0:64], mask_b[:], False, True, ['ident_b', 'mask_b'], [pAk])
                tt('dve', seg[b2][:].rearrange("p (r c) -> p r c", r=8), pA[0:64, :].rearrange("p (r c) -> p r c", r=8),
                   amb[:, hs].unsqueeze(2).to_broadcast([64, 8, 64]), ALU.subtract, [pAk, 'amb'], [('seg', b2)])
                act(seg[b2][:], seg[b2][:], AF.Exp, [('seg', b2)], [('seg', b2)])
                tt('dve', mixT[b2][:].rearrange("p (r c) -> p r c", r=8), seg[b2][:].rearrange("p (r c) -> p r c", r=8),
                   cbT[:, g * 64:(g + 1) * 64].unsqueeze(1).to_broadcast([64, 8, 64]), ALU.mult, [('seg', b2), 'cbT'], [('mixT', b2)])
                yield
                py, pyk = bank()
                for r in range(8):
                    h = g * 8 + r
                    mm(py[0:64, r * 64:(r + 1) * 64], mixT[b2][:, r * 64:(r + 1) * 64], xtok[:, h * 64:(h + 1) * 64], True, False,
                       [('mixT', b2), 'xtok'], [pyk])
                    mm(py[0:64, r * 64:(r + 1) * 64], Dm[i][:, h * 64:(h + 1) * 64], xtok[:, h * 64:(h + 1) * 64], False, True,
                       [('Dm', 0), 'xtok'], [pyk])
                po, pok = bank()
                mm(po[0:64, :], xbcs[:, 20 + g, col:col + 64], st_b[:, g * 512:(g + 1) * 512], True, True, [('xb', 20 + g), ('stbf', g)], [pok])
                tt('dve', t1[b2][:].rearrange("p (r c) -> p r c", r=8), po[0:64, :].rearrange("p (r c) -> p r c", r=8),
                   E2[:, hs].unsqueeze(2).to_broadcast([64, 8, 64]), ALU.mult, [pok, 'E2'], [('t1', b2)])
                tt('dve', ytok[b2][:], t1[b2][:], py[0:64, :], ALU.add, [('t1', b2), pyk], [('ytok', b2)])
                yield
                pyt, pytk = bank()
                pyv = pyt[:].bitcast(BF16)
                for fc in range(4):
                    tr(pyv[:, fc * 64:(fc + 1) * 64], ytok[b2][:, fc * 128:(fc + 1) * 128], ident_b[0:64, 0:64], [('ytok', b2), 'ident_b'], [pytk])
                mz = mix[:, g * 4:(g + 1) * 4, col:col + 64]
                mzk = keys('mix', range(g * 4, g * 4 + 4))
                gyv = gy[b2][:].rearrange("p (a c) -> p a c", a=4)
                tt('dve', gyv, pyv[:, 0:256].rearrange("p (a c) -> p a c", a=4), mz, ALU.mult, [pytk] + mzk, [('gy', b2)])
                act(sqg[b2][:], gy[b2][:], AF.Square, [('gy', b2)], [('sqg', b2)])
                pss2, pss2k = bank()
                for fc in range(4):
                    mm(pss2[:, 0:64], ones_b[:], sqg[b2][:, fc * 64:(fc + 1) * 64], fc == 0, fc == 3, [('sqg', b2), 'ones_b'], [pss2k])
                rsqrt(rsg[b2][:], pss2[:, 0:64], 1.0 / 512, [pss2k, 'eps_c'], ('rsg', b2))
                tt('dve', gyv, gyv, rsg[b2][:].unsqueeze(1).to_broadcast([128, 4, 64]), ALU.mult, [('gy', b2), ('rsg', b2)], [('gy', b2)])
                tt('pool', mz, gyv, nssm[:, i * 16 + g * 4:i * 16 + g * 4 + 4].unsqueeze(2).to_broadcast([128, 4, 64]), ALU.mult,
                   [('gy', b2), 'nssm'], mzk)
                yield
                pst_, pstk = bank()
                mm(pst_[:, :], xtok[:, 2048 + g * 128:2048 + (g + 1) * 128], xw[:, g * 512:(g + 1) * 512], True, True, ['xtok', 'xw'], [pstk])
                sv = st_f[:, g * 512:(g + 1) * 512]
                tt('dve', sv.rearrange("p (r c) -> p r c", r=8), sv.rearrange("p (r c) -> p r c", r=8),
                   Edec[:, hs].unsqueeze(2).to_broadcast([128, 8, 64]), ALU.mult, [st_kf(g), 'Edec', ('stbf', g)], [st_kf(g)])
                tt('dve', sv, sv, pst_[:, :], ALU.add, [st_kf(g), pstk], [st_kf(g)])
                cp('act', st_b[:, g * 512:(g + 1) * 512], sv, [st_kf(g)], [('stbf', g)])
                yield
            yield

        def attn_gen(ci, c, quads, sidx):
            col = c['col']
            isS = c['seg'] == 's'
            if isS:
                KB, KBs, kkey, kskey = kS, kSs, 'kS', 'kSs'
                kc0 = 0; nblk = 3
                vblk = [(vS, j, 'vS') for j in range(3)]
            else:
                KB, KBs, kkey, kskey = kP[i], kPs[i], ('kP', i), ('kPs', i)
                nblk = min(3, c['gidx'] + 1)
                kc0 = 64 * c['cl'] + 64 * (3 - nblk)
                vblk = [(vP[i], c['cl'] + (3 - nblk) + j, ('vP', i)) for j in range(nblk)]
            nv = nblk * 64
            psWt = PW[sidx]; pwk = PWK[sidx]
            PQK = PKS[sidx]; PXK = PKS[sidx]
            for qd in quads:
                b2 = sidx
                kv = qd
                fk = kv // 2; khalf = kv % 2
                scv = psWt[0:64, :].rearrange("p (h c) -> p h c", h=4)
                for hh in range(4):
                    half = hh // 2; fq = qd * 2 + hh % 2
                    ksrc, ksk = (KB, kkey) if khalf == half else (KBs, kskey)
                    mm(psWt[0:64, hh * 256:hh * 256 + nv], qT[half * 64:(half + 1) * 64, fq, col:col + 64],
                       ksrc[half * 64:(half + 1) * 64, fk, kc0:kc0 + nv], True, True, [('q', fq), ksk], [pwk])
                ckpt(f'AT1_{qd}')
                A = asm[b2]
                ak = lambda n: ('asm', b2, n)
                T.add('dve', lambda e, o=A['mx'][:], i_=scv[:, :, 0:nv]: e.reduce_max(o, i_, AX.X), [pwk], [ak('mx')])
                snq = snk[:, i * 32 + qd * 4:i * 32 + qd * 4 + 4].rearrange("p (f h) -> p h f", h=2)
                hv = lambda t_: t_[:].rearrange("p (h f) -> p h f", f=2)
                tt('dve', hv(A['mx']), hv(A['mx']), snq, ALU.max, [ak('mx'), 'snk'], [ak('mx')])
                tsc('dve', A['negm'][:], A['mx'][:], -1.0, None, ALU.mult, None, [ak('mx')], [ak('negm')])
                Pv = Pq[b2][:].rearrange("p (h c) -> p h c", h=4)
                memset('dve', A['rs'][:], 0.0, [ak('rs')])
                for hh in range(4):
                    act(Pv[:, hh, 0:nv], scv[:, hh, 0:nv], AF.Exp, [pwk, ak('negm')], [*PQK, ak('rs')],
                        bias=A['negm'][:, hh:hh + 1], accum=A['rs'][:, hh:hh + 1])
                tt('dve', hv(A['es']), snq, hv(A['negm']), ALU.add, ['snk', ak('negm')], [ak('es')])
                act(A['es'][:], A['es'][:], AF.Exp, [ak('es')], [ak('es')])
                tt('dve', A['den'][:], A['rs'][:], A['es'][:], ALU.add, [ak('rs'), ak('es')], [ak('den')])
                T.add('dve', lambda e, o=A['rinv'][:], i_=A['den'][:]: e.reciprocal(o, i_), [ak('den')], [ak('rinv')])
                Pnv = Pn[b2][:].rearrange("p (h c) -> p h c", h=4)
                tt('dve', Pnv[:, :, 0:nv], Pv[:, :, 0:nv], A['rinv'][:].unsqueeze(2).to_broadcast([64, 4, nv]), ALU.mult,
                   [*PQK, ak('rinv')], [*PXK])
                ckpt(f'AT2_{qd}')
                yield
                ppt, pptk = bank()
                ppv = ppt[:].bitcast(BF16)
                for hh in range(4):
                    for j in range(nblk):
                        tr(ppv[0:64, (hh * 3 + j) * 64:(hh * 3 + j + 1) * 64], Pnv[:, hh, j * 64:(j + 1) * 64], ident_b[0:64, 0:64],
                           [*PXK, 'ident_b'], [pptk])
                cp('act', PTq[b2][:, 0:768].rearrange("p (h c) -> p h c", h=4)[:, :, 0:nv], ppv[0:64, 0:768].rearrange("p (h c) -> p h c", h=4)[:, :, 0:nv], [pptk], [*PXK])
                ckpt(f'AT3_{qd}')
                pov, povk = bank()
                for hh in range(4):
                    fql = hh % 2; half = hh // 2
                    for j in range(nblk):
                        vt, vb, vk = vblk[j]
                        mm(pov[half * 64:(half + 1) * 64, fql * 64:(fql + 1) * 64], vt[:, vb, kv * 64:(kv + 1) * 64],
                           PTq[b2][:, (hh * 3 + j) * 64:(hh * 3 + j + 1) * 64], j == 0, j == nblk - 1, [vk, *PXK], [povk])
                ckpt(f'AT4_{qd}')
                mg = mix[:, 16 + qd * 2:16 + qd * 2 + 2, col:col + 64]
                mgk = keys('mix', range(16 + qd * 2, 16 + qd * 2 + 2))
                tt('dve', mg, pov[:, 0:128].rearrange("p (a c) -> p a c", a=2), mg, ALU.mult, [povk] + mgk, mgk)
                yield
            yield

        pb2 = projB2()
        active = [pb2]
        prev_attn = []

        def drain(required):
            req = list(required)
            for g_ in req:
                if g_ not in active:
                    active.append(g_)
            while any(g_ in active for g_ in req):
                for g_ in list(active):
                    try:
                        next(g_)
                    except StopIteration:
                        active.remove(g_)

        memset('dve', modt[:, 0:1], 0.0, [('modt', 0)] + SCRK + SETB)
        for ci, c in enumerate(chunks):
            drain([ssd_pre(ci, c)])
            if ci == 0:
                drain([ssd_grp(ci, c, [0, 1, 2, 3], 0)])
            else:
                drain([ssd_grp(ci, c, [0, 1], 0), ssd_grp(ci, c, [2, 3], 1)])
            ckpt(f'S{l}_{ti}_{ci}')
            if ci == 0:
                drain([pb2])
            drain([g_ for g_ in prev_attn if g_ in active])
            prev_attn[:] = [attn_gen(ci, c, range(0, 4), 0), attn_gen(ci, c, range(4, 8), 1)]
            active.extend(prev_attn)
        drain(list(active))
        memset('dve', modt[:, 0:1], 0.0, [('modt', 0)] + SCRK + SETB)
        dbg(f'mix{l}_{ti}', mix[:, :, 0:Tt], keys('mix', range(32)), [128, 32, Tt])
        ckpt(f'C{l}_{ti}')
        if has_s:
            dma('sp', o_nsss[i], stS[:], STSK, [('onsss', i)], 'outs')
        if last:
            dma('sp', o_nssp[i], stT[i][:], [('stT', i, g_) for g_ in range(4)], [('onssp', i)], 'outs')
        else:
            cp('pool', kP[i][:, :, 0:128], kP[i][:, :, TP:TP + 128], [('kP', i)], [('kP', i)])
            cp('pool', kPs[i][:, :, 0:128], kPs[i][:, :, TP:TP + 128], [('kPs', i)], [('kPs', i)])
            cp('pool', vP[i][:, 0:2, :], vP[i][:, NCP:NCP + 2, :], [('vP', i)], [('vP', i)])
        phase_D(l, Tt, segs, last, ti)

    def odd_layer(l, ti, Tt, segs, chunks):
        i = l // 2
        last = (ti == NT - 1)
        has_s = (ti == 0)
        L = Tt + (2 if has_s else 0)
        if ti == 0:
            memset('dve', vcar[i][:], 0.0, [('vcar', i)])
            dma('sp', sccS[:], d_scc[i], [], ['sccS'], 'sin')
        phase_A(l, Tt, segs)
        for j in range(32):
            wt, wk = wnext()
            pu, puk = proj_fm(wt, wk, 0, 256, Tt, lambda kc: hT[:, kc, 0:Tt], lambda kc: [('h', kc)], KC)
            pgc, pgck = proj_fm(wt, wk, 128, 256, Tt, lambda kc: hT[:, kc, 0:Tt], lambda kc: [('h', kc)], KC)
            wt2, wk2 = wnext()
            pgb, pgbk = proj_fm(wt2, wk2, 0, 256, Tt, lambda kc: hT[:, kc, 0:Tt], lambda kc: [('h', kc)], KC)
            pg, pgk = proj_fm(wt2, wk2, 128, 256, Tt, lambda kc: hT[:, kc, 0:Tt], lambda kc: [('h', kc)], KC)
            su = stg[(j % 2) * 2]; suk = ('tmpA', 0) if j % 2 == 0 else ('stg', 2)
            sg_ = stg[(j % 2) * 2 + 1]; sgk = ('tmpA', 1) if j % 2 == 0 else ('stg', 3)
            cs = cst[0]; ck = ('cst', 0); ca = cacc[0]; cak = ('cacc', 0)
            cp('act', su[:, 0:Tt], pu[:, 0:Tt], [puk], [suk])
            tt('dve', cs[:, 2:2 + TP], pgc[:, 0:TP], su[:, 0:TP], ALU.mult, [pgck, suk], [ck])
            cp('dve', cs[:, 0:2], vcar[i][:, j * 2:j * 2 + 2], [('vcar', i)], [ck])
            if has_s:
                tt('dve', cs[:, 4 + TP:4 + TP + CH], pgc[:, TP:TP + CH], su[:, TP:TP + CH], ALU.mult, [pgck, suk], [ck])
                cp('dve', cs[:, 2 + TP:4 + TP], sccS[:, j * 2:j * 2 + 2], ['sccS'], [ck])
                cp('dve', sccS[:, j * 2:j * 2 + 2], cs[:, 2 + TP + CH:4 + TP + CH], [ck], ['sccS'])
            cp('dve', vcar[i][:, j * 2:j * 2 + 2], cs[:, TP:TP + 2], [ck], [('vcar', i)])
            wv = ccw[:, (i * 32 + j) * 3:(i * 32 + j) * 3 + 3]
            tsc('dve', ca[:, 0:L], cs[:, 0:L], wv[:, 0:1], None, ALU.mult, None, [ck, 'ccw'], [cak])
            for tap in (1, 2):
                stt('dve', ca[:, 0:L], cs[:, tap:tap + L], wv[:, tap:tap + 1], ca[:, 0:L], ALU.mult, ALU.add, [ck, cak, 'ccw'], [cak])
            act(sg_[:, 0:Tt], pg[:, 0:Tt], AF.Silu, [pgk], [sgk])
            tt('dve', ca[:, 0:TP], ca[:, 0:TP], pgb[:, 0:TP], ALU.mult, [cak, pgbk], [cak])
            tt('pool', mix[:, j, 0:TP], ca[:, 0:TP], sg_[:, 0:TP], ALU.mult, [cak, sgk], [('mix', j)])
            if has_s:
                tt('dve', ca[:, TP + 2:TP + 2 + CH], ca[:, TP + 2:TP + 2 + CH], pgb[:, TP:TP + CH], ALU.mult, [cak, pgbk], [cak])
                tt('pool', mix[:, j, TP:TP + CH], ca[:, TP + 2:TP + 2 + CH], sg_[:, TP:TP + CH], ALU.mult, [cak, sgk], [('mix', j)])
        if last:
            dma('sp', o_nccp[i], vcar[i][:], [('vcar', i)], [('onccp', i)], 'outs')
        if has_s:
            dma('sp', o_nccs[i], sccS[:], ['sccS'], [('onccs', i)], 'outs')
        dbg(f'mix{l}_{ti}', mix[:, :, 0:Tt], keys('mix', range(32)), [128, 32, Tt])
        phase_D(l, Tt, segs, last, ti)

    def main_schedule():
        for ti in range(NT):
            has_s = (ti == 0)
            Tt = TP + (CH if has_s else 0)
            segs = [(0, TP, 0)] + ([(TP, TP + CH, 1)] if has_s else [])
            chunks = [dict(col=cl * CH, seg='p', cl=cl, gidx=ti * NCP + cl) for cl in range(NCP)]
            if has_s:
                chunks.append(dict(col=TP, seg='s', cl=0, gidx=0))
            dma('sp', xT[:, :, 0:TP], d_xp[:, :, ti * TP:(ti + 1) * TP], [], XK, 'xin')
            if has_s:
                dma('sp', xT[:, :, TP:TP + CH], d_xs, [], XK, 'xin')
            for l in range(DEPTH):
                if l % 2 == 0:
                    even_layer(l, ti, Tt, segs, chunks)
                else:
                    odd_layer(l, ti, Tt, segs, chunks)
                dbg(f'x{l}_{ti}', xT[:, :, 0:Tt], XK, [128, KC, Tt])
            dma('sp', o_yp[:, :, ti * TP:(ti + 1) * TP], xT[:, :, 0:TP], XK, [('oyp', ti)], 'xout')
            if has_s:
                dma('sp', o_ys, xT[:, :, TP:TP + CH], XK, ['oys'], 'xout')


    try:
        ckpt('prologue')
        main_schedule()
    except _Stop:
        pass

    T.emit(nc, es)
    es.close()
    return nc, sorted(dbg_out.keys())


def _fm(a, nch):
    return np.ascontiguousarray(a.reshape(nch, 128, -1).transpose(1, 0, 2))


def _wgroups(w, col_lists, kchunks):
    out = []
    for cols in col_lists:
        cols = np.asarray(cols)
        blk = w[:, np.maximum(cols, 0)]
        if (cols < 0).any():
            blk = blk.copy(); blk[:, cols < 0] = 0.0
        out.append(blk.reshape(kchunks, 128, len(cols)).transpose(1, 0, 2).reshape(128, kchunks * len(cols)))
    return np.ascontiguousarray(np.stack(out))


def prep_shared(inp, cfg):
    DEPTH = cfg['DEPTH']
    NE = (DEPTH + 1) // 2; NO = DEPTH // 2
    f32 = np.float32
    sh = {}
    sh['wada'] = np.ascontiguousarray(inp['w_ada'].reshape(DEPTH, KC, 128, 6144).transpose(0, 2, 1, 3))
    sh['bada'] = np.ascontiguousarray(inp['b_ada'].reshape(DEPTH, 48, 128).transpose(2, 0, 1).reshape(128, DEPTH * 48))
    sh['npre'] = np.ascontiguousarray(inp['norm_pre'].reshape(DEPTH, 16, 128).transpose(2, 0, 1).reshape(128, DEPTH * 16))
    sh['npost'] = np.ascontiguousarray(inp['norm_post'].reshape(DEPTH, 16, 128).transpose(2, 0, 1).reshape(128, DEPTH * 16))
    sh['c_ident'] = np.eye(128, dtype=f32)
    sh['c_triu'] = np.triu(np.ones((64, 64), f32))
    m = np.where(np.arange(64)[None, :] >= np.arange(64)[:, None], 0.0, -1e30).astype(f32)
    sh['c_mask'] = np.ascontiguousarray(np.tile(m, (1, 8)))
    for i in range(NE):
        w = inp['w_in_even'][i]
        gl = []
        for (typ, gi) in EV_GROUPS:
            c0 = EV_COLS[typ][0] if typ in EV_COLS else 0
            if typ == 'ks':
                c0 = EV_COLS['k'][0]
                cols = []
                for jj in range(2):
                    b0 = c0 + (gi * 2 + jj) * 128
                    cols += list(range(b0 + 64, b0 + 128)) + list(range(b0, b0 + 64))
            elif typ == 'dt':
                cols = list(range(c0, c0 + 32)) + [-1] * 224
            else:
                cols = list(range(c0 + gi * 256, c0 + (gi + 1) * 256))
            gl.append(cols)
        sh[f'wie{i}'] = _wgroups(w, gl, 16)
        sh[f'woe{i}'] = _wgroups(inp['w_out_even'][i], [list(range(g * 128, (g + 1) * 128)) for g in range(16)], 32)
    for i in range(NO):
        w = inp['w_in_odd'][i]
        gl = []
        for j in range(32):
            gl.append(list(range(j * 128, (j + 1) * 128)) + list(range(8192 + j * 128, 8192 + (j + 1) * 128)))
            gl.append(list(range(4096 + j * 128, 4096 + (j + 1) * 128)) + list(range(12288 + j * 128, 12288 + (j + 1) * 128)))
        sh[f'wio{i}'] = _wgroups(w, gl, 16)
        sh[f'woo{i}'] = _wgroups(inp['w_out_odd'][i], [list(range(g * 128, (g + 1) * 128)) for g in range(16)], 32)
    sh['caw'] = np.ascontiguousarray(inp['conv_a_w'].reshape(NE, 4, 24, 128).transpose(3, 0, 2, 1).reshape(128, NE * 96))
    sh['cab'] = np.ascontiguousarray(inp['conv_a_b'].reshape(NE, 24, 128).transpose(2, 0, 1).reshape(128, NE * 24))
    sh['nssm'] = np.ascontiguousarray(inp['norm_ssm'].reshape(NE, 16, 128).transpose(2, 0, 1).reshape(128, NE * 16))
    for nm, key in [('dtb', 'dt_bias'), ('alog', 'a_log'), ('dsk', 'd_skip'), ('snk', 'sinks')]:
        sh[nm] = np.ascontiguousarray(np.broadcast_to(inp[key].reshape(1, NE * 32), (64, NE * 32)))
    if NO > 0:
        sh['ccw'] = np.ascontiguousarray(inp['conv_c_w'].reshape(NO, 3, 32, 128).transpose(3, 0, 2, 1).reshape(128, NO * 96))
    else:
        sh['ccw'] = np.zeros((128, 96), f32)
    return sh


def prep_core(inp, b, cfg):
    DEPTH = cfg['DEPTH']
    NE = (DEPTH + 1) // 2; NO = DEPTH // 2
    c = {}
    c['xp'] = _fm(inp['x_prompt'][b].T, 16)
    c['xs'] = _fm(inp['x_sample'][b].T, 16)
    cc = np.stack([inp['c_prompt'][b], inp['c_sample'][b]], axis=1)
    c['cT'] = _fm(cc, 16).reshape(128, 32)
    c['ckT'] = np.ascontiguousarray(inp['cache_k'][:, b].reshape(NE, 128, 4, 128).transpose(0, 3, 2, 1))
    c['cv'] = np.ascontiguousarray(inp['cache_v'][:, b].reshape(NE, 2, 64, 512))
    c['sca'] = np.ascontiguousarray(inp['state_conv_a'][:, b].reshape(NE, 3, 24, 128).transpose(0, 3, 2, 1).reshape(NE, 128, 72))
    c['sst'] = np.ascontiguousarray(inp['state_ssm'][:, b].reshape(NE, 2048, 128).transpose(0, 2, 1))
    if NO > 0:
        c['scc'] = np.ascontiguousarray(inp['state_conv_c'][:, b].reshape(NO, 2, 32, 128).transpose(0, 3, 2, 1).reshape(NO, 128, 64))
    else:
        c['scc'] = np.zeros((1, 128, 64), np.float32)
    return c


def assemble(results, cfg, nb):
    DEPTH = cfg['DEPTH']; SEQ = cfg['SEQ']
    NE = (DEPTH + 1) // 2; NO = DEPTH // 2

    def st(f):
        return np.stack([f(r) for r in results], axis=0)

    def unfm(a):
        return a.transpose(2, 1, 0).reshape(a.shape[2], -1)
    yp = st(lambda r: unfm(r['yp']))
    ys = st(lambda r: unfm(r['ys']))

    def kfix(a):
        return a.transpose(0, 3, 2, 1).reshape(NE, 128, 8, 64)

    def cafix(a):
        return a.reshape(NE, 128, 24, 3).transpose(0, 3, 2, 1).reshape(NE, 3, 3072)

    def ssfix(a):
        return a.transpose(0, 2, 1).reshape(NE, 32, 64, 128)

    def ccfix(a):
        return a[:NO].reshape(NO, 128, 32, 2).transpose(0, 3, 2, 1).reshape(NO, 2, 4096)
    outs = [yp, ys]
    for sfx in ['p', 's']:
        outs.append(np.stack([kfix(r['nk' + sfx]) for r in results], axis=1))
        outs.append(np.stack([r['nv' + sfx].reshape(NE, 128, 8, 64) for r in results], axis=1))
        outs.append(np.stack([cafix(r['nca' + sfx]) for r in results], axis=1))
        outs.append(np.stack([ssfix(r['nss' + sfx]) for r in results], axis=1))
        outs.append(np.stack([ccfix(r['ncc' + sfx]) for r in results], axis=1))
    return tuple(np.ascontiguousarray(o.astype(np.float32)) for o in outs)


_CACHE = {}


def kernel(**inputs):
    inp = {k: np.asarray(v) for k, v in inputs.items()}
    B, SEQ, _ = inp['x_prompt'].shape
    DEPTH = inp['w_ada'].shape[0]
    cfg = dict(SEQ=SEQ, DEPTH=DEPTH, TP=256)
    key = (SEQ, DEPTH)
    if key not in _CACHE:
        _CACHE[key] = build(cfg)[0]
    nc = _CACHE[key]
    sh = prep_shared(inp, cfg)
    in_maps = []
    for b in range(B):
        m = dict(sh)
        m.update(prep_core(inp, b, cfg))
        in_maps.append(m)
    res = run_bass_kernel_spmd(nc, in_maps, core_ids=list(range(B)))
    return assemble(res.results, cfg, B)
```

```python
import numpy as np
from contextlib import ExitStack
import concourse.bass as bass
import concourse.mybir as mybir
from concourse.bass_utils import run_bass_kernel_spmd

F32 = mybir.dt.float32
BF16 = mybir.dt.bfloat16
AF = mybir.ActivationFunctionType
ALU = mybir.AluOpType
AX = mybir.AxisListType

D = 2048
KC = 16
CH = 64
EPS = 1e-6
NS = 3
SAME_SYNC = True

EV_COLS = dict(z=(0, 2048), xbc=(2048, 5120), dt=(5120, 5152), q=(5152, 7200), k=(7200, 7712),
               v=(7712, 8224), g=(8224, 10272))
EV_GROUPS = ([('dt', 0)] + [('xbc', i) for i in range(12)] + [('z', i) for i in range(8)] +
             [('k', i) for i in range(2)] + [('ks', i) for i in range(2)] + [('v', i) for i in range(2)] +
             [('q', i) for i in range(8)] + [('g', i) for i in range(8)])


class Op:
    __slots__ = ('eng', 'fn', 'waits', 'inc', 'chan', 'count')

    def __init__(self, eng, fn, chan):
        self.eng = eng; self.fn = fn; self.chan = chan; self.waits = set(); self.inc = False; self.count = 0


class Tracker:
    def __init__(self):
        self.ops = []
        self.lastw = {}
        self.readers = {}
        self.chan_last = {}

    def add(self, eng, fn, reads=(), writes=(), chan=None):
        idx = len(self.ops)
        op = Op(eng, fn, chan)
        deps = set()
        for k in reads:
            w = self.lastw.get(k)
            if w is not None:
                deps.add(w)
            if k in ('psW', 'psW2') or (isinstance(k, tuple) and k[0] == 'ps'):
                for st_, r in self.readers.get(k, {}).items():
                    if st_ != (('c', chan) if chan else ('e', eng)):
                        deps.add(r)
        for k in writes:
            w = self.lastw.get(k)
            if w is not None:
                deps.add(w)
            for r in self.readers.get(k, {}).values():
                deps.add(r)
        stream = ('c', chan) if chan else ('e', eng)
        for k in reads:
            self.readers.setdefault(k, {})[stream] = idx
        for k in writes:
            self.lastw[k] = idx
            self.readers[k] = {}
        for d in deps:
            p = self.ops[d]
            if p.chan is not None:
                op.waits.add(self.chan_last[p.chan])
            else:
                if p.eng == eng and chan is None and (eng == 'pe' or not SAME_SYNC):
                    continue
                p.inc = True
                op.waits.add(d)
        if chan:
            prev = self.chan_last.get(chan)
            if prev is not None:
                op.waits.add(prev)
            self.chan_last[chan] = idx
        self.ops.append(op)

    def emit(self, nc, es):
        engines = {'pe': nc.tensor, 'act': nc.scalar, 'dve': nc.vector, 'pool': nc.gpsimd, 'sp': nc.sync}
        ecount = {}
        ccount = {}
        for op in self.ops:
            if op.chan:
                ccount[op.chan] = ccount.get(op.chan, 0) + 16
                op.count = ccount[op.chan]
            elif op.inc:
                ecount[op.eng] = ecount.get(op.eng, 0) + 1
                op.count = ecount[op.eng]
        sems = {}
        for e in ecount:
            sems[('e', e)] = es.enter_context(nc.semaphore('se_' + e))
        for c in ccount:
            sems[('c', c)] = es.enter_context(nc.semaphore('sc_' + c))
        waited = {e: {} for e in engines}
        for op in self.ops:
            E = engines[op.eng]
            need = {}
            for d in op.waits:
                p = self.ops[d]
                key = ('c', p.chan) if p.chan else ('e', p.eng)
                need[key] = max(need.get(key, 0), p.count)
            wd = waited[op.eng]
            for key, val in need.items():
                if wd.get(key, 0) < val:
                    E.wait_ge(sems[key], val)
                    wd[key] = val
            ins = op.fn(E)
            if op.chan:
                ins.then_inc(sems[('c', op.chan)], 16)
            elif op.inc:
                ins.then_inc(sems[('e', op.eng)], 1)
        for c, val in ccount.items():
            nc.sync.wait_ge(sems[('c', c)], val)


class _Stop(Exception):
    pass


def build(cfg, dbg_names=()):
    SEQ = cfg['SEQ']; DEPTH = cfg['DEPTH']; TP = cfg['TP']
    NT = SEQ // TP
    TW = TP + CH
    NE = (DEPTH + 1) // 2
    NO = DEPTH // 2
    NCP = TP // CH
    nc = bass.Bass("TRN2", target_bir_lowering=False)
    es = ExitStack()
    T = Tracker()
    dbg_out = {}

    def din(name, shape):
        return nc.dram_tensor(name, list(shape), F32, kind="ExternalInput").ap()

    def dout(name, shape):
        return nc.dram_tensor(name, list(shape), F32, kind="ExternalOutput").ap()

    d_xp = din('xp', [128, KC, SEQ]); d_xs = din('xs', [128, KC, CH]); d_cT = din('cT', [128, KC * 2])
    d_wada = din('wada', [DEPTH, 128, KC, 6144])
    d_bada = din('bada', [128, DEPTH * 48]); d_npre = din('npre', [128, DEPTH * 16]); d_npost = din('npost', [128, DEPTH * 16])
    d_ident = din('c_ident', [128, 128]); d_triu = din('c_triu', [64, 64]); d_mask = din('c_mask', [64, 512])
    d_wie = [din(f'wie{i}', [len(EV_GROUPS), 128, 4096]) for i in range(NE)]
    d_woe = [din(f'woe{i}', [16, 128, 4096]) for i in range(NE)]
    d_wio = [din(f'wio{i}', [64, 128, 4096]) for i in range(NO)]
    d_woo = [din(f'woo{i}', [16, 128, 4096]) for i in range(NO)]
    d_caw = din('caw', [128, NE * 24 * 4]); d_cab = din('cab', [128, NE * 24])
    d_dtb = din('dtb', [64, NE * 32]); d_alog = din('alog', [64, NE * 32]); d_dsk = din('dsk', [64, NE * 32])
    d_snk = din('snk', [64, NE * 32]); d_nssm = din('nssm', [128, NE * 16])
    d_ccw = din('ccw', [128, max(NO, 1) * 32 * 3])
    d_ckT = din('ckT', [NE, 128, 4, 128]); d_cv = din('cv', [NE, 2, 64, 512])
    d_sca = din('sca', [NE, 128, 24 * 3]); d_sst = din('sst', [NE, 128, 2048]); d_scc = din('scc', [max(NO, 1), 128, 32 * 2])
    o_yp = dout('yp', [128, KC, SEQ]); o_ys = dout('ys', [128, KC, CH])
    o_nkp = dout('nkp', [NE, 128, 4, 128]); o_nvp = dout('nvp', [NE, 128, 512]); o_ncap = dout('ncap', [NE, 128, 72])
    o_nssp = dout('nssp', [NE, 128, 2048]); o_nccp = dout('nccp', [max(NO, 1), 128, 64])
    o_nks = dout('nks', [NE, 128, 4, 128]); o_nvs = dout('nvs', [NE, 128, 512]); o_ncas = dout('ncas', [NE, 128, 72])
    o_nsss = dout('nsss', [NE, 128, 2048]); o_nccs = dout('nccs', [max(NO, 1), 128, 64])

    def sb(name, shape, dt=F32):
        return es.enter_context(nc.sbuf_tensor('s_' + name, list(shape), dt))

    def ps(name, shape, dt=F32):
        return es.enter_context(nc.psum_tensor(name, list(shape), dt))

    xT = sb('xT', [128, KC, TW])
    hT = sb('hT', [128, KC, TW], BF16)
    hflat = hT[:].rearrange("p a b -> p (a b)")
    arena2 = sb('arena2', [128, 40 * TW], BF16)
    xbcs = arena2[:, 0:24 * TW].rearrange("p (a b) -> p a b", a=24)
    qT = arena2[:, 24 * TW:40 * TW].rearrange("p (a b) -> p a b", a=16)
    obuf = arena2[:, 0:32 * TW].bitcast(F32).rearrange("p (a b) -> p a b", a=16)
    mix = sb('mix', [128, 32, TW], BF16)
    sq = mix[:, 0:16, :]
    wbuf = [sb(f'wbuf{i}', [128, 4096], BF16) for i in range(NS)]
    C8 = TW + 8
    scr = sb('scr', [128, 5 * TW + 2 * C8 + 256 + TW // 2])
    tmpA = [scr[:, 0:TW], scr[:, TW:2 * TW]]; rstd = scr[:, 2 * TW:3 * TW]
    ident_f = sb('ident_f', [128, 128]); ident_b = sb('ident_b', [128, 128], BF16)
    ones_b = sb('ones_b', [128, 128], BF16); ones_f = sb('ones_f', [64, 128])
    triu_f = sb('triu_f', [64, 64]); mask_b = sb('mask_b', [64, 512], BF16)
    eps_c = sb('eps_c', [128, 2])
    cT = sb('cT', [128, KC * 2]); scT = sb('scT', [128, KC * 2]); scb = sb('scb', [128, KC * 2], BF16)
    bada = sb('bada', [128, DEPTH * 48]); npre = sb('npre', [128, DEPTH * 16]); npost = sb('npost', [128, DEPTH * 16])
    modt = sb('modt', [128, DEPTH * 96])
    modc = sb('modc', [128, DEPTH * 2 * 3 * 16])
    caw = sb('caw', [128, NE * 96]); cab = sb('cab', [128, NE * 24]); nssm = sb('nssm', [128, NE * 16])
    dtb = sb('dtb', [64, NE * 32]); alog = sb('alog', [64, NE * 32]); dsk = sb('dsk', [64, NE * 32]); snk = sb('snk', [64, NE * 32])
    aneg = sb('aneg', [64, NE * 32])
    Dm = [sb('Dm0', [64, 32 * 64], BF16)] * NE
    ccw = sb('ccw', [128, max(NO, 1) * 96])
    kP = [sb(f'kP{i}', [128, 4, 128 + TP], BF16) for i in range(NE)]
    kPs = [sb(f'kPs{i}', [128, 4, 128 + TP], BF16) for i in range(NE)]
    vP = [sb(f'vP{i}', [64, 2 + NCP, 512], BF16) for i in range(NE)]
    stT = [sb(f'stT{i}', [128, 2048]) for i in range(NE)]
    ccar = [sb(f'ccar{i}', [128, 24 * 3]) for i in range(NE)]
    vcar = [sb(f'vcar{i}', [128, 32 * 2]) for i in range(NO)]
    kS = sb('kS', [128, 4, 192], BF16); kSs = sb('kSs', [128, 4, 192], BF16)
    vS = sb('vS', [64, 3, 512], BF16)
    stS = sb('stS', [128, 2048]); scaS = sb('scaS', [128, 72]); sccS = sb('sccS', [128, 64])
    stbf = sb('stbf', [128, 2048], BF16)
    o_ = 3 * TW
    cst = [scr[:, o_:o_ + C8], scr[:, 0:C8]]
    cacc = [scr[:, o_ + C8:o_ + 2 * C8], scr[:, C8:2 * C8]]
    o_ += 2 * C8
    stg = [tmpA[0], tmpA[1], scr[:, o_:o_ + TW], scr[:, o_ + TW:o_ + 2 * TW]]
    o_ += 2 * TW
    kvout = [scr[:, o_:o_ + 256]] * 2
    sqo = [scr[:, o_ + 256:o_ + 256 + TW // 2].bitcast(BF16)] * 2
    dtall = sb('dtall', [64, (NCP + 1) * 32]); spx = sb('spx', [64, (NCP + 1) * 32])
    sm = {n: sb('sm_' + n, [128, 32]) for n in ['dta', 'acum', 'ldt', 'amb', 'E2', 'wte', 'tw', 'Edec']}
    xtok = sb('xtok', [64, 2560], BF16); xw = sb('xw', [64, 2048], BF16)
    cbT = sb('cbT', [64, 256])
    Dgh = [sb('Dgh0', [64, 512], BF16)[:], hflat[0:64, 11 * TW:11 * TW + 512]]
    Dgl = [sb('Dgl0', [64, 512], BF16)[:], hflat[0:64, 11 * TW + 512:11 * TW + 1024]]
    DGK = [[('Dg', 0)], [('h', k_) for k_ in range(11, 15)]]
    ahl = sb('ahl', [64, 64], BF16)
    seg = [sb('seg0', [64, 512])[:], scr[0:64, 0:512]]
    mixT = [sb('mixT0', [64, 512], BF16)[:], scr[0:64, 1024:1280].bitcast(BF16)]
    t1 = [sb('t10', [64, 512])[:], scr[0:64, 512:1024]]
    ytok = [sb('ytok0', [64, 512], BF16)[:], scr[0:64, 1280:1536].bitcast(BF16)]
    gy = [sb('gy0', [128, 256])[:], scr[:, 1536:1792]]
    sqg = [sb('sqg0', [128, 256], BF16)[:], scr[:, 1792:1920].bitcast(BF16)]
    rsg = [sb('rsg0', [128, 64])[:], scr[:, 1920:1984]]
    Pn = [hflat[0:64, 0:768], hflat[0:64, 5 * TW:5 * TW + 768]]
    Pq = Pn
    PTq = [hflat[0:64, 768:1536], hflat[0:64, 5 * TW + 768:5 * TW + 1536]]
    asm = [{n: sb(f'asm{i}_' + n, [64, 4]) for n in ['mx', 'negm', 'rs', 'es', 'den', 'rinv']} for i in range(2)]
    NSB = 4
    psS = [ps(f'psS{i}', [128, 512]) for i in range(NSB)]
    psW = ps('psW', [128, 1024])
    psW2 = ps('psW2', [128, 1024])
    PW = [psW, psW2]; PWK = ['psW', 'psW2']
    ps_ctr = [0]

    def bank():
        i = ps_ctr[0] % NSB
        ps_ctr[0] += 1
        return psS[i], ('ps', i)

    def mm(out, lhsT, rhs, start, stop, reads, writes):
        T.add('pe', lambda e: e.matmul(out, lhsT, rhs, start=start, stop=stop), reads, writes)

    def tr(out, in_, ident, reads, writes):
        T.add('pe', lambda e: e.transpose(out, in_, ident), reads, writes)

    def act(out, in_, func, reads, writes, bias=None, scale=None, accum=None, eng='act'):
        kw = {}
        if bias is not None: kw['bias'] = bias
        if scale is not None: kw['scale'] = scale
        if accum is not None: kw['accum_out'] = accum
        T.add('act', lambda e: e.activation(out=out, in_=in_, func=func, **kw), reads, writes)

    def tt(eng, out, in0, in1, op, reads, writes):
        T.add(eng, lambda e: e.tensor_tensor(out, in0, in1, op), reads, writes)

    def tsc(eng, out, in0, s1, s2, op0, op1, reads, writes):
        if op1 is None:
            T.add(eng, lambda e: e.tensor_scalar(out, in0, s1, None, op0), reads, writes)
        else:
            T.add(eng, lambda e: e.tensor_scalar(out, in0, s1, s2, op0, op1), reads, writes)

    def stt(eng, out, in0, scalar, in1, op0, op1, reads, writes):
        T.add(eng, lambda e: e.scalar_tensor_tensor(out, in0, scalar, in1, op0, op1), reads, writes)

    def rsqrt(out, in_, scale, reads, wkey):
        act(out, in_, AF.Ln, reads, [wkey], bias=eps_c[0:out.shape[0], 0:1], scale=scale)
        act(out, out, AF.Exp, [wkey], [wkey], scale=-0.5)

    def cp(eng, out, in_, reads, writes):
        if eng == 'act':
            T.add('act', lambda e: e.copy(out, in_), reads, writes)
        else:
            T.add(eng, lambda e: e.tensor_copy(out, in_), reads, writes)

    def memset(eng, ap, val, writes):
        T.add(eng, lambda e: e.memset(ap, val), (), writes)

    def dma(eng, out, in_, reads, writes, chan):
        T.add(eng, lambda e: e.dma_start(out=out, in_=in_), reads, writes, chan=chan)

    def dbg(name, ap, reads, shape):
        if name not in dbg_names:
            return
        if name not in dbg_out:
            dbg_out[name] = dout('dbg_' + name, shape)
        dma('pool', dbg_out[name], ap, reads, [('dbgo', name)], 'dbg')

    def ckpt(name):
        if cfg.get('stop') == name:
            raise _Stop()

    def keys(name, rng):
        return [(name, i) for i in rng]

    XK = keys('x', range(KC)); HK = keys('h', range(KC))
    PKS = [keys('h', range(0, 5)), keys('h', range(5, 10))]
    STSK = [('stS', g_) for g_ in range(4)]; STBK = [('stbf', g_) for g_ in range(4)]
    SCRK = [('tmpA', 0), ('tmpA', 1), 'rstd', ('cst', 0), ('cacc', 0), ('cst', 1), ('cacc', 1), ('stg', 2), ('stg', 3), ('kvout', 0), ('sqo', 0)]
    SETB = [(n_, 1) for n_ in ['seg', 'mixT', 't1', 'ytok', 'gy', 'sqg', 'rsg']]

    dma('sp', ident_f[:], d_ident, [], ['ident_f'], 'cst')
    dma('pool', ident_b[:], d_ident, [], ['ident_b'], 'cstb')
    dma('sp', triu_f[:], d_triu, [], ['triu_f'], 'cst')
    dma('pool', mask_b[:], d_mask, [], ['mask_b'], 'cstb')
    memset('dve', ones_b[:], 1.0, ['ones_b'])
    memset('dve', ones_f[:], 1.0, ['ones_f'])
    memset('dve', eps_c[:, 0:1], EPS, ['eps_c'])
    memset('dve', eps_c[:, 1:2], 1.0, ['eps_c'])
    for (t_, d_, k_) in [(cT, d_cT, 'cT'), (bada, d_bada, 'bada'), (npre, d_npre, 'npre'), (npost, d_npost, 'npost'),
                         (caw, d_caw, 'caw'), (cab, d_cab, 'cab'), (nssm, d_nssm, 'nssm'), (dtb, d_dtb, 'dtb'),
                         (alog, d_alog, 'alog'), (dsk, d_dsk, 'dsk'), (snk, d_snk, 'snk'), (ccw, d_ccw, 'ccw')]:
        dma('sp', t_[:], d_, [], [k_], 'cst')
    act(scT[:], cT[:], AF.Silu, ['cT'], ['scT'])
    act(aneg[:], alog[:], AF.Exp, ['alog'], ['aneg'])
    tsc('dve', aneg[:], aneg[:], -1.0, None, ALU.mult, None, ['aneg'], ['aneg'])
    cp('dve', scb[:], scT[:], ['scT'], ['scb'])
    nb_ctr = 0
    for l in range(DEPTH):
        pst, pk = psW[:, 512:1024], 'psW'
        for nb in range(24):
            s_ = nb_ctr % NS
            wt = wbuf[s_]; wk = ('w', s_)
            dma('pool', wt[:].rearrange("p (a b) -> p a b", a=KC), d_wada[l, :, :, nb * 256:(nb + 1) * 256], [], [wk], f'w{s_}')
            nb_ctr += 1
            pr, prk = bank()
            for kc in range(KC):
                mm(pr[0:2, 0:256], scb[:, kc * 2:kc * 2 + 2], wt[:, kc * 256:(kc + 1) * 256], kc == 0, kc == KC - 1, [wk, 'scb'], [prk])
            mr = tmpA[nb % 2]; mrk = ('tmpA', nb % 2)
            cp('dve', mr[0:2, 0:256], pr[0:2, 0:256], [prk], [mrk])
            for jj in range(2):
                j = nb * 2 + jj
                tr(pst[:, j * 2:j * 2 + 2], mr[0:2, jj * 128:(jj + 1) * 128], ident_f[0:2, 0:2], [mrk, 'ident_f'], [pk])
        tt('dve', modt[:, l * 96:(l + 1) * 96].rearrange("p (j w) -> p j w", w=2),
           pst[:, 0:96].rearrange("p (j w) -> p j w", w=2),
           bada[:, l * 48:(l + 1) * 48].unsqueeze(2).to_broadcast([128, 48, 2]), ALU.add,
           [pk, 'bada'], [('modt', l)])
        for w in range(2):
            base = ((l * 2 + w) * 3) * 16
            mv = modt[:, l * 96:(l + 1) * 96].rearrange("p (j w) -> p j w", w=2)
            stt('dve', modc[:, base:base + 16], mv[:, 16:32, w], 1.0, npre[:, l * 16:(l + 1) * 16], ALU.add, ALU.mult,
                [('modt', l), 'npre'], [('modc', l)])
            cp('dve', modc[:, base + 16:base + 32], mv[:, 0:16, w], [('modt', l)], [('modc', l)])
            tt('dve', modc[:, base + 32:base + 48], mv[:, 32:48, w], npost[:, l * 16:(l + 1) * 16], ALU.mult,
               [('modt', l), 'npost'], [('modc', l)])


    def mc_unused():
        pass

    def mc(l, w, kind, kc):
        o = ((l * 2 + w) * 3 + kind) * 16 + kc
        return modc[:, o:o + 1]

    wseq = []
    for ti in range(NT):
        for l in range(DEPTH):
            i = l // 2
            if l % 2 == 0:
                wseq += [d_wie[i][g] for g in range(len(EV_GROUPS))] + [d_woe[i][g] for g in range(16)]
            else:
                wseq += [d_wio[i][g] for g in range(64)] + [d_woo[i][g] for g in range(16)]
    wst = dict(issued=0, consumed=0)
    NG = len(wseq) // NT
    wsc = None
    if NT > 1:
        wsc = []
        for l in range(DEPTH):
            ng_l = (len(EV_GROUPS) + 16) if l % 2 == 0 else 80
            t_ = nc.dram_tensor(f'wscratch{l}', [ng_l, 128, 4096], BF16, kind="Internal").ap()
            wsc += [t_[g_] for g_ in range(ng_l)]
        assert len(wsc) == NG

    def wnext():
        while wst['issued'] < min(len(wseq), wst['consumed'] + NS):
            n = wst['issued']
            s = n % NS
            g_ = n % NG
            if n < NG:
                dma('pool', wbuf[s][:], wseq[n], [], [('w', s)], f'w{s}')
                if wsc is not None:
                    dma('sp', wsc[g_], wbuf[s][:], [('w', s)], [('wsc', g_)], f'wb{s}')
            else:
                dma('pool', wbuf[s][:], wsc[g_], [('wsc', g_)], [('w', s)], f'w{s}')
            wst['issued'] += 1
        s = wst['consumed'] % NS
        wst['consumed'] += 1
        return wbuf[s], ('w', s)

    def phase_A(l, Tt, segs):
        ckpt('A0')
        act(sq[:, :, 0:Tt], xT[:, :, 0:Tt], AF.Square, XK, keys('mix', range(16)))
        ckpt('A1')
        pst, pk = bank()
        for kc in range(KC):
            mm(pst[:, 0:Tt], ones_b[:], sq[:, kc, 0:Tt], kc == 0, kc == KC - 1, [('mix', kc), 'ones_b'], [pk])
        ckpt('A2')
        rsqrt(rstd[:, 0:Tt], pst[:, 0:Tt], 1.0 / D, [pk, 'eps_c'], 'rstd')
        ckpt('A3')
        for kc in range(KC):
            tb = tmpA[kc % 2]; tk = ('tmpA', kc % 2)
            tt('dve', tb[:, 0:Tt], xT[:, kc, 0:Tt], rstd[:, 0:Tt], ALU.mult, [('x', kc), 'rstd'], [tk])
            for (c0, c1, w) in segs:
                act(hT[:, kc, c0:c1], tb[:, c0:c1], AF.Identity, [tk, ('modc', l)], [('h', kc)],
                    bias=mc(l, w, 1, kc), scale=mc(l, w, 0, kc))

    def proj_fm(wt, wk, col0, ncols_k, Tt, rhs_fn, rkeys, nk):
        pst, pk = bank()
        for kc in range(nk):
            mm(pst[:, 0:Tt], wt[:, kc * ncols_k + col0: kc * ncols_k + col0 + 128], rhs_fn(kc), kc == 0, kc == nk - 1,
               [wk] + rkeys(kc), [pk])
        return pst, pk

    def phase_D(l, Tt, segs, last_tile, ti):
        ssb, ssk = psW[:, 512:1024], 'psW'
        for j in range(KC):
            wt, wk = wnext()
            pst, pk = proj_fm(wt, wk, 0, 128, Tt, lambda kc: mix[:, kc, 0:Tt], lambda kc: [('mix', kc)], 32)
            wr = [('o', j)]
            if j == 0:
                wr = wr + HK + keys('xb', range(24)) + keys('q', range(16))
            cp('act', obuf[:, j, 0:Tt], pst[:, 0:Tt], [pk], wr)
            sb_ = sqo[j % 2]; sk = ('sqo', 0)
            act(sb_[:, 0:Tt], pst[:, 0:Tt], AF.Square, [pk], [sk])
            mm(ssb[:, 0:Tt], ones_b[:], sb_[:, 0:Tt], j == 0, j == KC - 1, [sk, 'ones_b'], [ssk])
        rsqrt(rstd[:, 0:Tt], ssb[:, 0:Tt], 1.0 / D, [ssk, 'eps_c'], 'rstd')
        for j in range(KC):
            tb = tmpA[j % 2]; tk = ('tmpA', j % 2)
            tt('dve', tb[:, 0:Tt], obuf[:, j, 0:Tt], rstd[:, 0:Tt], ALU.mult, [('o', j), 'rstd'], [tk])
            for (c0, c1, w) in segs:
                stt('dve', xT[:, j, c0:c1], tb[:, c0:c1], mc(l, w, 2, j), xT[:, j, c0:c1], ALU.mult, ALU.add,
                    [tk, ('modc', l), ('x', j)], [('x', j)])

    def even_layer(l, ti, Tt, segs, chunks):
        i = l // 2
        last = (ti == NT - 1)
        has_s = (ti == 0)
        L = Tt + (3 if has_s else 0)
        if ti == 0:
            memset('dve', stT[i][:], 0.0, [('stT', i, g_) for g_ in range(4)])
            memset('dve', ccar[i][:], 0.0, [('ccar', i)])
            dma('sp', scaS[:], d_sca[i], [], ['scaS'], 'sin')
            dma('sp', stS[:], d_sst[i], [], STSK, 'sin')
            dma('pool', kS[:, :, 0:128], d_ckT[i], [], ['kS'], 'sinb')
            dma('pool', kSs[0:64, :, 0:128], d_ckT[i, 64:128], [], ['kSs'], 'sinb')
            dma('pool', kSs[64:128, :, 0:128], d_ckT[i, 0:64], [], ['kSs'], 'sinb')
            dma('pool', vS[:, 0:2, :], d_cv[i].rearrange("b s c -> s b c"), [], ['vS'], 'sinb')
            dma('sp', o_nks[i, :, :, 0:64], d_ckT[i, :, :, 64:128], [], [('onks', i)], 'outs')
            dma('sp', o_nvs[i, 0:64, :], d_cv[i, 1], [], [('onvs', i)], 'outs')
        ckpt(f'pre{l}_{ti}')
        tt('dve', Dm[i][:].rearrange("p (h c) -> p h c", h=32),
           dsk[:, i * 32:(i + 1) * 32].unsqueeze(2).to_broadcast([64, 32, 64]),
           ident_f[0:64, 0:64].unsqueeze(1).to_broadcast([64, 32, 64]), ALU.mult,
           ['dsk', 'ident_f'], [('Dm', 0)])
        phase_A(l, Tt, segs)
        ckpt(f'A{l}_{ti}')
        dbg(f'h{l}_{ti}', hT[:, :, 0:Tt], HK, [128, KC, Tt])
        def do_group(typ, gi):
            ckpt(f'G{typ}{gi}')
            wt, wk = wnext()
            if typ == 'dt':
                dtps, dtk = bank()
                for ci, c in enumerate(chunks):
                    for kc in range(KC):
                        mm(dtps[0:64, ci * 32:(ci + 1) * 32], hT[:, kc, c['col']:c['col'] + 64], wt[:, kc * 256:kc * 256 + 32],
                           kc == 0, kc == KC - 1, [wk, ('h', kc)], [dtk])
                nch = len(chunks)
                n32 = nch * 32
                xa = spx[:, 0:n32]
                dv = dtall[:, 0:n32]
                tt('dve', dv.rearrange("p (c h) -> p c h", h=32), dtps[0:64, 0:n32].rearrange("p (c h) -> p c h", h=32),
                   dtb[:, i * 32:(i + 1) * 32].unsqueeze(1).to_broadcast([64, nch, 32]), ALU.add, [dtk, 'dtb'], ['dtall'])
                act(xa, dv, AF.Abs, ['dtall'], ['spx'])
                act(xa, xa, AF.Exp, ['spx'], ['spx'], scale=-1.0)
                act(xa, xa, AF.Ln, ['spx', 'eps_c'], ['spx'], bias=eps_c[0:64, 1:2])
                stt('dve', dv, dv, 0.0, xa, ALU.max, ALU.add, ['dtall', 'spx'], ['dtall'])
            elif typ == 'xbc':
                for jj in range(2):
                    f = gi * 2 + jj
                    pst, pk = proj_fm(wt, wk, jj * 128, 256, Tt, lambda kc: hT[:, kc, 0:Tt], lambda kc: [('h', kc)], KC)
                    cs = cst[f % 2]; ck = ('cst', f % 2); ca = cacc[f % 2]; cak = ('cacc', f % 2)
                    cp('act', cs[:, 3:3 + TP], pst[:, 0:TP], [pk], [ck])
                    cp('dve', cs[:, 0:3], ccar[i][:, f * 3:f * 3 + 3], [('ccar', i)], [ck])
                    if has_s:
                        cp('act', cs[:, 6 + TP:6 + TP + CH], pst[:, TP:TP + CH], [pk], [ck])
                        cp('dve', cs[:, 3 + TP:6 + TP], scaS[:, f * 3:f * 3 + 3], ['scaS'], [ck])
                        cp('dve', scaS[:, f * 3:f * 3 + 3], cs[:, 3 + TP + CH:6 + TP + CH], [ck], ['scaS'])
                    cp('dve', ccar[i][:, f * 3:f * 3 + 3], cs[:, TP:TP + 3], [ck], [('ccar', i)])
                    wv = caw[:, (i * 24 + f) * 4:(i * 24 + f) * 4 + 4]
                    tsc('dve', ca[:, 0:L], cs[:, 3:3 + L], wv[:, 3:4], cab[:, i * 24 + f:i * 24 + f + 1], ALU.mult, ALU.add,
                        [ck, 'caw', 'cab'], [cak])
                    for tap in range(3):
                        stt('dve', ca[:, 0:L], cs[:, tap:tap + L], wv[:, tap:tap + 1], ca[:, 0:L], ALU.mult, ALU.add,
                            [ck, cak, 'caw'], [cak])
                    act(xbcs[:, f, 0:TP], ca[:, 0:TP], AF.Silu, [cak], [('xb', f)] + (keys('o', range(KC)) if f == 0 else []))
                    if has_s:
                        act(xbcs[:, f, TP:TP + CH], ca[:, TP + 3:TP + 3 + CH], AF.Silu, [cak], [('xb', f)])
            elif typ in ('z', 'g'):
                for jj in range(2):
                    f = gi * 2 + jj
                    pst, pk = proj_fm(wt, wk, jj * 128, 256, Tt, lambda kc: hT[:, kc, 0:Tt], lambda kc: [('h', kc)], KC)
                    mf = f if typ == 'z' else 16 + f
                    act(mix[:, mf, 0:Tt], pst[:, 0:Tt], AF.Silu, [pk], [('mix', mf)])
            elif typ == 'q':
                for jj in range(2):
                    f = gi * 2 + jj
                    pst, pk = proj_fm(wt, wk, jj * 128, 256, Tt, lambda kc: hT[:, kc, 0:Tt], lambda kc: [('h', kc)], KC)
                    tsc('dve', qT[:, f, 0:Tt], pst[:, 0:Tt], 0.125, None, ALU.mult, None, [pk], [('q', f)] + (keys('o', range(KC)) if f == 0 else []))
            elif typ == 'k':
                for jj in range(2):
                    f = gi * 2 + jj
                    pst, pk = proj_fm(wt, wk, jj * 128, 256, Tt, lambda kc: hT[:, kc, 0:Tt], lambda kc: [('h', kc)], KC)
                    cp('dve', kP[i][:, f, 128:128 + TP], pst[:, 0:TP], [pk], [('kP', i)])
                    ckpt(f'K1_{f}')
                    if has_s:
                        cp('dve', kS[:, f, 128:192], pst[:, TP:TP + CH], [pk], ['kS'])
                        ckpt(f'K2_{f}')
                        ko = kvout[f % 2]; kk = ('kvout', 0)
                        cp('dve', ko[:, 0:CH], pst[:, TP:TP + CH], [pk], [kk])
                        ckpt(f'K3_{f}')
                        dma('sp', o_nks[i, :, f, 64:128], ko[:, 0:CH], [kk], [('onks', i)], 'outs')
                        ckpt(f'K4_{f}')
                    if last:
                        ko = kvout[f % 2]; kk = ('kvout', 0)
                        cp('act', ko[:, 0:128], pst[:, TP - 128:TP], [pk], [kk])
                        dma('sp', o_nkp[i, :, f, :], ko[:, 0:128], [kk], [('onkp', i)], 'outs')
            elif typ == 'ks':
                for jj in range(2):
                    f = gi * 2 + jj
                    pst, pk = proj_fm(wt, wk, jj * 128, 256, Tt, lambda kc: hT[:, kc, 0:Tt], lambda kc: [('h', kc)], KC)
                    cp('dve', kPs[i][:, f, 128:128 + TP], pst[:, 0:TP], [pk], [('kPs', i)])
                    if has_s:
                        cp('dve', kSs[:, f, 128:192], pst[:, TP:TP + CH], [pk], ['kSs'])
            elif typ == 'v':
                for ci, c in enumerate(chunks):
                    pst, pk = bank()
                    for kc in range(KC):
                        mm(pst[0:64, 0:256], hT[:, kc, c['col']:c['col'] + 64], wt[:, kc * 256:(kc + 1) * 256],
                           kc == 0, kc == KC - 1, [wk, ('h', kc)], [pk])
                    if c['seg'] == 'p':
                        cp('dve', vP[i][:, 2 + c['cl'], gi * 256:(gi + 1) * 256], pst[0:64, 0:256], [pk], [('vP', i)])
                        if last and c['cl'] >= NCP - 2:
                            blk = c['cl'] - (NCP - 2)
                            ko = kvout[ci % 2]; kk = ('kvout', 0)
                            cp('act', ko[0:64, 0:256], pst[0:64, 0:256], [pk], [kk])
                            dma('sp', o_nvp[i, blk * 64:(blk + 1) * 64, gi * 256:(gi + 1) * 256], ko[0:64, 0:256], [kk], [('onvp', i)], 'outs')
                    else:
                        cp('dve', vS[:, 2, gi * 256:(gi + 1) * 256], pst[0:64, 0:256], [pk], ['vS'])
                        ko = kvout[ci % 2]; kk = ('kvout', 0)
                        cp('act', ko[0:64, 0:256], pst[0:64, 0:256], [pk], [kk])
                        dma('sp', o_nvs[i, 64:128, gi * 256:(gi + 1) * 256], ko[0:64, 0:256], [kk], [('onvs', i)], 'outs')

        NG1 = 1 + 12
        for (typ, gi) in EV_GROUPS[:NG1]:
            do_group(typ, gi)

        def projB2():
            for (typ, gi) in EV_GROUPS[NG1:]:
                do_group(typ, gi)
                yield
        if last:
            dma('sp', o_ncap[i], ccar[i][:], [('ccar', i)], [('oncap', i)], 'outs')
        if has_s:
            dma('sp', o_ncas[i], scaS[:], ['scaS'], [('oncas', i)], 'outs')
        dbg(f'xbcs{l}_{ti}', xbcs[:, :, 0:Tt], keys('xb', range(24)), [128, 24, Tt])
        dbg(f'dt{l}_{ti}', dtall[:, 0:len(chunks) * 32], ['dtall'], [64, len(chunks) * 32])
        dbg(f'q{l}_{ti}', qT[:, :, 0:Tt], keys('q', range(16)), [128, 16, Tt])
        ckpt(f'B{l}_{ti}')
        cp('act', stbf[:], stT[i][:], [('stT', i, g_) for g_ in range(4)], STBK)
        def ssd_ctx(ci, c):
            isS = c['seg'] == 's'
            d = dict(col=c['col'], isS=isS)
            d['st_f'] = stS if isS else stT[i]
            d['st_k'] = (lambda g: ('stS', g)) if isS else (lambda g: ('stT', i, g))
            d['dtc'] = dtall[:, ci * 32:(ci + 1) * 32]
            for n in ['dta', 'acum', 'ldt', 'amb', 'E2', 'wte', 'tw']:
                d[n] = sm[n][0:64, :]
            d['Edec'] = sm['Edec']
            return d

        def ssd_pre(ci, c):
            X = ssd_ctx(ci, c)
            dtc, dta, acum, ldt, amb, E2, wte, tw, Edec = (X[k_] for k_ in ['dtc', 'dta', 'acum', 'ldt', 'amb', 'E2', 'wte', 'tw', 'Edec'])
            col = c['col']
            isS = c['seg'] == 's'
            if isS:
                cp('act', stbf[:], stS[:], STSK, STBK)
            tt('dve', dta, dtc, aneg[:, i * 32:(i + 1) * 32], ALU.mult, ['dtall', 'aneg'], ['dta'])
            pss, psk = bank()
            mm(pss[0:64, 0:32], triu_f[:], dta, True, True, ['triu_f', 'dta'], [psk])
            mm(pss[:, 32:64], ones_f[:], dta, True, True, ['ones_f', 'dta'], [psk])
            cp('act', acum, pss[0:64, 0:32], [psk], ['acum'])
            cp('dve', ahl[:, 0:32], acum, ['acum'], ['ahl'])
            tt('dve', ldt, acum, ahl[:, 0:32], ALU.subtract, ['acum', 'ahl'], ['ldt'])
            cp('dve', ahl[:, 32:64], ldt, ['ldt'], ['ahl'])
            act(ldt, dtc, AF.Ln, ['dtall'], ['ldt'])
            tt('dve', amb, acum, ldt, ALU.subtract, ['acum', 'ldt'], ['amb'])
            act(E2, acum, AF.Exp, ['acum'], ['E2'])
            tt('dve', tw, pss[0:64, 32:64], acum, ALU.subtract, [psk, 'acum'], ['tw'])
            act(tw, tw, AF.Exp, ['tw'], ['tw'])
            tt('dve', wte, tw, dtc, ALU.mult, ['tw', 'dtall'], ['wte'])
            act(Edec[:], pss[:, 32:64], AF.Exp, [psk], ['Edec'])
            yield
            for b0 in range(0, 20, 8):
                nb_ = min(8, 20 - b0)
                ptb, ptk = bank()
                pv = ptb[:].bitcast(BF16)
                for f in range(b0, b0 + nb_):
                    tr(pv[0:64, (f - b0) * 128:(f - b0 + 1) * 128], xbcs[:, f, col:col + 64], ident_b[:], [('xb', f), 'ident_b'], [ptk])
                cp('act' if b0 == 8 else 'dve', xtok[:, b0 * 128:(b0 + nb_) * 128], pv[0:64, 0:nb_ * 128], [ptk], ['xtok'])
            tt('dve', xw[:].rearrange("p (h c) -> p h c", h=32), xtok[:, 0:2048].rearrange("p (h c) -> p h c", h=32),
               wte.unsqueeze(2).to_broadcast([64, 32, 64]), ALU.mult, ['xtok', 'wte'], ['xw'])
            yield
            pcb, pcbk = bank()
            for g in range(4):
                mm(pcb[0:64, g * 64:(g + 1) * 64], xbcs[:, 16 + g, col:col + 64], xbcs[:, 20 + g, col:col + 64], True, True,
                   [('xb', 16 + g), ('xb', 20 + g)], [pcbk])
            cp('act', cbT[:], pcb[0:64, 0:256], [pcbk], ['cbT'])
            yield

        def ssd_grp(ci, c, groups, b2):
            X = ssd_ctx(ci, c)
            col = X['col']; st_f = X['st_f']; st_kf = X['st_k']
            acum, amb, E2, Edec = X['acum'], X['amb'], X['E2'], X['Edec']
            st_b = stbf
            for g in groups:
                hs = slice(g * 8, (g + 1) * 8)
                for (dgt, off) in ((Dgh[b2], 0), (Dgl[b2], 32)):
                    tt('pool', dgt.rearrange("p (r c) -> p r c", r=8), ahl[:, off + g * 8:off + (g + 1) * 8].unsqueeze(2).to_broadcast([64, 8, 64]),
                       ident_b[0:64, 0:64].unsqueeze(1).to_broadcast([64, 8, 64]), ALU.mult, ['ahl', 'ident_b'], DGK[b2])
                pA, pAk = bank()
                mm(pA[0:64, :], ones_b[0:64, 0:64], Dgh[b2], True, False, ['ones_b'] + DGK[b2], [pAk])
                mm(pA[0:64, :], ones_b[0:64, 0:64], Dgl[b2], False, False, ['ones_b'] + DGK[b2], [pAk])
                mm(pA[0:64, :], ident_b[0:64, 0:64], mask_b[:], False, True, ['ident_b', 'mask_b'], [pAk])
                tt('dve', seg[b2][:].rearrange("p (r c) -> p r c", r=8), pA[0:64, :].rearrange("p (r c) -> p r c", r=8),
                   amb[:, hs].unsqueeze(2).to_broadcast([64, 8, 64]), ALU.subtract, [pAk, 'amb'], [('seg', b2)])
                act(seg[b2][:], seg[b2][:], AF.Exp, [('seg', b2)], [('seg', b2)])
                tt('dve', mixT[b2][:].rearrange("p (r c) -> p r c", r=8), seg[b2][:].rearrange("p (r c) -> p r c", r=8),
                   cbT[:, g * 64:(g + 1) * 64].unsqueeze(1).to_broadcast([64, 8, 64]), ALU.mult, [('seg', b2), 'cbT'], [('mixT', b2)])
                yield
                py, pyk = bank()
                for r in range(8):
                    h = g * 8 + r
                    mm(py[0:64, r * 64:(r + 1) * 64], mixT[b2][:, r * 64:(r + 1) * 64], xtok[:, h * 64:(h + 1) * 64], True, False,
                       [('mixT', b2), 'xtok'], [pyk])
                    mm(py[0:64, r * 64:(r + 1) * 64], Dm[i][:, h * 64:(h + 1) * 64], xtok[:, h * 64:(h + 1) * 64], False, True,
                       [('Dm', 0), 'xtok'], [pyk])
                po, pok = bank()
                mm(po[0:64, :], xbcs[:, 20 + g, col:col + 64], st_b[:, g * 512:(g + 1) * 512], True, True, [('xb', 20 + g), ('stbf', g)], [pok])
                tt('dve', t1[b2][:].rearrange("p (r c) -> p r c", r=8), po[0:64, :].rearrange("p (r c) -> p r c", r=8),
                   E2[:, hs].unsqueeze(2).to_broadcast([64, 8, 64]), ALU.mult, [pok, 'E2'], [('t1', b2)])
                tt('dve', ytok[b2][:], t1[b2][:], py[0:64, :], ALU.add, [('t1', b2), pyk], [('ytok', b2)])
                yield
                pyt, pytk = bank()
                pyv = pyt[:].bitcast(BF16)
                for fc in range(4):
                    tr(pyv[:, fc * 64:(fc + 1) * 64], ytok[b2][:, fc * 128:(fc + 1) * 128], ident_b[0:64, 0:64], [('ytok', b2), 'ident_b'], [pytk])
                mz = mix[:, g * 4:(g + 1) * 4, col:col + 64]
                mzk = keys('mix', range(g * 4, g * 4 + 4))
                gyv = gy[b2][:].rearrange("p (a c) -> p a c", a=4)
                tt('dve', gyv, pyv[:, 0:256].rearrange("p (a c) -> p a c", a=4), mz, ALU.mult, [pytk] + mzk, [('gy', b2)])
                act(sqg[b2][:], gy[b2][:], AF.Square, [('gy', b2)], [('sqg', b2)])
                pss2, pss2k = bank()
                for fc in range(4):
                    mm(pss2[:, 0:64], ones_b[:], sqg[b2][:, fc * 64:(fc + 1) * 64], fc == 0, fc == 3, [('sqg', b2), 'ones_b'], [pss2k])
                rsqrt(rsg[b2][:], pss2[:, 0:64], 1.0 / 512, [pss2k, 'eps_c'], ('rsg', b2))
                tt('dve', gyv, gyv, rsg[b2][:].unsqueeze(1).to_broadcast([128, 4, 64]), ALU.mult, [('gy', b2), ('rsg', b2)], [('gy', b2)])
                tt('pool', mz, gyv, nssm[:, i * 16 + g * 4:i * 16 + g * 4 + 4].unsqueeze(2).to_broadcast([128, 4, 64]), ALU.mult,
                   [('gy', b2), 'nssm'], mzk)
                yield
                pst_, pstk = bank()
                mm(pst_[:, :], xtok[:, 2048 + g * 128:2048 + (g + 1) * 128], xw[:, g * 512:(g + 1) * 512], True, True, ['xtok', 'xw'], [pstk])
                sv = st_f[:, g * 512:(g + 1) * 512]
                tt('dve', sv.rearrange("p (r c) -> p r c", r=8), sv.rearrange("p (r c) -> p r c", r=8),
                   Edec[:, hs].unsqueeze(2).to_broadcast([128, 8, 64]), ALU.mult, [st_kf(g), 'Edec', ('stbf', g)], [st_kf(g)])
                tt('dve', sv, sv, pst_[:, :], ALU.add, [st_kf(g), pstk], [st_kf(g)])
                cp('act', st_b[:, g * 512:(g + 1) * 512], sv, [st_kf(g)], [('stbf', g)])
                yield
            yield

        def attn_gen(ci, c, quads, sidx):
            col = c['col']
            isS = c['seg'] == 's'
            if isS:
                KB, KBs, kkey, kskey = kS, kSs, 'kS', 'kSs'
                kc0 = 0; nblk = 3
                vblk = [(vS, j, 'vS') for j in range(3)]
            else:
                KB, KBs, kkey, kskey = kP[i], kPs[i], ('kP', i), ('kPs', i)
                nblk = min(3, c['gidx'] + 1)
                kc0 = 64 * c['cl'] + 64 * (3 - nblk)
                vblk = [(vP[i], c['cl'] + (3 - nblk) + j, ('vP', i)) for j in range(nblk)]
            nv = nblk * 64
            psWt = PW[sidx]; pwk = PWK[sidx]
            PQK = PKS[sidx]; PXK = PKS[sidx]
            for qd in quads:
                b2 = sidx
                kv = qd
                fk = kv // 2; khalf = kv % 2
                scv = psWt[0:64, :].rearrange("p (h c) -> p h c", h=4)
                for hh in range(4):
                    half = hh // 2; fq = qd * 2 + hh % 2
                    ksrc, ksk = (KB, kkey) if khalf == half else (KBs, kskey)
                    mm(psWt[0:64, hh * 256:hh * 256 + nv], qT[half * 64:(half + 1) * 64, fq, col:col + 64],
                       ksrc[half * 64:(half + 1) * 64, fk, kc0:kc0 + nv], True, True, [('q', fq), ksk], [pwk])
                ckpt(f'AT1_{qd}')
                A = asm[b2]
                ak = lambda n: ('asm', b2, n)
                T.add('dve', lambda e, o=A['mx'][:], i_=scv[:, :, 0:nv]: e.reduce_max(o, i_, AX.X), [pwk], [ak('mx')])
                snq = snk[:, i * 32 + qd * 4:i * 32 + qd * 4 + 4].rearrange("p (f h) -> p h f", h=2)
                hv = lambda t_: t_[:].rearrange("p (h f) -> p h f", f=2)
                tt('dve', hv(A['mx']), hv(A['mx']), snq, ALU.max, [ak('mx'), 'snk'], [ak('mx')])
                tsc('dve', A['negm'][:], A['mx'][:], -1.0, None, ALU.mult, None, [ak('mx')], [ak('negm')])
                Pv = Pq[b2][:].rearrange("p (h c) -> p h c", h=4)
                memset('dve', A['rs'][:], 0.0, [ak('rs')])
                for hh in range(4):
                    act(Pv[:, hh, 0:nv], scv[:, hh, 0:nv], AF.Exp, [pwk, ak('negm')], [*PQK, ak('rs')],
                        bias=A['negm'][:, hh:hh + 1], accum=A['rs'][:, hh:hh + 1])
                tt('dve', hv(A['es']), snq, hv(A['negm']), ALU.add, ['snk', ak('negm')], [ak('es')])
                act(A['es'][:], A['es'][:], AF.Exp, [ak('es')], [ak('es')])
                tt('dve', A['den'][:], A['rs'][:], A['es'][:], ALU.add, [ak('rs'), ak('es')], [ak('den')])
                T.add('dve', lambda e, o=A['rinv'][:], i_=A['den'][:]: e.reciprocal(o, i_), [ak('den')], [ak('rinv')])
                Pnv = Pn[b2][:].rearrange("p (h c) -> p h c", h=4)
                tt('dve', Pnv[:, :, 0:nv], Pv[:, :, 0:nv], A['rinv'][:].unsqueeze(2).to_broadcast([64, 4, nv]), ALU.mult,
                   [*PQK, ak('rinv')], [*PXK])
                ckpt(f'AT2_{qd}')
                yield
                ppt, pptk = bank()
                ppv = ppt[:].bitcast(BF16)
                for hh in range(4):
                    for j in range(nblk):
                        tr(ppv[0:64, (hh * 3 + j) * 64:(hh * 3 + j + 1) * 64], Pnv[:, hh, j * 64:(j + 1) * 64], ident_b[0:64, 0:64],
                           [*PXK, 'ident_b'], [pptk])
                cp('act', PTq[b2][:, 0:768].rearrange("p (h c) -> p h c", h=4)[:, :, 0:nv], ppv[0:64, 0:768].rearrange("p (h c) -> p h c", h=4)[:, :, 0:nv], [pptk], [*PXK])
                ckpt(f'AT3_{qd}')
                pov, povk = bank()
                for hh in range(4):
                    fql = hh % 2; half = hh // 2
                    for j in range(nblk):
                        vt, vb, vk = vblk[j]
                        mm(pov[half * 64:(half + 1) * 64, fql * 64:(fql + 1) * 64], vt[:, vb, kv * 64:(kv + 1) * 64],
                           PTq[b2][:, (hh * 3 + j) * 64:(hh * 3 + j + 1) * 64], j == 0, j == nblk - 1, [vk, *PXK], [povk])
                ckpt(f'AT4_{qd}')
                mg = mix[:, 16 + qd * 2:16 + qd * 2 + 2, col:col + 64]
                mgk = keys('mix', range(16 + qd * 2, 16 + qd * 2 + 2))
                tt('dve', mg, pov[:, 0:128].rearrange("p (a c) -> p a c", a=2), mg, ALU.mult, [povk] + mgk, mgk)
                yield
            yield

        pb2 = projB2()
        active = [pb2]
        prev_attn = []

        def drain(required):
            req = list(required)
            for g_ in req:
                if g_ not in active:
                    active.append(g_)
            while any(g_ in active for g_ in req):
                for g_ in list(active):
                    try:
                        next(g_)
                    except StopIteration:
                        active.remove(g_)

        memset('dve', modt[:, 0:1], 0.0, [('modt', 0)] + SCRK + SETB)
        for ci, c in enumerate(chunks):
            drain([ssd_pre(ci, c)])
            if ci == 0:
                drain([ssd_grp(ci, c, [0, 1, 2, 3], 0)])
            else:
                drain([ssd_grp(ci, c, [0, 1], 0), ssd_grp(ci, c, [2, 3], 1)])
            ckpt(f'S{l}_{ti}_{ci}')
            if ci == 0:
                drain([pb2])
            drain([g_ for g_ in prev_attn if g_ in active])
            prev_attn[:] = [attn_gen(ci, c, range(0, 4), 0), attn_gen(ci, c, range(4, 8), 1)]
            active.extend(prev_attn)
        drain(list(active))
        memset('dve', modt[:, 0:1], 0.0, [('modt', 0)] + SCRK + SETB)
        dbg(f'mix{l}_{ti}', mix[:, :, 0:Tt], keys('mix', range(32)), [128, 32, Tt])
        ckpt(f'C{l}_{ti}')
        if has_s:
            dma('sp', o_nsss[i], stS[:], STSK, [('onsss', i)], 'outs')
        if last:
            dma('sp', o_nssp[i], stT[i][:], [('stT', i, g_) for g_ in range(4)], [('onssp', i)], 'outs')
        else:
            cp('pool', kP[i][:, :, 0:128], kP[i][:, :, TP:TP + 128], [('kP', i)], [('kP', i)])
            cp('pool', kPs[i][:, :, 0:128], kPs[i][:, :, TP:TP + 128], [('kPs', i)], [('kPs', i)])
            cp('pool', vP[i][:, 0:2, :], vP[i][:, NCP:NCP + 2, :], [('vP', i)], [('vP', i)])
        phase_D(l, Tt, segs, last, ti)

    def odd_layer(l, ti, Tt, segs, chunks):
        i = l // 2
        last = (ti == NT - 1)
        has_s = (ti == 0)
        L = Tt + (2 if has_s else 0)
        if ti == 0:
            memset('dve', vcar[i][:], 0.0, [('vcar', i)])
            dma('sp', sccS[:], d_scc[i], [], ['sccS'], 'sin')
        phase_A(l, Tt, segs)
        for j in range(32):
            wt, wk = wnext()
            pu, puk = proj_fm(wt, wk, 0, 256, Tt, lambda kc: hT[:, kc, 0:Tt], lambda kc: [('h', kc)], KC)
            pgc, pgck = proj_fm(wt, wk, 128, 256, Tt, lambda kc: hT[:, kc, 0:Tt], lambda kc: [('h', kc)], KC)
            wt2, wk2 = wnext()
            pgb, pgbk = proj_fm(wt2, wk2, 0, 256, Tt, lambda kc: hT[:, kc, 0:Tt], lambda kc: [('h', kc)], KC)
            pg, pgk = proj_fm(wt2, wk2, 128, 256, Tt, lambda kc: hT[:, kc, 0:Tt], lambda kc: [('h', kc)], KC)
            su = stg[(j % 2) * 2]; suk = ('tmpA', 0) if j % 2 == 0 else ('stg', 2)
            sg_ = stg[(j % 2) * 2 + 1]; sgk = ('tmpA', 1) if j % 2 == 0 else ('stg', 3)
            cs = cst[0]; ck = ('cst', 0); ca = cacc[0]; cak = ('cacc', 0)
            cp('act', su[:, 0:Tt], pu[:, 0:Tt], [puk], [suk])
            tt('dve', cs[:, 2:2 + TP], pgc[:, 0:TP], su[:, 0:TP], ALU.mult, [pgck, suk], [ck])
            cp('dve', cs[:, 0:2], vcar[i][:, j * 2:j * 2 + 2], [('vcar', i)], [ck])
            if has_s:
                tt('dve', cs[:, 4 + TP:4 + TP + CH], pgc[:, TP:TP + CH], su[:, TP:TP + CH], ALU.mult, [pgck, suk], [ck])
                cp('dve', cs[:, 2 + TP:4 + TP], sccS[:, j * 2:j * 2 + 2], ['sccS'], [ck])
                cp('dve', sccS[:, j * 2:j * 2 + 2], cs[:, 2 + TP + CH:4 + TP + CH], [ck], ['sccS'])
            cp('dve', vcar[i][:, j * 2:j * 2 + 2], cs[:, TP:TP + 2], [ck], [('vcar', i)])
            wv = ccw[:, (i * 32 + j) * 3:(i * 32 + j) * 3 + 3]
            tsc('dve', ca[:, 0:L], cs[:, 0:L], wv[:, 0:1], None, ALU.mult, None, [ck, 'ccw'], [cak])
            for tap in (1, 2):
                stt('dve', ca[:, 0:L], cs[:, tap:tap + L], wv[:, tap:tap + 1], ca[:, 0:L], ALU.mult, ALU.add, [ck, cak, 'ccw'], [cak])
            act(sg_[:, 0:Tt], pg[:, 0:Tt], AF.Silu, [pgk], [sgk])
            tt('dve', ca[:, 0:TP], ca[:, 0:TP], pgb[:, 0:TP], ALU.mult, [cak, pgbk], [cak])
            tt('pool', mix[:, j, 0:TP], ca[:, 0:TP], sg_[:, 0:TP], ALU.mult, [cak, sgk], [('mix', j)])
            if has_s:
                tt('dve', ca[:, TP + 2:TP + 2 + CH], ca[:, TP + 2:TP + 2 + CH], pgb[:, TP:TP + CH], ALU.mult, [cak, pgbk], [cak])
                tt('pool', mix[:, j, TP:TP + CH], ca[:, TP + 2:TP + 2 + CH], sg_[:, TP:TP + CH], ALU.mult, [cak, sgk], [('mix', j)])
        if last:
            dma('sp', o_nccp[i], vcar[i][:], [('vcar', i)], [('onccp', i)], 'outs')
        if has_s:
            dma('sp', o_nccs[i], sccS[:], ['sccS'], [('onccs', i)], 'outs')
        dbg(f'mix{l}_{ti}', mix[:, :, 0:Tt], keys('mix', range(32)), [128, 32, Tt])
        phase_D(l, Tt, segs, last, ti)

    def main_schedule():
        for ti in range(NT):
            has_s = (ti == 0)
            Tt = TP + (CH if has_s else 0)
            segs = [(0, TP, 0)] + ([(TP, TP + CH, 1)] if has_s else [])
            chunks = [dict(col=cl * CH, seg='p', cl=cl, gidx=ti * NCP + cl) for cl in range(NCP)]
            if has_s:
                chunks.append(dict(col=TP, seg='s', cl=0, gidx=0))
            dma('sp', xT[:, :, 0:TP], d_xp[:, :, ti * TP:(ti + 1) * TP], [], XK, 'xin')
            if has_s:
                dma('sp', xT[:, :, TP:TP + CH], d_xs, [], XK, 'xin')
            for l in range(DEPTH):
                if l % 2 == 0:
                    even_layer(l, ti, Tt, segs, chunks)
                else:
                    odd_layer(l, ti, Tt, segs, chunks)
                dbg(f'x{l}_{ti}', xT[:, :, 0:Tt], XK, [128, KC, Tt])
            dma('sp', o_yp[:, :, ti * TP:(ti + 1) * TP], xT[:, :, 0:TP], XK, [('oyp', ti)], 'xout')
            if has_s:
                dma('sp', o_ys, xT[:, :, TP:TP + CH], XK, ['oys'], 'xout')


    try:
        ckpt('prologue')
        main_schedule()
    except _Stop:
        pass

    T.emit(nc, es)
    es.close()
    return nc, sorted(dbg_out.keys())


def _fm(a, nch):
    return np.ascontiguousarray(a.reshape(nch, 128, -1).transpose(1, 0, 2))


def _wgroups(w, col_lists, kchunks):
    out = []
    for cols in col_lists:
        cols = np.asarray(cols)
        blk = w[:, np.maximum(cols, 0)]
        if (cols < 0).any():
            blk = blk.copy(); blk[:, cols < 0] = 0.0
        out.append(blk.reshape(kchunks, 128, len(cols)).transpose(1, 0, 2).reshape(128, kchunks * len(cols)))
    return np.ascontiguousarray(np.stack(out))


def prep_shared(inp, cfg):
    DEPTH = cfg['DEPTH']
    NE = (DEPTH + 1) // 2; NO = DEPTH // 2
    f32 = np.float32
    sh = {}
    sh['wada'] = np.ascontiguousarray(inp['w_ada'].reshape(DEPTH, KC, 128, 6144).transpose(0, 2, 1, 3))
    sh['bada'] = np.ascontiguousarray(inp['b_ada'].reshape(DEPTH, 48, 128).transpose(2, 0, 1).reshape(128, DEPTH * 48))
    sh['npre'] = np.ascontiguousarray(inp['norm_pre'].reshape(DEPTH, 16, 128).transpose(2, 0, 1).reshape(128, DEPTH * 16))
    sh['npost'] = np.ascontiguousarray(inp['norm_post'].reshape(DEPTH, 16, 128).transpose(2, 0, 1).reshape(128, DEPTH * 16))
    sh['c_ident'] = np.eye(128, dtype=f32)
    sh['c_triu'] = np.triu(np.ones((64, 64), f32))
    m = np.where(np.arange(64)[None, :] >= np.arange(64)[:, None], 0.0, -1e30).astype(f32)
    sh['c_mask'] = np.ascontiguousarray(np.tile(m, (1, 8)))
    for i in range(NE):
        w = inp['w_in_even'][i]
        gl = []
        for (typ, gi) in EV_GROUPS:
            c0 = EV_COLS[typ][0] if typ in EV_COLS else 0
            if typ == 'ks':
                c0 = EV_COLS['k'][0]
                cols = []
                for jj in range(2):
                    b0 = c0 + (gi * 2 + jj) * 128
                    cols += list(range(b0 + 64, b0 + 128)) + list(range(b0, b0 + 64))
            elif typ == 'dt':
                cols = list(range(c0, c0 + 32)) + [-1] * 224
            else:
                cols = list(range(c0 + gi * 256, c0 + (gi + 1) * 256))
            gl.append(cols)
        sh[f'wie{i}'] = _wgroups(w, gl, 16)
        sh[f'woe{i}'] = _wgroups(inp['w_out_even'][i], [list(range(g * 128, (g + 1) * 128)) for g in range(16)], 32)
    for i in range(NO):
        w = inp['w_in_odd'][i]
        gl = []
        for j in range(32):
            gl.append(list(range(j * 128, (j + 1) * 128)) + list(range(8192 + j * 128, 8192 + (j + 1) * 128)))
            gl.append(list(range(4096 + j * 128, 4096 + (j + 1) * 128)) + list(range(12288 + j * 128, 12288 + (j + 1) * 128)))
        sh[f'wio{i}'] = _wgroups(w, gl, 16)
        sh[f'woo{i}'] = _wgroups(inp['w_out_odd'][i], [list(range(g * 128, (g + 1) * 128)) for g in range(16)], 32)
    sh['caw'] = np.ascontiguousarray(inp['conv_a_w'].reshape(NE, 4, 24, 128).transpose(3, 0, 2, 1).reshape(128, NE * 96))
    sh['cab'] = np.ascontiguousarray(inp['conv_a_b'].reshape(NE, 24, 128).transpose(2, 0, 1).reshape(128, NE * 24))
    sh['nssm'] = np.ascontiguousarray(inp['norm_ssm'].reshape(NE, 16, 128).transpose(2, 0, 1).reshape(128, NE * 16))
    for nm, key in [('dtb', 'dt_bias'), ('alog', 'a_log'), ('dsk', 'd_skip'), ('snk', 'sinks')]:
        sh[nm] = np.ascontiguousarray(np.broadcast_to(inp[key].reshape(1, NE * 32), (64, NE * 32)))
    if NO > 0:
        sh['ccw'] = np.ascontiguousarray(inp['conv_c_w'].reshape(NO, 3, 32, 128).transpose(3, 0, 2, 1).reshape(128, NO * 96))
    else:
        sh['ccw'] = np.zeros((128, 96), f32)
    return sh


def prep_core(inp, b, cfg):
    DEPTH = cfg['DEPTH']
    NE = (DEPTH + 1) // 2; NO = DEPTH // 2
    c = {}
    c['xp'] = _fm(inp['x_prompt'][b].T, 16)
    c['xs'] = _fm(inp['x_sample'][b].T, 16)
    cc = np.stack([inp['c_prompt'][b], inp['c_sample'][b]], axis=1)
    c['cT'] = _fm(cc, 16).reshape(128, 32)
    c['ckT'] = np.ascontiguousarray(inp['cache_k'][:, b].reshape(NE, 128, 4, 128).transpose(0, 3, 2, 1))
    c['cv'] = np.ascontiguousarray(inp['cache_v'][:, b].reshape(NE, 2, 64, 512))
    c['sca'] = np.ascontiguousarray(inp['state_conv_a'][:, b].reshape(NE, 3, 24, 128).transpose(0, 3, 2, 1).reshape(NE, 128, 72))
    c['sst'] = np.ascontiguousarray(inp['state_ssm'][:, b].reshape(NE, 2048, 128).transpose(0, 2, 1))
    if NO > 0:
        c['scc'] = np.ascontiguousarray(inp['state_conv_c'][:, b].reshape(NO, 2, 32, 128).transpose(0, 3, 2, 1).reshape(NO, 128, 64))
    else:
        c['scc'] = np.zeros((1, 128, 64), np.float32)
    return c


def assemble(results, cfg, nb):
    DEPTH = cfg['DEPTH']; SEQ = cfg['SEQ']
    NE = (DEPTH + 1) // 2; NO = DEPTH // 2

    def st(f):
        return np.stack([f(r) for r in results], axis=0)

    def unfm(a):
        return a.transpose(2, 1, 0).reshape(a.shape[2], -1)
    yp = st(lambda r: unfm(r['yp']))
    ys = st(lambda r: unfm(r['ys']))

    def kfix(a):
        return a.transpose(0, 3, 2, 1).reshape(NE, 128, 8, 64)

    def cafix(a):
        return a.reshape(NE, 128, 24, 3).transpose(0, 3, 2, 1).reshape(NE, 3, 3072)

    def ssfix(a):
        return a.transpose(0, 2, 1).reshape(NE, 32, 64, 128)

    def ccfix(a):
        return a[:NO].reshape(NO, 128, 32, 2).transpose(0, 3, 2, 1).reshape(NO, 2, 4096)
    outs = [yp, ys]
    for sfx in ['p', 's']:
        outs.append(np.stack([kfix(r['nk' + sfx]) for r in results], axis=1))
        outs.append(np.stack([r['nv' + sfx].reshape(NE, 128, 8, 64) for r in results], axis=1))
        outs.append(np.stack([cafix(r['nca' + sfx]) for r in results], axis=1))
        outs.append(np.stack([ssfix(r['nss' + sfx]) for r in results], axis=1))
        outs.append(np.stack([ccfix(r['ncc' + sfx]) for r in results], axis=1))
    return tuple(np.ascontiguousarray(o.astype(np.float32)) for o in outs)


_CACHE = {}


def kernel(**inputs):
    inp = {k: np.asarray(v) for k, v in inputs.items()}
    B, SEQ, _ = inp['x_prompt'].shape
    DEPTH = inp['w_ada'].shape[0]
    cfg = dict(SEQ=SEQ, DEPTH=DEPTH, TP=256)
    key = (SEQ, DEPTH)
    if key not in _CACHE:
        _CACHE[key] = build(cfg)[0]
    nc = _CACHE[key]
    sh = prep_shared(inp, cfg)
    in_maps = []
    for b in range(B):
        m = dict(sh)
        m.update(prep_core(inp, b, cfg))
        in_maps.append(m)
    res = run_bass_kernel_spmd(nc, in_maps, core_ids=list(range(B)))
    return assemble(res.results, cfg, B)
```
